# Optimizing a Trainium2 kernel written in Bass

```python
import jax, jax.numpy as jnp
from jax import lax
import numpy as np

D_MODEL = 1024
BATCH = 4
SEQ = 4096
DEPTH = 1

N_META = 16
BLOCK = 128
LEAD = BLOCK
FOX_HD = 64
FOX_HEADS = D_MODEL // FOX_HD
FOX_W = FOX_HEADS * FOX_HD
RET_HEADS = 4
RET_DK = D_MODEL // (2 * RET_HEADS)
RET_DV = 2 * RET_DK
RET_QK = RET_HEADS * RET_DK
RET_V = RET_HEADS * RET_DV
D_FF = ((8 * D_MODEL // 3 + 127) // 128) * 128
IN_SIZES = (FOX_W, FOX_W, FOX_W, FOX_HEADS, RET_QK, RET_QK, RET_V, RET_V, D_MODEL, D_MODEL)
N_IN = sum(IN_SIZES)
IN_SPLITS = [int(s) for s in np.cumsum(IN_SIZES)[:-1]]
EPS = 1e-6
GN_EPS = 1e-5
ROPE_BASE = 10000.0
FORGET_BIAS_INIT = 3.0
NEG = -1e30

kernel_name = "fox_retnet_macaron_hybrid"


def rmsnorm(x, g):
    xf = x.astype(jnp.float32)
    y = xf * lax.rsqrt(jnp.mean(xf * xf, axis=-1, keepdims=True) + EPS)
    return (y * g.astype(jnp.float32)).astype(x.dtype)


def swiglu(x, w_in, w_out):
    a, b = jnp.split(x @ w_in, 2, axis=-1)
    return (jax.nn.silu(a) * b) @ w_out


def rotary(x, pos):
    half = x.shape[-1] // 2
    inv = ROPE_BASE ** (-jnp.arange(half, dtype=jnp.float32) / half)
    ang = pos[:, None] * inv[None, :]
    cos, sin = jnp.cos(ang)[None, :, None, :], jnp.sin(ang)[None, :, None, :]
    xf = x.astype(jnp.float32)
    x1, x2 = xf[..., :half], xf[..., half:]
    return jnp.concatenate([x1 * cos - x2 * sin, x2 * cos + x1 * sin], axis=-1)


def forgetting_attention(q, k, v, logf, valid):
    B, H, P, hd = q.shape
    nb = P // BLOCK
    c = jnp.cumsum(logf, axis=-1)
    kpos = jnp.arange(P)
    scale = hd ** -0.5

    def block(i):
        s0 = i * BLOCK
        qb = lax.dynamic_slice_in_dim(q, s0, BLOCK, axis=2)
        cb = lax.dynamic_slice_in_dim(c, s0, BLOCK, axis=2)
        logits = jnp.einsum('bhqd,bhkd->bhqk', qb, k) * scale + cb[..., None] - c[:, :, None, :]
        qpos = s0 + jnp.arange(BLOCK)
        allowed = (kpos[None, :] <= qpos[:, None]) & (valid[None, :] | (kpos[None, :] == qpos[:, None]))
        p = jax.nn.softmax(jnp.where(allowed[None, None], logits, NEG), axis=-1)
        return jnp.einsum('bhqk,bhkd->bhqd', p, v)

    out = lax.map(block, jnp.arange(nb))
    return out.transpose(1, 2, 0, 3, 4).reshape(B, H, P, hd)


def retention(q, k, v, log_gamma):
    B, H, P, dk = q.shape
    dv = v.shape[-1]
    nc = P // BLOCK
    qc = q.reshape(B, H, nc, BLOCK, dk)
    kc = k.reshape(B, H, nc, BLOCK, dk)
    vc = v.reshape(B, H, nc, BLOCK, dv)
    n = jnp.arange(BLOCK, dtype=jnp.float32)
    diff = n[:, None] - n[None, :]
    lg = log_gamma[:, None, None]
    decay = jnp.where(diff >= 0, jnp.exp(lg * jnp.maximum(diff, 0.0)), 0.0)
    scores = jnp.einsum('bhcnd,bhcmd->bhcnm', qc, kc) * decay[None, :, None]
    out = jnp.einsum('bhcnm,bhcme->bhcne', scores, vc)
    zeta = jnp.exp(log_gamma[:, None] * (BLOCK - 1 - n)[None, :])
    kv = jnp.einsum('bhcmd,bhcme->cbhde', kc * zeta[None, :, None, :, None], vc)
    chunk_decay = jnp.exp(log_gamma * BLOCK)[None, :, None, None]

    def step(R, kv_c):
        return chunk_decay * R + kv_c, R

    _, r_prev = lax.scan(step, jnp.zeros((B, H, dk, dv), jnp.float32), kv)
    xi = jnp.exp(log_gamma[:, None] * (n + 1.0)[None, :])
    out = out + jnp.einsum('bhcnd,cbhde->bhcne', qc * xi[None, :, None, :, None], r_prev)
    return out.reshape(B, H, P, dv)


def head_groupnorm(y, g):
    mu = jnp.mean(y, axis=-1, keepdims=True)
    var = jnp.mean(jnp.square(y - mu), axis=-1, keepdims=True)
    yn = (y - mu) * lax.rsqrt(var + GN_EPS)
    B, H, P, dv = y.shape
    return yn.transpose(0, 2, 1, 3).reshape(B, P, H * dv) * g.astype(jnp.float32)


def hybrid_layer(h, valid, pos, log_gamma, norm_ffn1, w_ffn1_in, w_ffn1_out, norm_mix, w_in,
                 b_forget, b_gate, fox_q_norm, fox_k_norm, w_o_fox, ret_gn, w_o_ret, w_out,
                 norm_ffn2, w_ffn2_in, w_ffn2_out):
    dt = h.dtype
    B, P, _ = h.shape
    h = h + 0.5 * swiglu(rmsnorm(h, norm_ffn1), w_ffn1_in, w_ffn1_out)
    u = rmsnorm(h, norm_mix)
    fq, fk, fv, ff, rq, rk, rv, rg, ga, gb = jnp.split(u @ w_in, IN_SPLITS, axis=-1)
    vmask = valid[None, :, None, None].astype(jnp.float32)

    def fox_heads(t):
        return t.reshape(B, P, FOX_HEADS, FOX_HD)
    q_a = rmsnorm(fox_heads(fq), fox_q_norm).astype(jnp.float32).transpose(0, 2, 1, 3)
    k_a = rmsnorm(fox_heads(fk), fox_k_norm).astype(jnp.float32).transpose(0, 2, 1, 3)
    v_a = fox_heads(fv).astype(jnp.float32).transpose(0, 2, 1, 3)
    logf = jax.nn.log_sigmoid(ff.astype(jnp.float32) + b_forget.astype(jnp.float32))
    logf = jnp.where(valid[None, :, None], logf, 0.0).transpose(0, 2, 1)
    y_a = forgetting_attention(q_a, k_a, v_a, logf, valid)
    y_a = y_a.transpose(0, 2, 1, 3).reshape(B, P, FOX_W).astype(dt)

    q_b = rotary(rq.reshape(B, P, RET_HEADS, RET_DK), pos).transpose(0, 2, 1, 3)
    k_b = (rotary(rk.reshape(B, P, RET_HEADS, RET_DK), pos) * (RET_DK ** -0.5) * vmask).transpose(0, 2, 1, 3)
    v_b = (rv.reshape(B, P, RET_HEADS, RET_DV).astype(jnp.float32) * vmask).transpose(0, 2, 1, 3)
    y_b = head_groupnorm(retention(q_b, k_b, v_b, log_gamma), ret_gn)
    y_b = (jax.nn.silu(rg.astype(jnp.float32)) * y_b).astype(dt)

    g_a = jax.nn.sigmoid(ga + b_gate[:D_MODEL])
    g_b = jax.nn.sigmoid(gb + b_gate[D_MODEL:])
    mixed = g_a * (y_a @ w_o_fox) + g_b * (y_b @ w_o_ret)
    h = h + mixed @ w_out

    h = h + 0.5 * swiglu(rmsnorm(h, norm_ffn2), w_ffn2_in, w_ffn2_out)
    return h


def setup_inputs(seed: int = 0) -> dict:
    key = jax.random.key(seed)
    ks = jax.random.split(key, 20)
    f32 = jnp.float32

    def w(k, shape, fan_in):
        return jax.random.normal(k, shape, f32) * fan_in ** -0.5

    def gain(k, shape):
        return 1.0 + 0.01 * jax.random.normal(k, shape, f32)

    L = DEPTH
    return {
        "x": jax.random.normal(ks[0], (BATCH, SEQ, D_MODEL), f32),
        "meta_tokens": jax.random.normal(ks[1], (N_META, D_MODEL), f32),
        "norm_ffn1": gain(ks[2], (L, D_MODEL)),
        "w_ffn1_in": w(ks[3], (L, D_MODEL, 2 * D_FF), D_MODEL),
        "w_ffn1_out": w(ks[4], (L, D_FF, D_MODEL), D_FF),
        "norm_mix": gain(ks[5], (L, D_MODEL)),
        "w_in": w(ks[6], (L, D_MODEL, N_IN), D_MODEL),
        "b_forget": FORGET_BIAS_INIT + 0.1 * jax.random.normal(ks[7], (L, FOX_HEADS), f32),
        "b_gate": 0.01 * jax.random.normal(ks[8], (L, 2 * D_MODEL), f32),
        "fox_q_norm": gain(ks[9], (L, FOX_HD)),
        "fox_k_norm": gain(ks[10], (L, FOX_HD)),
        "w_o_fox": w(ks[11], (L, FOX_W, D_MODEL), FOX_W),
        "ret_gn": gain(ks[12], (L, RET_V)),
        "w_o_ret": w(ks[13], (L, RET_V, D_MODEL), RET_V),
        "w_out": w(ks[14], (L, D_MODEL, D_MODEL), D_MODEL),
        "norm_ffn2": gain(ks[15], (L, D_MODEL)),
        "w_ffn2_in": w(ks[16], (L, D_MODEL, 2 * D_FF), D_MODEL),
        "w_ffn2_out": w(ks[17], (L, D_FF, D_MODEL), D_FF),
    }


def reference(x, meta_tokens, norm_ffn1, w_ffn1_in, w_ffn1_out, norm_mix, w_in, b_forget, b_gate,
              fox_q_norm, fox_k_norm, w_o_fox, ret_gn, w_o_ret, w_out, norm_ffn2, w_ffn2_in,
              w_ffn2_out):
    B, S, D = x.shape
    n_empty = LEAD - N_META
    h = jnp.concatenate([
        jnp.zeros((B, n_empty, D), x.dtype),
        jnp.broadcast_to(meta_tokens.astype(x.dtype)[None], (B, N_META, D)),
        x], axis=1)
    P = h.shape[1]
    idx = jnp.arange(P)
    valid = idx >= n_empty
    pos = (idx - n_empty).astype(jnp.float32)
    log_gamma = jnp.log1p(-jnp.exp2(-5.0 - jnp.arange(RET_HEADS, dtype=jnp.float32)))
    for l in range(DEPTH):
        h = hybrid_layer(h, valid, pos, log_gamma, norm_ffn1[l], w_ffn1_in[l], w_ffn1_out[l],
                         norm_mix[l], w_in[l], b_forget[l], b_gate[l], fox_q_norm[l], fox_k_norm[l],
                         w_o_fox[l], ret_gn[l], w_o_ret[l], w_out[l], norm_ffn2[l], w_ffn2_in[l],
                         w_ffn2_out[l])
    return h[:, LEAD:]
```

```python
import numpy as np
import concourse.bass as bass
import concourse.mybir as mybir
from concourse.bass_utils import run_bass_kernel_spmd
from contextlib import ExitStack

F32 = mybir.dt.float32
BF16 = mybir.dt.bfloat16
AF = mybir.ActivationFunctionType
ALU = mybir.AluOpType
AX = mybir.AxisListType

NCH = 33
NT = NCH * 128
D = 1024
DFF = 2816
NOWN = 16
EPS = 1e-6
GN_EPS = 1e-5
TILES = [list(range(0, 7)), list(range(7, 14)), list(range(14, 21)), list(range(21, 27)), list(range(27, 33))]
TW = 896
PARTS = [(0, 3), (3, 6), (6, 9), (9, 11)]
WIN_SLABS = [("fk0", 1024), ("fk1", 1536), ("fv0", 2048), ("fv1", 2560), ("fq0", 0), ("fq1", 512),
             ("rk", 3600), ("rks", -3600), ("rv0", 4112), ("rv1", 4624), ("rq", 3088), ("rqs", -3088), ("rg0", 5136), ("rg1", 5648),
             ("ga0", 6160), ("ga1", 6672), ("gb0", 7184), ("gb1", 7696)]
WIN_IDX = {n: i for i, (n, _) in enumerate(WIN_SLABS)}
FF_OFF = 3072


def is_own(i):
    return i % 2 == 0 and i > 0


class Tok:
    __slots__ = ("eng", "sem", "val", "key")

    def __init__(self, eng, sem, val, key):
        self.eng, self.sem, self.val, self.key = eng, sem, val, key


class Buf:
    def __init__(self, name):
        self.name = name
        self.w = None
        self.r = []
        self.dsem = None
        self.dcnt = 0
        self.dtok = None


class Ctx:
    def __init__(self, nc, es):
        self.nc, self.es = nc, es
        self.engs = {"pe": nc.tensor, "act": nc.scalar, "dve": nc.vector, "pool": nc.gpsimd, "sp": nc.sync}
        self.sem = {k: es.enter_context(nc.semaphore("s_" + k)) for k in ["pe", "act", "dve", "pool"]}
        self.cnt = {k: 0 for k in self.sem}
        self.seen = {k: {} for k in self.engs}
        self.pending = {k: [] for k in self.sem}
        self.nsem = 0
        self.ninst = {k: 0 for k in self.engs}
        self.dbufs = []

    def _deps(self, reads, writes):
        toks = []
        for b in reads:
            if b.w is not None:
                toks.append(b.w)
        for b in writes:
            if b.w is not None:
                toks.append(b.w)
            toks.extend(b.r)
        return toks

    def _wait(self, e, toks):
        need = {}
        for t in toks:
            if t.eng == e and e == "pe":
                continue
            assert t.val is not None, "dependency on un-milestoned instruction (%s)" % t.eng
            if t.key not in need or need[t.key][1] < t.val:
                need[t.key] = (t.sem, t.val)
        for key, (sem, val) in need.items():
            if self.seen[e].get(key, 0) >= val:
                continue
            self.engs[e].wait_ge(sem, val)
            self.seen[e][key] = val

    def op(self, e, fn, reads=(), writes=(), inc=True):
        self._wait(e, self._deps(reads, writes))
        ins = fn()
        self.ninst[e] += 1
        tok = Tok(e, self.sem[e], None, e)
        for b in reads:
            b.r.append(tok)
        for b in writes:
            b.w = tok
            b.r = []
        self.pending[e].append(tok)
        if inc:
            ins.then_inc(self.sem[e], 1)
            self.cnt[e] += 1
            for t in self.pending[e]:
                t.val = self.cnt[e]
            self.pending[e] = []
        return ins

    def dma(self, q, out, in_, reads=(), writes=(), sembuf=None, group=False, **kw):
        if sembuf.dsem is None:
            sembuf.dsem = self.es.enter_context(self.nc.semaphore("d%d" % self.nsem))
            sembuf.dkey = "d%d" % self.nsem
            self.nsem += 1
            self.dbufs.append(sembuf)
        toks = self._deps(reads, writes)
        if sembuf.dtok is not None and not group:
            toks.append(sembuf.dtok)
        self._wait(q, toks)
        ins = self.engs[q].dma_start(out=out, in_=in_, **kw)
        ins.then_inc(sembuf.dsem, 16)
        self.ninst[q] += 1
        sembuf.dcnt += 16
        tok = Tok("dma", sembuf.dsem, sembuf.dcnt, sembuf.dkey)
        sembuf.dtok = tok
        for b in reads:
            b.r.append(tok)
        for b in writes:
            b.w = tok
            b.r = []
        return tok

    def barrier(self):
        toks = []
        for e in self.sem:
            assert not self.pending[e], e
            if self.cnt[e] > 0:
                toks.append(Tok(e + "_b", self.sem[e], self.cnt[e], e))
        for b in self.dbufs:
            if b.dtok is not None:
                toks.append(b.dtok)
        for e in self.engs:
            self._wait(e, [t for t in toks if t.key != e])


class Rot:
    def __init__(self, items):
        self.items = items
        self.i = 0

    def next(self):
        it = self.items[self.i % len(self.items)]
        self.i += 1
        return it


def build_nc(dbg=False):
    nc = bass.Bass("TRN2", target_bir_lowering=False)

    def I(n, s):
        return nc.dram_tensor(n, list(s), F32, kind="ExternalInput")

    kind_s = "ExternalOutput" if dbg else "Internal"

    def S(n, s, dt):
        return nc.dram_tensor(n, list(s), dt, kind=kind_s)

    xs = I("xs", [NT, D])
    w1i, w1o = I("w_ffn1_in", [D, 2 * DFF]), I("w_ffn1_out", [DFF, D])
    w2i, w2o = I("w_ffn2_in", [D, 2 * DFF]), I("w_ffn2_out", [DFF, D])
    w_in = I("w_in", [D, 8208])
    w_of, w_or, w_ou = I("w_o_fox", [D, D]), I("w_o_ret", [D, D]), I("w_out", [D, D])
    g1, gm, g2 = I("norm_ffn1", [D]), I("norm_mix", [D]), I("norm_ffn2", [D])
    b_forget, b_gate = I("b_forget", [16]), I("b_gate", [2048])
    qn, kn, gn = I("fox_q_norm", [64]), I("fox_k_norm", [64]), I("ret_gn", [D])
    ck_t, sk_t = I("ck_t", [128, NCH, 128]), I("sk_t", [128, NCH, 128])
    cq_t, sq_t = I("cq_t", [128, NCH, 128]), I("sq_t", [128, NCH, 128])
    vcol_d, nvcol_d, vkcol_d = I("vcol", [128, NCH]), I("nvcol", [128, NCH]), I("vkcol", [128, NCH])
    decayT_d = I("decayT", [128, 4, 128])
    zeta_d, xi_d, gam_d = I("zeta", [128, 4]), I("xi", [128, 4]), I("gam", [128, 4])
    tri_d = I("tri", [128, 128])
    out = nc.dram_tensor("out", [NOWN * 128, D], F32, kind="ExternalOutput")

    wb1i, wb2i = S("wb1i", [11, 128, 8, 2, 256], BF16), S("wb2i", [11, 128, 8, 2, 256], BF16)
    wb1o, wb2o = S("wb1o", [DFF, D], BF16), S("wb2o", [DFF, D], BF16)
    wbin = S("wbin", [18, 128, 8, 512], BF16)
    wbff = S("wbff", [128, 8, 16], BF16)
    wbof, wbor, wbou = S("wbof", [D, D], BF16), S("wbor", [D, D], BF16), S("wbou", [D, D], BF16)
    H1 = S("H1", [NOWN * 128, D], F32)
    KTa = S("KTa", [16, 68, NT], BF16)
    VAa = S("VAa", [NT, 16 * 65], BF16)
    QTa = S("QTa", [16, 68, NOWN * 128], BF16)
    KZ = S("KZ", [NT, 512], BF16)
    VB = S("VB", [NT, 1024], BF16)
    KRT = S("KRT", [4, 128, NT], BF16)
    QRT = S("QRT", [4, 128, NOWN * 128], BF16)
    QXT = S("QXT", [4, 128, NOWN * 128], BF16)
    SRG = S("SRG", [NOWN * 128, D], F32)
    GT = S("GT", [16, 128, NOWN * 128], F32)
    YB = S("YB", [NOWN * 128, D], BF16)

    with ExitStack() as es:
        def sb(n, s, d, stack=None):
            return (stack or es).enter_context(nc.sbuf_tensor(n, list(s), d))

        PS = es.enter_context(nc.psum_tensor("PS", [128, 8, 512], F32))
        PSB = PS.bitcast(BF16)
        block = es.enter_context(nc.Block())
        c = Ctx(nc, es)
        psb_ = [Buf("ps%d" % i) for i in range(8)]

        def A(e, fn, r=(), w=(), inc=True):
            return c.op(e, fn, r, w, inc)

        V, G, ACT, PE = nc.vector, nc.gpsimd, nc.scalar, nc.tensor

        def bcast(t, off, dims):
            return bass.AP(tensor=t, offset=off, ap=[list(d) for d in dims])

        ident = sb("ident", [128, 128], BF16)
        identf = sb("identf", [128, 128], F32)
        tri_f = sb("tri_f", [128, 128], F32)
        ones_f = sb("ones_f", [128, 128], F32)
        maskT = sb("maskT", [128, 128], BF16)
        g1T, gmT, g2T = sb("g1T", [128, 8], F32), sb("gmT", [128, 8], F32), sb("g2T", [128, 8], F32)
        bgT = sb("bgT", [128, 16], F32)
        vcol, nvcol, vkcol = sb("vcol_s", [128, NCH], F32), sb("nvcol_s", [128, NCH], F32), sb("vkcol_s", [128, NCH], F32)
        Bc = Buf("consts")
        YAb = [Buf("YA%d" % i) for i in range(NOWN)]

        @block.sync
        def _(sync):
            sp = "sp"
            Bw = {}
            late = []

            def emit_late(n=1):
                for _ in range(n):
                    if late:
                        late.pop(0)()

            def castbuf(n):
                Bw[n] = Buf(n)
                return Bw[n]

            def cast_ffn_in(src, dst, name):
                sv = src.ap().rearrange("(kc p) n -> p kc n", p=128)
                for part, (s0, s1) in enumerate([(0, 4), (4, 8), (8, 11)]):
                    b = castbuf("%s_%d" % (name, part))
                    for sl in range(s0, s1):
                        for ab in range(2):
                            late.append(lambda b=b, sl=sl, ab=ab: c.dma("pool", dst.ap()[sl, :, :, ab, :], sv[:, :, ab * DFF + sl * 256: ab * DFF + sl * 256 + 256],
                                                                    writes=[b], sembuf=b, group=True))

            def cast_rows(src, dst, name, nsplit):
                b = castbuf(name)
                rows = src.shape[0]
                step = rows // nsplit
                for k in range(nsplit):
                    r0 = k * step
                    r1 = rows if k == nsplit - 1 else r0 + step
                    late.append(lambda b=b, r0=r0, r1=r1: c.dma("pool", dst.ap()[r0:r1, :], src.ap()[r0:r1, :], writes=[b], sembuf=b, group=True))

            def cast_win():
                sv = w_in.ap().rearrange("(kc p) n -> p kc n", p=128)
                b = castbuf("wbff")
                c.dma("pool", wbff.ap(), sv[:, :, FF_OFF:FF_OFF + 16], writes=[b], sembuf=b, group=True)

            cast_win()
            cast_rows(w_of, wbof, "wof", 2)
            cast_rows(w_or, wbor, "wor", 2)
            cast_rows(w_ou, wbou, "wou", 2)
            cast_ffn_in(w2i, wb2i, "w2i")
            cast_rows(w2o, wb2o, "w2o", 4)

            c.dma(sp, tri_f[:], tri_d.ap(), writes=[Bc], sembuf=Bc, group=True)
            for t, d_ in [(g1T, g1), (gmT, gm), (g2T, g2)]:
                c.dma(sp, t[:], d_.ap().rearrange("(k p) -> p k", p=128), writes=[Bc], sembuf=Bc, group=True,
                      allow_slow_non_contiguous=True)
            c.dma(sp, bgT[:], b_gate.ap().rearrange("(k p) -> p k", p=128), writes=[Bc], sembuf=Bc, group=True,
                  allow_slow_non_contiguous=True)
            for t, d_ in [(vcol, vcol_d), (nvcol, nvcol_d), (vkcol, vkcol_d)]:
                c.dma(sp, t[:], d_.ap(), writes=[Bc], sembuf=Bc, group=True)
            A("pool", lambda: G.memset(identf[:], 0.0), w=[Bc])
            A("pool", lambda: G.affine_select(out=identf[:], in_=identf[:], pattern=[[-1, 128]], compare_op=ALU.not_equal,
                                              fill=1.0, base=0, channel_multiplier=1), r=[Bc], w=[Bc])
            A("pool", lambda: G.memset(ones_f[:], 1.0), w=[Bc])
            A("dve", lambda: V.tensor_copy(out=ident[:], in_=identf[:]), r=[Bc], w=[Bc])
            A("dve", lambda: V.tensor_copy(out=maskT[:], in_=tri_f[:]), r=[Bc], w=[Bc])

            psrot = Rot(list(range(8)))

            def ps():
                b = psrot.next()
                return b, psb_[b]

            def ps2():
                if psrot.i % 2 == 1:
                    psrot.i += 1
                b = psrot.next()
                psrot.next()
                return b, [psb_[b], psb_[b + 1]]

            def norm_tile(n, hx, hxb, gainT, xT, xTb, small, xn_rot, pre=None):
                for g0 in range(0, n, 4):
                    grp = list(range(g0, min(n, g0 + 4)))
                    items = []
                    for lc in grp:
                        if pre is not None:
                            pre(lc)
                        src, srcb = hx[:, lc, :], hxb[lc]
                        st, stb = small.next()
                        xn, xnb = xn_rot.next()
                        A("act", lambda: ACT.activation(out=xn[:], in_=src, func=AF.Square, accum_out=st[:, 0:1]), r=[srcb], w=[stb, xnb])
                        A("act", lambda: ACT.activation(out=st[:, 1:2], in_=st[:, 0:1], func=AF.Sqrt, scale=1.0 / D, bias=EPS), r=[stb], w=[stb])
                        items.append((lc, src, srcb, st, stb, xn, xnb))
                    for n_, (lc, src, srcb, st, stb, xn, xnb) in enumerate(items):
                        A("dve", lambda: V.reciprocal(out=st[:, 2:3], in_=st[:, 1:2]), r=[stb], w=[stb])
                        if n_ % 2 == 0:
                            A("dve", lambda: V.tensor_scalar(out=xn[:], in0=src, scalar1=st[:, 2:3], scalar2=None, op0=ALU.mult), r=[srcb, stb], w=[xnb])
                        else:
                            A("act", lambda: ACT.activation(out=xn[:], in_=src, func=AF.Copy, scale=st[:, 2:3]), r=[srcb, stb], w=[xnb])
                    for (lc, src, srcb, st, stb, xn, xnb) in items:
                        b, bb = ps()
                        for k in range(8):
                            A("pe", lambda: PE.transpose(out=PSB[:, b, k * 128:(k + 1) * 128], in_=xn[:, k * 128:(k + 1) * 128], identity=ident[:]),
                              r=[xnb, Bc], w=[bb], inc=(k == 7))
                        A("dve", lambda: V.tensor_tensor(out=xT[:, :, lc * 128:(lc + 1) * 128], in0=PSB[:, b, :].rearrange("p (k t) -> p k t", k=8),
                                                         in1=bcast(gainT, 0, [[8, 128], [1, 8], [0, 128]]), op=ALU.mult), r=[bb, Bc], w=[xTb[lc]])

            class Slabs:
                def __init__(self, slots, jobs, stage=None):
                    self.slots, self.jobs, self.stage = slots, jobs, stage
                    self.loadptr, self.convptr, self.nload_f32, self.nconv_f32 = 0, 0, 0, 0
                    self.stage_of = {}

                def _load(self, j):
                    t, b = self.slots[j % len(self.slots)]
                    job = self.jobs[j]
                    if job[0] == "bf16":
                        c.dma(sp, t[:], job[1], reads=job[2], writes=[b], sembuf=b)
                    elif job[0] == "f32":
                        st, stb = self.stage.next()
                        self.stage_of[j] = (st, stb)
                        for k_, (vf, src) in enumerate(job[1]):
                            c.dma(sp, vf(st), src, writes=[stb], sembuf=stb, group=(k_ > 0))
                        self.nload_f32 += 1

                def _conv(self, j):
                    t, b = self.slots[j % len(self.slots)]
                    job = self.jobs[j]
                    if job[0] == "f32":
                        st, stb = self.stage_of.pop(j)
                        A("act", lambda: ACT.copy(out=t[:, 0:2048], in_=st[:, 0:2048]), r=[stb], w=[b])
                        A("dve", lambda: V.tensor_copy(out=t[:, 2048:4096], in_=st[:, 2048:4096]), r=[stb], w=[b])
                        c.dma("pool", job[2], t[:], reads=[b], writes=[job[3]], sembuf=b)
                        self.nconv_f32 += 1
                    elif job[0] == "swap":
                        ts_, bs_ = self.slots[(j - 1) % len(self.slots)]
                        ov = t[:].rearrange("p (k h t d) -> p k h t d", k=8, h=4, t=2)
                        iv = ts_[:].rearrange("p (k h t d) -> p k h t d", k=8, h=4, t=2)
                        A("act", lambda: ACT.copy(out=ov[:, :, :, 0, :], in_=iv[:, :, :, 1, :]), r=[bs_], w=[b])
                        A("dve", lambda: V.tensor_copy(out=ov[:, :, :, 1, :], in_=iv[:, :, :, 0, :]), r=[bs_], w=[b])

                def get(self, j, oldest=None):
                    oldest = j if oldest is None else oldest
                    n = len(self.slots)
                    while True:
                        progressed = False
                        while self.convptr < self.loadptr and self.convptr <= j + 1 and self.convptr < oldest + n:
                            self._conv(self.convptr)
                            self.convptr += 1
                            progressed = True
                        while self.loadptr < min(len(self.jobs), oldest + n):
                            if self.jobs[self.loadptr][0] == "f32" and self.nload_f32 - self.nconv_f32 >= len(self.stage.items):
                                break
                            self._load(self.loadptr)
                            self.loadptr += 1
                            progressed = True
                        if not progressed:
                            break
                    assert self.convptr > j, (self.convptr, self.loadptr, j)
                    return self.slots[j % n]

            def ffn(tile_n, hx, hxb, xT, xTb, gT, gTb, w2h, w2hb, slabs, j0, wbo, wbob, sil_rot, last=None, first_src=None, stage=None):
                ntok = tile_n * 128
                cgs = [(o, min(512, ntok - o)) for o in range(0, ntok, 512)]
                for half, (sl0, sl1) in enumerate(PARTS):
                    nj = (sl1 - sl0) * 2
                    jbase = sl0 * 2
                    if first_src is None:
                        c.dma(sp, w2h[:, 0:nj, :], wbo.ap()[jbase * 128:(jbase + nj) * 128, :].rearrange("(j p) n -> p j n", p=128),
                              reads=[wbob[half]], writes=[w2hb], sembuf=w2hb)
                    else:
                        for q0 in range(nj):
                            st, stb = stage.next()
                            r0 = (jbase + q0) * 128
                            c.dma(sp, st[:], first_src.ap()[r0:r0 + 128, :], writes=[stb], sembuf=stb)
                            A("act" if q0 % 2 == 0 else "dve", lambda: (ACT.copy if q0 % 2 == 0 else V.tensor_copy)(out=w2h[:, q0, :], in_=st[:]),
                              r=[stb], w=[w2hb])
                        c.dma("pool", wbo.ap()[jbase * 128:(jbase + nj) * 128, :].rearrange("(j p) n -> p j n", p=128), w2h[:, 0:nj, :],
                              reads=[w2hb], writes=[wbob[half]], sembuf=w2hb)
                    for sl in range(sl0, sl1):
                        wt, wtb = slabs.get(j0 + sl)
                        wv = wt[:].rearrange("p (k a c) -> p k a c", k=8, a=2)
                        for jj in range(2):
                            jl = (sl - sl0) * 2 + jj
                            for (o, n) in cgs:
                                lcs = list(range(o // 128, (o + n) // 128))
                                pa, pab = ps()
                                pb, pbb = ps()
                                for k in range(8):
                                    A("pe", lambda: PE.matmul(PS[:, pa, 0:n], lhsT=wv[:, k, 0, jj * 128:(jj + 1) * 128], rhs=xT[:, k, o:o + n],
                                                              start=(k == 0), stop=(k == 7)), r=[wtb] + [xTb[l] for l in lcs], w=[pab], inc=(k == 7))
                                for k in range(8):
                                    A("pe", lambda: PE.matmul(PS[:, pb, 0:n], lhsT=wv[:, k, 1, jj * 128:(jj + 1) * 128], rhs=xT[:, k, o:o + n],
                                                              start=(k == 0), stop=(k == 7)), r=[wtb] + [xTb[l] for l in lcs], w=[pbb], inc=(k == 7))
                                sa, sab = sil_rot.next()
                                A("act", lambda: ACT.activation(out=sa[:, 0:n], in_=PS[:, pa, 0:n], func=AF.Silu), r=[pab], w=[sab])
                                A("dve", lambda: V.tensor_tensor(out=gT[:, jl, o:o + n], in0=sa[:, 0:n], in1=PS[:, pb, 0:n], op=ALU.mult),
                                  r=[sab, pbb], w=[gTb[o // 512]])
                    for lc in range(tile_n):
                        for fh in range(2):
                            po, pob = ps()
                            for jl in range(nj):
                                A("pe", lambda: PE.matmul(PS[:, po, :], lhsT=gT[:, jl, lc * 128:(lc + 1) * 128], rhs=w2h[:, jl, fh * 512:(fh + 1) * 512],
                                                          start=(jl == 0), stop=(jl == nj - 1)), r=[gTb[lc // 4], w2hb], w=[pob], inc=(jl == nj - 1))
                            A("dve", lambda: V.scalar_tensor_tensor(out=hx[:, lc, fh * 512:(fh + 1) * 512], in0=PS[:, po, :], scalar=0.5,
                                                                    in1=hx[:, lc, fh * 512:(fh + 1) * 512], op0=ALU.mult, op1=ALU.add),
                              r=[pob, hxb[lc]], w=[hxb[lc]])
                        if half == len(PARTS) - 1 and last is not None:
                            last(lc)

            with ExitStack() as ea:
                hx = sb("hx", [128, 7, D], F32, ea)
                xT = sb("xT", [128, 8, TW], BF16, ea)
                gT = sb("gT", [128, 6, TW], BF16, ea)
                w2h = sb("w2h", [128, 6, D], BF16, ea)
                slots = [(sb("wsl%d" % i, [128, 4096], BF16, ea), Buf("wsl%d" % i)) for i in range(4)]
                small = Rot([(sb("sm%d" % i, [128, 64], F32, ea), Buf("sm%d" % i)) for i in range(8)])
                xn_rot = Rot([(sb("xn%d" % i, [128, D], BF16, ea), Buf("xn%d" % i)) for i in range(4)])
                sil_rot = Rot([(sb("sil%d" % i, [128, 512], F32, ea), Buf("sil%d" % i)) for i in range(3)])
                tf_rot = Rot([(sb("tf%d" % i, [128, 512], F32, ea), Buf("tf%d" % i)) for i in range(4)])
                tb_rot = Rot([(sb("tb%d" % i, [128, 1024], BF16, ea), Buf("tb%d" % i)) for i in range(6)])
                kA_rot = Rot([(sb("kA%d" % i, [128, 16, 68], BF16, ea), Buf("kA%d" % i)) for i in range(2)])
                qA_rot = Rot([(sb("qA%d" % i, [128, 16, 68], BF16, ea), Buf("qA%d" % i)) for i in range(2)])
                vA_rot = Rot([(sb("vA%d" % i, [128, 16, 65], BF16, ea), Buf("vA%d" % i)) for i in range(2)])
                st_rot = Rot([(sb("stg%d" % i, [128, 16, 128], BF16, ea), Buf("stg%d" % i)) for i in range(2)])
                cs_rot = Rot([(sb("cs%d" % i, [128, 2, 128], F32, ea), Buf("cs%d" % i)) for i in range(3)])
                wff = sb("wff", [128, 8, 16], BF16, ea)
                bfb = sb("bfb", [128, 16], F32, ea)
                qgc = sb("qgc", [128, 1], F32, ea)
                kgc = sb("kgc", [128, 1], F32, ea)
                tf2_rot = Rot([(sb("tf2_%d" % i, [128, 1024], F32, ea), Buf("tf2_%d" % i)) for i in range(2)])
                cc_all = sb("cc_all", [128, NCH + 1, 16], F32, ea)
                cref_all = sb("cref_all", [128, NCH + 1, 16], F32, ea)
                zeta, xi = sb("zeta_s", [128, 4], F32, ea), sb("xi_s", [128, 4], F32, ea)
                hxb = [Buf("hx%d" % i) for i in range(9)]
                xTb = [Buf("xT%d" % i) for i in range(9)]
                gTb = [Buf("gT%d" % i) for i in range(3)]
                w2hb = Buf("w2h")
                Ba = Buf("constsA")
                ccb = Buf("cc")

                c.dma(sp, wff[:], wbff.ap(), reads=[Bw["wbff"]], writes=[Ba], sembuf=Ba, group=True)
                c.dma(sp, bfb[:], bcast(b_forget, 0, [[0, 128], [1, 16]]), writes=[Ba], sembuf=Ba, group=True)
                A("pool", lambda: G.memset(qgc[:], 1.0), w=[Ba])
                A("pool", lambda: G.memset(kgc[:], 1.0), w=[Ba])
                c.dma(sp, qgc[0:64, :], qn.ap().rearrange("(d o) -> d o", o=1), writes=[Ba], sembuf=Ba)
                c.dma(sp, kgc[0:64, :], kn.ap().rearrange("(d o) -> d o", o=1), writes=[Ba], sembuf=Ba)
                c.dma(sp, zeta[:], zeta_d.ap(), writes=[Ba], sembuf=Ba, group=True)
                c.dma(sp, xi[:], xi_d.ap(), writes=[Ba], sembuf=Ba, group=True)
                A("act", lambda: ACT.mul(qgc[0:64, :], qgc[0:64, :], 0.125), r=[Ba], w=[Ba])
                A("pool", lambda: G.memset(cc_all[:, 0, :], 0.0), w=[ccb])
                A("pool", lambda: G.memset(cref_all[:, 0, :], 0.0), r=[ccb], w=[ccb])
                for (t, _) in kA_rot.items:
                    A("pool", lambda: G.memset(t[:, :, 67:68], 1.0), w=[_])
                for (t, _) in qA_rot.items:
                    A("pool", lambda: G.memset(t[:, :, 64:67], 1.0), w=[_])

                jobs = []
                tile_j0 = []

                def v_ffn(t):
                    return t[:].rearrange("p (k a c) -> p k a c", k=8, a=2)

                def v_win(t):
                    return t[:].rearrange("p (k c) -> p k c", k=8)

                Bw1i = [Buf("w1i_%d" % sl) for sl in range(11)]
                Bwin = {n: Buf("win_" + n) for n, _ in WIN_SLABS}
                Bw1o = [Buf("w1o_%d" % k) for k in range(len(PARTS))]
                stage = Rot([(sb("stage%d" % i, [128, 4096], F32, ea), Buf("stage%d" % i)) for i in range(2)])
                w1i_v = w1i.ap().rearrange("(kc p) n -> p kc n", p=128)
                win_v = w_in.ap().rearrange("(kc p) n -> p kc n", p=128)
                for ti, tile in enumerate(TILES):
                    tile_j0.append(len(jobs))
                    for sl in range(11):
                        dst = wb1i.ap()[sl].rearrange("p k a c -> p (k a c)")
                        if ti == 0:
                            parts = [((lambda st, ab=ab: st[:].rearrange("p (k a c) -> p k a c", k=8, a=2)[:, :, ab, :]),
                                      w1i_v[:, :, ab * DFF + sl * 256: ab * DFF + sl * 256 + 256]) for ab in range(2)]
                            jobs.append(("f32", parts, dst, Bw1i[sl]))
                        else:
                            jobs.append(("bf16", dst, [Bw1i[sl]]))
                    for n, off in WIN_SLABS:
                        dst = wbin.ap()[WIN_IDX[n]].rearrange("p k c -> p (k c)")
                        if off < 0:
                            jobs.append(("swap",))
                        elif ti == 0:
                            parts = [((lambda st: st[:].rearrange("p (k c) -> p k c", k=8)), win_v[:, :, off:off + 512])]
                            jobs.append(("f32", parts, dst, Bwin[n]))
                        else:
                            jobs.append(("bf16", dst, [Bwin[n]]))
                slabs = Slabs(slots, jobs, stage)

                def load_x(ti_):
                    for lc_, i_ in enumerate(TILES[ti_]):
                        c.dma(sp, hx[:, lc_, :], xs.ap()[i_ * 128:(i_ + 1) * 128, :], writes=[hxb[lc_]], sembuf=hxb[lc_])

                for ti, tile in enumerate(TILES):
                    tn = len(tile)
                    ntok = tn * 128
                    j0 = tile_j0[ti]
                    own_lc = [lc for lc in range(tn) if is_own(tile[lc])]
                    if ti == 0:
                        load_x(0)
                    norm_tile(tn, hx, hxb, g1T, xT, xTb, small, xn_rot)
                    ffn(tn, hx, hxb, xT, xTb, gT, gTb, w2h, w2hb, slabs, j0, wb1o, Bw1o, sil_rot,
                        first_src=(w1o if ti == 0 else None), stage=tf2_rot)
                    def store_h1(lc):
                        i = tile[lc]
                        if is_own(i):
                            s = i // 2 - 1
                            c.dma("pool", H1.ap()[s * 128:(s + 1) * 128, :], hx[:, lc, :], reads=[hxb[lc]], sembuf=hxb[lc])

                    norm_tile(tn, hx, hxb, gmT, xT, xTb, small, xn_rot, pre=store_h1)
                    if ti + 1 < len(TILES):
                        load_x(ti + 1)
                    jw = j0 + 11
                    J = lambda n_: jw + WIN_IDX[n_]

                    def proj(lc, wt, wtb, ncols=512):
                        b, bb = ps()
                        wv = wt[:].rearrange("p (k c) -> p k c", k=8) if ncols == 512 else wt
                        for k in range(8):
                            A("pe", lambda: PE.matmul(PS[:, b, 0:ncols], lhsT=xT[:, k, lc * 128:(lc + 1) * 128], rhs=wv[:, k, 0:ncols],
                                                      start=(k == 0), stop=(k == 7)), r=[wtb, xTb[lc]], w=[bb], inc=(k == 7))
                        return b, bb

                    def proj2(lc, wa, wab, wb_, wbb):
                        b, bbs = ps2()
                        for g_, (wt, wtb) in enumerate([(wa, wab), (wb_, wbb)]):
                            wv = wt[:].rearrange("p (k c) -> p k c", k=8)
                            for k in range(8):
                                A("pe", lambda: PE.matmul(PS[:, b + g_, :], lhsT=xT[:, k, lc * 128:(lc + 1) * 128], rhs=wv[:, k, :],
                                                          start=(k == 0), stop=(k == 7)), r=[wtb, xTb[lc]], w=[bbs[g_]], inc=(k == 7))
                        return b, bbs

                    for lc in range(tn):
                        i = tile[lc]
                        b, bb = proj(lc, wff, Ba, 16)
                        st, stb = small.next()
                        A("dve", lambda: V.tensor_tensor(out=st[:, 0:16], in0=PS[:, b, 0:16], in1=bfb[:], op=ALU.add), r=[bb, Ba], w=[stb])
                        A("act", lambda: ACT.activation(out=st[:, 16:32], in_=st[:, 0:16], func=AF.Exp, scale=-1.0), r=[stb], w=[stb])
                        A("act", lambda: ACT.activation(out=st[:, 32:48], in_=st[:, 16:32], func=AF.Ln, bias=1.0), r=[stb], w=[stb])
                        A("dve", lambda: V.tensor_scalar(out=st[:, 48:64], in0=st[:, 32:48], scalar1=nvcol[:, i:i + 1], scalar2=None, op0=ALU.mult),
                          r=[stb, Bc], w=[stb])
                        b2, bb2 = ps()
                        A("pe", lambda: PE.matmul(PS[:, b2, 0:16], lhsT=tri_f[:], rhs=st[:, 48:64], start=True, stop=True), r=[Bc, stb], w=[bb2], inc=False)
                        A("pe", lambda: PE.matmul(PS[:, b2, 16:32], lhsT=ones_f[:], rhs=st[:, 48:64], start=True, stop=True), r=[Bc, stb], w=[bb2])
                        A("dve", lambda: V.tensor_tensor(out=cc_all[:, i + 1, :], in0=PS[:, b2, 0:16], in1=cref_all[:, i, :], op=ALU.add), r=[bb2, ccb], w=[ccb])
                        A("dve", lambda: V.tensor_tensor(out=cref_all[:, i + 1, :], in0=PS[:, b2, 16:32], in1=cref_all[:, i, :], op=ALU.add), r=[bb2, ccb], w=[ccb])

                    def headnorm(b, bbs, dstA, dstAb):
                        tf, tfb = tf2_rot.next()
                        st, stb = small.next()
                        A("act", lambda: ACT.activation(out=tf[:].rearrange("p (g c) -> p g c", g=2), in_=PS[:, b:b + 2, :], func=AF.Square), r=bbs, w=[tfb])
                        A("dve", lambda: V.tensor_reduce(out=st[:, 0:16], in_=tf[:].rearrange("p (h d) -> p h d", h=16), axis=AX.X, op=ALU.add), r=[tfb], w=[stb])
                        A("act", lambda: ACT.activation(out=st[:, 16:32], in_=st[:, 0:16], func=AF.Sqrt, scale=1.0 / 64, bias=EPS), r=[stb], w=[stb])
                        A("dve", lambda: V.reciprocal(out=st[:, 32:48], in_=st[:, 16:32]), r=[stb], w=[stb])
                        A("dve", lambda: V.tensor_tensor(out=dstA[:, :, 0:64], in0=PS[:, b:b + 2, :].rearrange("p g (h d) -> p (g h) d", h=8),
                                                         in1=bcast(st, 32, [[64, 128], [1, 16], [0, 64]]), op=ALU.mult), r=bbs + [stb], w=[dstAb])

                    def transpose16(srcA, srcAb, width, dst_dram_fn, gcol=None):
                        stg, stgb = st_rot.next()
                        for half in range(2):
                            b, bb = ps()
                            for hh in range(8):
                                h = half * 8 + hh
                                A("pe", lambda: PE.transpose(out=PSB[0:width, b, hh * 128:(hh + 1) * 128], in_=srcA[:, h, 0:width], identity=ident[:]),
                                  r=[srcAb, Bc], w=[bb], inc=(hh == 7))
                            if gcol is None:
                                A("act", lambda: ACT.copy(out=stg[0:width, half * 8:half * 8 + 8, :], in_=PSB[0:width, b, :].rearrange("p (h t) -> p h t", h=8)),
                                  r=[bb], w=[stgb])
                            else:
                                A("act", lambda: ACT.activation(out=stg[0:width, half * 8:half * 8 + 8, :], in_=PSB[0:width, b, :].rearrange("p (h t) -> p h t", h=8),
                                                                func=AF.Copy, scale=gcol[0:width, 0:1]), r=[bb, Ba], w=[stgb])
                        c.dma("pool", dst_dram_fn(), stg[0:width, :, :], reads=[stgb], sembuf=stgb)

                    w0, w0b = slabs.get(J("fk0"))
                    w1, w1b = slabs.get(J("fk1"), J("fk0"))
                    pend = None

                    def k_stage1(lc):
                        i = tile[lc]
                        b, bbs = proj2(lc, w0, w0b, w1, w1b)
                        kA, kAb = kA_rot.next()
                        headnorm(b, bbs, kA, kAb)
                        cs, csb = cs_rot.next()
                        st, stb = small.next()
                        hb = st[:].bitcast(BF16)
                        cc = cc_all[:, i + 1, :]
                        A("dve", lambda: V.tensor_copy(out=hb[:, 0:16], in_=cc), r=[ccb], w=[stb])
                        A("dve", lambda: V.tensor_tensor(out=cs[:, 0, 0:16], in0=cc, in1=hb[:, 0:16], op=ALU.subtract), r=[ccb, stb], w=[csb])
                        A("dve", lambda: V.tensor_copy(out=hb[:, 16:32], in_=cs[:, 0, 0:16]), r=[csb], w=[stb])
                        A("dve", lambda: V.tensor_tensor(out=cs[:, 1, 0:16], in0=cs[:, 0, 0:16], in1=hb[:, 16:32], op=ALU.subtract), r=[csb, stb], w=[csb])
                        A("dve", lambda: V.tensor_scalar(out=kA[:, :, 64:65], in0=hb[:, 0:16].rearrange("p (h o) -> p h o", o=1), scalar1=-1.0, scalar2=None, op0=ALU.mult),
                          r=[stb], w=[kAb])
                        A("dve", lambda: V.tensor_scalar(out=kA[:, :, 65:66], in0=hb[:, 16:32].rearrange("p (h o) -> p h o", o=1), scalar1=-1.0, scalar2=None, op0=ALU.mult),
                          r=[stb], w=[kAb])
                        A("dve", lambda: V.tensor_scalar(out=kA[:, :, 66:67], in0=cs[:, 1, 0:16].rearrange("p (h o) -> p h o", o=1), scalar1=-1.0, scalar2=None, op0=ALU.mult),
                          r=[csb], w=[kAb])
                        return (kA, kAb, i)

                    def k_stage2(p_):
                        kA, kAb, i = p_
                        transpose16(kA, kAb, 68, lambda: KTa.ap()[:, :, i * 128:(i + 1) * 128].rearrange("h r t -> r h t"), gcol=kgc)

                    pend = k_stage1(0)
                    for lc in range(tn):
                        nxt = k_stage1(lc + 1) if lc + 1 < tn else None
                        k_stage2(pend)
                        pend = nxt
                        if ti >= 1:
                            emit_late(1)

                    w0, w0b = slabs.get(J("fv0"))
                    w1, w1b = slabs.get(J("fv1"), J("fv0"))
                    for lc in range(tn):
                        i = tile[lc]
                        vA, vAb = vA_rot.next()
                        for g_, (wt, wtb) in enumerate([(w0, w0b), (w1, w1b)]):
                            b, bb = proj(lc, wt, wtb)
                            A("act", lambda: ACT.activation(out=vA[:, 8 * g_:8 * g_ + 8, 0:64], in_=PS[:, b, :].rearrange("p (h d) -> p h d", h=8),
                                                            func=AF.Copy, scale=vcol[:, i:i + 1]), r=[bb, Bc], w=[vAb])
                        A("dve", lambda: V.tensor_copy(out=vA[:, :, 64:65], in_=bcast(vcol, i, [[NCH, 128], [0, 16], [1, 1]])), r=[Bc], w=[vAb])
                        c.dma("pool", VAa.ap()[i * 128:(i + 1) * 128, :], vA[:].rearrange("p h d -> p (h d)"), reads=[vAb], sembuf=vAb)
                        if ti >= 1:
                            emit_late(1)

                    w0, w0b = slabs.get(J("fq0"))
                    w1, w1b = slabs.get(J("fq1"), J("fq0"))

                    def q_stage1(lc):
                        i = tile[lc]
                        b, bbs = proj2(lc, w0, w0b, w1, w1b)
                        qA, qAb = qA_rot.next()
                        headnorm(b, bbs, qA, qAb)
                        A("dve", lambda: V.tensor_copy(out=qA[:, :, 67:68], in_=cref_all[:, i + 1, :].rearrange("p (h o) -> p h o", o=1)), r=[ccb], w=[qAb])
                        return (qA, qAb, i)

                    def q_stage2(p_):
                        qA, qAb, i = p_
                        s = i // 2 - 1
                        transpose16(qA, qAb, 68, lambda: QTa.ap()[:, :, s * 128:(s + 1) * 128].rearrange("h r t -> r h t"), gcol=qgc)

                    if own_lc:
                        pend = q_stage1(own_lc[0])
                        for n_, lc in enumerate(own_lc):
                            nxt = q_stage1(own_lc[n_ + 1]) if n_ + 1 < len(own_lc) else None
                            q_stage2(pend)
                            pend = nxt

                    def rotary(b, bbs, i, ct, st_):
                        cs, csb = cs_rot.next()
                        c.dma(sp, cs[:, 0, :], ct.ap()[:, i, :], writes=[csb], sembuf=csb, group=True)
                        c.dma(sp, cs[:, 1, :], st_.ap()[:, i, :], writes=[csb], sembuf=csb, group=True)
                        t1, t1b = tf_rot.next()
                        t2, t2b = tf_rot.next()
                        A("dve", lambda: V.tensor_tensor(out=t1[:].rearrange("p (h d) -> p h d", h=4), in0=PS[:, b, :].rearrange("p (h d) -> p h d", h=4),
                                                         in1=bcast(cs, 0, [[256, 128], [0, 4], [1, 128]]), op=ALU.mult), r=[bbs[0], csb], w=[t1b])
                        A("dve", lambda: V.tensor_tensor(out=t2[:].rearrange("p (h d) -> p h d", h=4), in0=PS[:, b + 1, :].rearrange("p (h d) -> p h d", h=4),
                                                         in1=bcast(cs, 128, [[256, 128], [0, 4], [1, 128]]), op=ALU.mult), r=[bbs[1], csb], w=[t2b])
                        o, ob = tb_rot.next()
                        A("dve", lambda: V.tensor_tensor(out=o[:, 0:512], in0=t1[:], in1=t2[:], op=ALU.add), r=[t1b, t2b], w=[ob])
                        return o, ob

                    def transpose4(src, srcb, dst_fn):
                        stg, stgb = st_rot.next()
                        b, bb = ps()
                        for h in range(4):
                            A("pe", lambda: PE.transpose(out=PSB[:, b, h * 128:(h + 1) * 128], in_=src[:, h * 128:(h + 1) * 128], identity=ident[:]),
                              r=[srcb, Bc], w=[bb], inc=(h == 3))
                        A("act", lambda: ACT.copy(out=stg[:, 0:4, :], in_=PSB[:, b, 0:512].rearrange("p (h t) -> p h t", h=4)), r=[bb], w=[stgb])
                        c.dma("pool", dst_fn(), stg[:, 0:4, :], reads=[stgb], sembuf=stgb)

                    w0, w0b = slabs.get(J("rk"))
                    w1, w1b = slabs.get(J("rks"), J("rk"))

                    def rk_stage1(lc):
                        i = tile[lc]
                        b, bbs = proj2(lc, w0, w0b, w1, w1b)
                        kr, krb = rotary(b, bbs, i, ck_t, sk_t)
                        kz, kzb = tb_rot.next()
                        A("dve", lambda: V.tensor_tensor(out=kz[:, 0:512].rearrange("p (h d) -> p h d", h=4), in0=kr[:, 0:512].rearrange("p (h d) -> p h d", h=4),
                                                         in1=bcast(zeta, 0, [[4, 128], [1, 4], [0, 128]]), op=ALU.mult), r=[krb, Ba], w=[kzb])
                        c.dma("pool", KZ.ap()[i * 128:(i + 1) * 128, :], kz[:, 0:512], reads=[kzb], sembuf=kzb)
                        return (kr, krb, i)

                    pend = rk_stage1(0)
                    for lc in range(tn):
                        nxt = rk_stage1(lc + 1) if lc + 1 < tn else None
                        kr, krb, i = pend
                        transpose4(kr, krb, lambda: KRT.ap()[:, :, i * 128:(i + 1) * 128].rearrange("h d t -> d h t"))
                        pend = nxt

                    w0, w0b = slabs.get(J("rv0"))
                    w1, w1b = slabs.get(J("rv1"), J("rv0"))
                    for lc in range(tn):
                        i = tile[lc]
                        vb, vbb = tb_rot.next()
                        for g_, (wt, wtb) in enumerate([(w0, w0b), (w1, w1b)]):
                            b, bb = proj(lc, wt, wtb)
                            A("act", lambda: ACT.activation(out=vb[:, g_ * 512:(g_ + 1) * 512], in_=PS[:, b, :], func=AF.Copy, scale=vcol[:, i:i + 1]),
                              r=[bb, Bc], w=[vbb])
                        c.dma("pool", VB.ap()[i * 128:(i + 1) * 128, :], vb[:], reads=[vbb], sembuf=vbb)

                    w0, w0b = slabs.get(J("rq"))
                    w1, w1b = slabs.get(J("rqs"), J("rq"))

                    def rq_stage1(lc):
                        i = tile[lc]
                        b, bbs = proj2(lc, w0, w0b, w1, w1b)
                        qr, qrb = rotary(b, bbs, i, cq_t, sq_t)
                        qx, qxb = tb_rot.next()
                        A("dve", lambda: V.tensor_tensor(out=qx[:, 0:512].rearrange("p (h d) -> p h d", h=4), in0=qr[:, 0:512].rearrange("p (h d) -> p h d", h=4),
                                                         in1=bcast(xi, 0, [[4, 128], [1, 4], [0, 128]]), op=ALU.mult), r=[qrb, Ba], w=[qxb])
                        return (qr, qrb, qx, qxb, i)

                    if own_lc:
                        pend = rq_stage1(own_lc[0])
                        for n_, lc in enumerate(own_lc):
                            nxt = rq_stage1(own_lc[n_ + 1]) if n_ + 1 < len(own_lc) else None
                            qr, qrb, qx, qxb, i = pend
                            s = i // 2 - 1
                            transpose4(qr, qrb, lambda: QRT.ap()[:, :, s * 128:(s + 1) * 128].rearrange("h d t -> d h t"))
                            transpose4(qx, qxb, lambda: QXT.ap()[:, :, s * 128:(s + 1) * 128].rearrange("h d t -> d h t"))
                            pend = nxt

                    w0, w0b = slabs.get(J("rg0"))
                    w1, w1b = slabs.get(J("rg1"), J("rg0"))
                    for lc in own_lc:
                        i = tile[lc]
                        s = i // 2 - 1
                        for g_, (wt, wtb) in enumerate([(w0, w0b), (w1, w1b)]):
                            b, bb = proj(lc, wt, wtb)
                            tf, tfb = tf_rot.next()
                            A("act", lambda: ACT.activation(out=tf[:], in_=PS[:, b, :], func=AF.Silu), r=[bb], w=[tfb])
                            c.dma("pool", SRG.ap()[s * 128:(s + 1) * 128, g_ * 512:(g_ + 1) * 512], tf[:], reads=[tfb], sembuf=tfb)

                    if own_lc:
                        ogs = [own_lc[k:k + 4] for k in range(0, len(own_lc), 4)]
                        for gi in range(4):
                            wt, wtb = slabs.get(J("ga0") + gi)
                            wv = wt[:].rearrange("p (k c) -> p k c", k=8)
                            for fb in range(4):
                                f16 = gi * 4 + fb
                                for og in ogs:
                                    n = len(og) * 128
                                    s0 = tile[og[0]] // 2 - 1
                                    b, bb = ps()
                                    for k in range(8):
                                        rhs = bcast(xT, k * TW + og[0] * 128, [[8 * TW, 128], [256, len(og)], [1, 128]])
                                        A("pe", lambda: PE.matmul(PS[:, b, 0:n], lhsT=wv[:, k, fb * 128:(fb + 1) * 128], rhs=rhs, start=(k == 0), stop=(k == 7)),
                                          r=[wtb] + [xTb[l] for l in og], w=[bb], inc=(k == 7))
                                    tf, tfb = tf_rot.next()
                                    A("act", lambda: ACT.activation(out=tf[:, 0:n], in_=PS[:, b, 0:n], func=AF.Sigmoid, bias=bgT[:, f16:f16 + 1]), r=[bb, Bc], w=[tfb])
                                    c.dma("pool", GT.ap()[f16, :, s0 * 128:s0 * 128 + n], tf[:, 0:n], reads=[tfb], sembuf=tfb)
                emit_late(len(late))
                c.barrier()

            ya_cm = nc.sbuf_tensor("YA", [128, NOWN, D], BF16)
            YA = ya_cm.__enter__()
            eb1 = ExitStack()
            eb1.__enter__()
            if True:
                eb = eb1
                R = sb("R", [128, 4, 256], F32, eb)
                Rb = sb("Rb", [128, 4, 256], BF16, eb)
                decT = sb("decT", [128, 4, 128], F32, eb)
                gam = sb("gam_s", [128, 4], F32, eb)
                gnb = sb("gnb", [128, D], F32, eb)
                Rbuf, Rbb, Bb = Buf("R"), Buf("Rb"), Buf("constsB")
                kz_rot = Rot([(sb("kz%d" % i, [128, 512], BF16, eb), Buf("kz%d" % i)) for i in range(3)])
                vb_rot = Rot([(sb("vb%d" % i, [128, 1024], BF16, eb), Buf("vb%d" % i)) for i in range(3)])
                kt_rot = Rot([(sb("kt%d" % i, [128, 3, 4, 128], BF16, eb), Buf("kt%d" % i)) for i in range(2)])
                sg_rot = Rot([(sb("sg%d" % i, [128, D], F32, eb), Buf("sg%d" % i)) for i in range(2)])
                s4_rot = Rot([(sb("sT%d" % i, [128, 512], BF16, eb), Buf("sT%d" % i)) for i in range(2)])
                o_rot = Rot([(sb("o%d" % i, [128, D], F32, eb), Buf("o%d" % i)) for i in range(2)])
                yb_rot = Rot([(sb("yb%d" % i, [128, D], BF16, eb), Buf("yb%d" % i)) for i in range(2)])
                small1 = Rot([(sb("smb%d" % i, [128, 64], F32, eb), Buf("smb%d" % i)) for i in range(4)])
                c.dma(sp, decT[:], decayT_d.ap(), writes=[Bb], sembuf=Bb, group=True)
                c.dma(sp, gam[:], gam_d.ap(), writes=[Bb], sembuf=Bb, group=True)
                c.dma(sp, gnb[:], bcast(gn, 0, [[0, 128], [1, D]]), writes=[Bb], sembuf=Bb, group=True)
                A("pool", lambda: G.memset(R[:], 0.0), w=[Rbuf])
                A("pool", lambda: G.memset(Rb[:], 0.0), w=[Rbb])
                BO, BS = 6, 7

                def loadB(i):
                    kz, kzb = kz_rot.next()
                    vb, vbb = vb_rot.next()
                    c.dma(sp, kz[:], KZ.ap()[i * 128:(i + 1) * 128, :], writes=[kzb], sembuf=kzb)
                    c.dma(sp, vb[:], VB.ap()[i * 128:(i + 1) * 128, :], writes=[vbb], sembuf=vbb)
                    o = None
                    if is_own(i):
                        s = i // 2 - 1
                        kt, ktb = kt_rot.next()
                        sg, sgb = sg_rot.next()
                        c.dma(sp, kt[:, 0, :, :], KRT.ap()[:, :, i * 128:(i + 1) * 128].rearrange("h d t -> d h t"), writes=[ktb], sembuf=ktb, group=True)
                        c.dma(sp, kt[:, 1, :, :], QRT.ap()[:, :, s * 128:(s + 1) * 128].rearrange("h d t -> d h t"), writes=[ktb], sembuf=ktb, group=True)
                        c.dma(sp, kt[:, 2, :, :], QXT.ap()[:, :, s * 128:(s + 1) * 128].rearrange("h d t -> d h t"), writes=[ktb], sembuf=ktb, group=True)
                        c.dma(sp, sg[:], SRG.ap()[s * 128:(s + 1) * 128, :], writes=[sgb], sembuf=sgb)
                        o = (kt, ktb, sg, sgb, s)
                    return (kz, kzb, vb, vbb, o)

                def b1_gen():
                    nxt = loadB(0)
                    yield
                    for i in range(NCH):
                        kz, kzb, vb, vbb, o = nxt
                        if i + 1 < NCH:
                            nxt = loadB(i + 1)
                        if o is not None:
                            kt, ktb, sg, sgb, s = o
                            ot, otb = o_rot.next()
                            for h in range(4):
                                A("pe", lambda: PE.matmul(PS[:, BS, h * 128:(h + 1) * 128], lhsT=kt[:, 0, h, :], rhs=kt[:, 1, h, :], start=True, stop=True),
                                  r=[ktb], w=[psb_[BS]], inc=(h == 3))
                            yield
                            s4, s4b = s4_rot.next()
                            A("dve", lambda: V.tensor_tensor(out=s4[:], in0=PS[:, BS, :], in1=decT[:].rearrange("p h n -> p (h n)"), op=ALU.mult), r=[psb_[BS], Bb], w=[s4b])
                            yield
                            for hp in range(2):
                                for hh in range(2):
                                    h = hp * 2 + hh
                                    A("pe", lambda: PE.matmul(PS[:, BO, hh * 256:(hh + 1) * 256], lhsT=s4[:, h * 128:(h + 1) * 128], rhs=vb[:, h * 256:(h + 1) * 256],
                                                              start=True, stop=False), r=[s4b, vbb], w=[psb_[BO]], inc=False)
                                    A("pe", lambda: PE.matmul(PS[:, BO, hh * 256:(hh + 1) * 256], lhsT=kt[:, 2, h, :], rhs=Rb[:, h, :], start=False, stop=True),
                                      r=[ktb, Rbb], w=[psb_[BO]])
                                yield
                                A("dve", lambda: V.tensor_copy(out=ot[:, hp * 512:(hp + 1) * 512], in_=PS[:, BO, :]), r=[psb_[BO]], w=[otb])
                                yield
                            st, stb = small1.next()
                            sq, sqb = o_rot.next()
                            A("dve", lambda: V.tensor_reduce(out=st[:, 0:4], in_=ot[:].rearrange("p (h e) -> p h e", h=4), axis=AX.X, op=ALU.add), r=[otb], w=[stb])
                            A("dve", lambda: V.tensor_tensor(out=sq[:], in0=ot[:], in1=ot[:], op=ALU.mult), r=[otb], w=[sqb])
                            yield
                            A("dve", lambda: V.tensor_reduce(out=st[:, 4:8], in_=sq[:].rearrange("p (h e) -> p h e", h=4), axis=AX.X, op=ALU.add), r=[sqb], w=[stb])
                            A("dve", lambda: V.tensor_scalar(out=st[:, 8:12], in0=st[:, 0:4], scalar1=1.0 / 256, scalar2=None, op0=ALU.mult), r=[stb], w=[stb])
                            A("dve", lambda: V.tensor_tensor(out=st[:, 12:16], in0=st[:, 8:12], in1=st[:, 8:12], op=ALU.mult), r=[stb], w=[stb])
                            A("dve", lambda: V.scalar_tensor_tensor(out=st[:, 16:20], in0=st[:, 4:8], scalar=1.0 / 256, in1=st[:, 12:16], op0=ALU.mult, op1=ALU.subtract),
                              r=[stb], w=[stb])
                            yield
                            A("act", lambda: ACT.activation(out=st[:, 20:24], in_=st[:, 16:20], func=AF.Sqrt, bias=GN_EPS), r=[stb], w=[stb])
                            yield
                            A("dve", lambda: V.reciprocal(out=st[:, 24:28], in_=st[:, 20:24]), r=[stb], w=[stb])
                            for h in range(4):
                                A("dve", lambda: V.tensor_scalar(out=sq[:, h * 256:(h + 1) * 256], in0=ot[:, h * 256:(h + 1) * 256], scalar1=st[:, 8 + h:9 + h],
                                                                 scalar2=st[:, 24 + h:25 + h], op0=ALU.subtract, op1=ALU.mult), r=[otb, stb], w=[sqb])
                            yield
                            A("dve", lambda: V.tensor_tensor(out=sq[:], in0=sq[:], in1=gnb[:], op=ALU.mult), r=[sqb, Bb], w=[sqb])
                            yb, ybb = yb_rot.next()
                            A("dve", lambda: V.tensor_tensor(out=yb[:], in0=sq[:], in1=sg[:], op=ALU.mult), r=[sqb, sgb], w=[ybb])
                            c.dma("pool", YB.ap()[s * 128:(s + 1) * 128, :], yb[:], reads=[ybb], sembuf=ybb)
                            yield
                        if i + 1 < NCH:
                            for hp in range(2):
                                for hh in range(2):
                                    h = hp * 2 + hh
                                    A("pe", lambda: PE.matmul(PS[:, BS, hh * 256:(hh + 1) * 256], lhsT=kz[:, h * 128:(h + 1) * 128], rhs=vb[:, h * 256:(h + 1) * 256],
                                                              start=True, stop=True), r=[kzb, vbb], w=[psb_[BS]], inc=(hh == 1))
                                yield
                                for hh in range(2):
                                    h = hp * 2 + hh
                                    A("dve", lambda: V.scalar_tensor_tensor(out=R[:, h, :], in0=R[:, h, :], scalar=gam[:, h:h + 1], in1=PS[:, BS, hh * 256:(hh + 1) * 256],
                                                                            op0=ALU.mult, op1=ALU.add), r=[psb_[BS], Rbuf, Bb], w=[Rbuf])
                                yield
                            A("dve", lambda: V.tensor_copy(out=Rb[:], in_=R[:]), r=[Rbuf], w=[Rbb])
                            yield

                b1 = b1_gen()

            with ExitStack() as eb:
                NS = 2
                KTs = [(sb("KT%d" % i, [128, NT], BF16, eb), Buf("KT%d" % i)) for i in range(NS)]
                VAs = [(sb("VA%d" % i, [128, NCH, 65], BF16, eb), Buf("VA%d" % i)) for i in range(NS)]
                QTs = [(sb("QT%d" % i, [128, NOWN * 128], BF16, eb), Buf("QT%d" % i)) for i in range(NS)]
                pt_rot = Rot([(sb("pt%d" % i, [128, 1024], BF16, eb), Buf("pt%d" % i)) for i in range(3)])
                oT_rot = Rot([(sb("oT%d" % i, [128, 512], F32, eb), Buf("oT%d" % i)) for i in range(2)])
                small = Rot([(sb("smc%d" % i, [128, 8], F32, eb), Buf("smc%d" % i)) for i in range(4)])
                prot = Rot([0, 2])
                arot = Rot([4, 5])

                def loadH(h):
                    kt, ktb = KTs[h % NS]
                    va, vab = VAs[h % NS]
                    qt, qtb = QTs[h % NS]
                    c.dma(sp, kt[0:68, :], KTa.ap()[h], writes=[ktb], sembuf=ktb)
                    c.dma(sp, va[:], VAa.ap()[:, h * 65:(h + 1) * 65].rearrange("(c p) d -> p c d", p=128), writes=[vab], sembuf=vab)
                    c.dma(sp, qt[0:68, :], QTa.ap()[h], writes=[qtb], sembuf=qtb)

                loadH(0)
                for h in range(16):
                    if h + 1 < 16:
                        loadH(h + 1)
                    kt, ktb = KTs[h % NS]
                    va, vab = VAs[h % NS]
                    qt, qtb = QTs[h % NS]
                    work = []
                    for g in range(4):
                        nfull = 8 * g + 3
                        j = 0
                        while j + 1 < nfull - 1 or (j + 1 < nfull and (j % 2 == 0) and j + 1 <= 8 * g + 1):
                            work.append((g, [j, j + 1]))
                            j += 2
                        while j <= 8 * g + 8:
                            work.append((g, [j]))
                            j += 1

                    def stageS(wk):
                        g, js = wk
                        pb_ = prot.next()
                        pt, ptb = pt_rot.next()
                        col0s = []
                        for n_, j in enumerate(js):
                            smin = max(4 * g, (j - 1) // 2)
                            col0 = (smin - 4 * g) * 128
                            col0s.append(col0)
                            A("pe", lambda: PE.matmul(PS[:, pb_ + n_, col0:512], lhsT=kt[0:68, j * 128:(j + 1) * 128],
                                                      rhs=qt[0:68, 4 * g * 128 + col0:(4 * g + 4) * 128], start=True, stop=True), r=[ktb, qtb], w=[psb_[pb_ + n_]])
                        if len(js) == 2:
                            A("act", lambda: ACT.activation(out=pt[:].rearrange("p (b c) -> p b c", b=2), in_=PS[:, pb_:pb_ + 2, :], func=AF.Exp),
                              r=[psb_[pb_], psb_[pb_ + 1]], w=[ptb])
                        else:
                            col0 = col0s[0]
                            A("act", lambda: ACT.activation(out=pt[:, col0:512], in_=PS[:, pb_, col0:512], func=AF.Exp), r=[psb_[pb_]], w=[ptb])
                        for n_, j in enumerate(js):
                            if j >= 2 and j % 2 == 0 and 4 * g <= (j - 2) // 2 <= 4 * g + 3:
                                cd = n_ * 512 + ((j - 2) // 2 - 4 * g) * 128
                                A("dve", lambda: V.tensor_tensor(out=pt[:, cd:cd + 128], in0=pt[:, cd:cd + 128], in1=maskT[:], op=ALU.mult), r=[ptb, Bc], w=[ptb])
                        return (pt, ptb, col0s)

                    acc = {}

                    def stagePV(wk, p_):
                        g, js = wk
                        pt, ptb, col0s = p_
                        jmax = 8 * g + 8
                        for n_, j in enumerate(js):
                            col0 = col0s[n_]
                            if j == 0:
                                acc[g] = arot.next()
                            ab_ = acc[g]
                            abb = psb_[ab_]
                            A("pe", lambda: PE.matmul(PS[0:65, ab_, col0:512], lhsT=va[:, j, :], rhs=pt[:, n_ * 512 + col0:n_ * 512 + 512], start=(j == 0), stop=(j == jmax)),
                              r=[ptb, vab], w=[abb])
                        if js[-1] == jmax:
                            oT, oTb = oT_rot.next()
                            A("dve", lambda: V.tensor_copy(out=oT[0:65, :], in_=PS[0:65, ab_, :]), r=[abb], w=[oTb])
                            tb_ = prot.next()
                            tbb = psb_[tb_]
                            for sl in range(4):
                                A("pe", lambda: PE.transpose(out=PS[:, tb_, sl * 65:(sl + 1) * 65], in_=oT[0:65, sl * 128:(sl + 1) * 128], identity=identf[0:65, 0:65]),
                                  r=[oTb, Bc], w=[tbb], inc=(sl == 3))
                            st, stb = small.next()
                            A("dve", lambda: V.reciprocal(out=st[:, 0:4], in_=bass.AP(tensor=PS, offset=tb_ * 512 + 64, ap=[[4096, 128], [65, 4]])), r=[tbb], w=[stb])
                            A("dve", lambda: V.tensor_tensor(out=YA[:, 4 * g:4 * g + 4, h * 64:(h + 1) * 64],
                                                             in0=bass.AP(tensor=PS, offset=tb_ * 512, ap=[[4096, 128], [65, 4], [1, 64]]),
                                                             in1=bcast(st, 0, [[8, 128], [1, 4], [0, 64]]), op=ALU.mult),
                              r=[tbb, stb], w=[YAb[s_] for s_ in range(4 * g, 4 * g + 4)])

                    DEPTH = 1
                    pendq = [stageS(work[k_]) for k_ in range(min(DEPTH, len(work)))]
                    for n_, wk in enumerate(work):
                        if n_ + DEPTH < len(work):
                            pendq.append(stageS(work[n_ + DEPTH]))
                        stagePV(wk, pendq.pop(0))
                        if n_ % 3 == 2:
                            next(b1, None)
                for _ in b1:
                    pass
                c.barrier()
            eb1.close()

            with ExitStack() as ec:
                Wof = sb("Wof", [128, 8, D], BF16, ec)
                Wor = sb("Wor", [128, 8, D], BF16, ec)
                Wou = sb("Wou", [128, 8, D], BF16, ec)
                Bcw = Buf("constsC")
                c.dma(sp, Wof[:], wbof.ap().rearrange("(k p) n -> p k n", p=128), reads=[Bw["wof"]], writes=[Bcw], sembuf=Bcw, group=True)
                c.dma(sp, Wor[:], wbor.ap().rearrange("(k p) n -> p k n", p=128), reads=[Bw["wor"]], writes=[Bcw], sembuf=Bcw, group=True)
                c.dma(sp, Wou[:], wbou.ap().rearrange("(k p) n -> p k n", p=128), reads=[Bw["wou"]], writes=[Bcw], sembuf=Bcw, group=True)
                yT_rot = Rot([(sb("yT%d" % i, [128, 8, 512], BF16, ec), Buf("yT%d" % i)) for i in range(4)])
                ybl_rot = Rot([(sb("ybl%d" % i, [128, D], BF16, ec), Buf("ybl%d" % i)) for i in range(3)])
                mix_rot = Rot([(sb("mix%d" % i, [128, 8, 512], BF16, ec), Buf("mix%d" % i)) for i in range(2)])
                g_rot = Rot([(sb("gl%d" % i, [128, 2, 512], F32, ec), Buf("gl%d" % i)) for i in range(3)])
                t_rot = Rot([(sb("tc%d" % i, [128, 512], F32, ec), Buf("tc%d" % i)) for i in range(4)])
                h_rot = Rot([(sb("hc%d" % i, [128, D], F32, ec), Buf("hc%d" % i)) for i in range(3)])
                H1b = [Buf("H1_%d" % i) for i in range(NOWN)]
                for t4 in range(4):
                    s0 = t4 * 4
                    yaT, yaTb = yT_rot.next()
                    ybT, ybTb = yT_rot.next()
                    for sl in range(4):
                        s = s0 + sl
                        ybl, yblb = ybl_rot.next()
                        c.dma(sp, ybl[:], YB.ap()[s * 128:(s + 1) * 128, :], writes=[yblb], sembuf=yblb)
                        for (src_fn, srcb, dst, dstb) in [(lambda k: YA[:, s, k * 128:(k + 1) * 128], YAb[s], yaT, yaTb),
                                                          (lambda k: ybl[:, k * 128:(k + 1) * 128], yblb, ybT, ybTb)]:
                            b, bb = ps()
                            for k in range(8):
                                A("pe", lambda: PE.transpose(out=PSB[:, b, k * 128:(k + 1) * 128], in_=src_fn(k), identity=ident[:]), r=[srcb, Bc], w=[bb], inc=(k == 7))
                            A("act", lambda: ACT.copy(out=dst[:, :, sl * 128:(sl + 1) * 128], in_=PSB[:, b, :].rearrange("p (k t) -> p k t", k=8)), r=[bb], w=[dstb])
                    mix, mixb = mix_rot.next()
                    for fb in range(8):
                        gl, glb = g_rot.next()
                        c.dma(sp, gl[:, 0, :], GT.ap()[fb, :, s0 * 128:(s0 + 4) * 128], writes=[glb], sembuf=glb, group=True)
                        c.dma(sp, gl[:, 1, :], GT.ap()[8 + fb, :, s0 * 128:(s0 + 4) * 128], writes=[glb], sembuf=glb, group=True)
                        ba, bab = ps()
                        for k in range(8):
                            A("pe", lambda: PE.matmul(PS[:, ba, :], lhsT=Wof[:, k, fb * 128:(fb + 1) * 128], rhs=yaT[:, k, :], start=(k == 0), stop=(k == 7)),
                              r=[Bcw, yaTb], w=[bab], inc=(k == 7))
                        bbk, bbb = ps()
                        for k in range(8):
                            A("pe", lambda: PE.matmul(PS[:, bbk, :], lhsT=Wor[:, k, fb * 128:(fb + 1) * 128], rhs=ybT[:, k, :], start=(k == 0), stop=(k == 7)),
                              r=[Bcw, ybTb], w=[bbb], inc=(k == 7))
                        t1, t1b = t_rot.next()
                        t2, t2b = t_rot.next()
                        A("dve", lambda: V.tensor_tensor(out=t1[:], in0=PS[:, ba, :], in1=gl[:, 0, :], op=ALU.mult), r=[bab, glb], w=[t1b])
                        A("dve", lambda: V.tensor_tensor(out=t2[:], in0=PS[:, bbk, :], in1=gl[:, 1, :], op=ALU.mult), r=[bbb, glb], w=[t2b])
                        A("dve", lambda: V.tensor_tensor(out=mix[:, fb, :], in0=t1[:], in1=t2[:], op=ALU.add), r=[t1b, t2b], w=[mixb])
                    for sl in range(4):
                        s = s0 + sl
                        hc, hcb = h_rot.next()
                        c.dma(sp, hc[:], H1.ap()[s * 128:(s + 1) * 128, :], reads=[H1b[s]], writes=[hcb], sembuf=hcb)
                        for fh in range(2):
                            b, bb = ps()
                            for k in range(8):
                                A("pe", lambda: PE.matmul(PS[:, b, :], lhsT=mix[:, k, sl * 128:(sl + 1) * 128], rhs=Wou[:, k, fh * 512:(fh + 1) * 512],
                                                          start=(k == 0), stop=(k == 7)), r=[mixb, Bcw], w=[bb], inc=(k == 7))
                            A("dve", lambda: V.tensor_tensor(out=hc[:, fh * 512:(fh + 1) * 512], in0=PS[:, b, :], in1=hc[:, fh * 512:(fh + 1) * 512], op=ALU.add),
                              r=[bb, hcb], w=[hcb])
                        c.dma("pool", H1.ap()[s * 128:(s + 1) * 128, :], hc[:], reads=[hcb], writes=[H1b[s]], sembuf=hcb)
                c.barrier()
            ya_cm.__exit__(None, None, None)

            with ExitStack() as ec:
                hx = sb("hx2", [128, 8, D], F32, ec)
                xT = sb("xT2", [128, 8, 1024], BF16, ec)
                gT = sb("gT2", [128, 6, 1024], BF16, ec)
                w2h = sb("w2h2", [128, 6, D], BF16, ec)
                slots = [(sb("wsc%d" % i, [128, 4096], BF16, ec), Buf("wsc%d" % i)) for i in range(4)]
                small = Rot([(sb("smd%d" % i, [128, 64], F32, ec), Buf("smd%d" % i)) for i in range(6)])
                xn_rot = Rot([(sb("xnd%d" % i, [128, D], BF16, ec), Buf("xnd%d" % i)) for i in range(4)])
                sil_rot = Rot([(sb("sild%d" % i, [128, 512], F32, ec), Buf("sild%d" % i)) for i in range(3)])
                hxb = [Buf("hxd%d" % i) for i in range(8)]
                xTb = [Buf("xTd%d" % i) for i in range(8)]
                gTb = [Buf("gTd%d" % i) for i in range(2)]
                w2hb = Buf("w2hd")
                outb = Buf("out")

                def v_ffn2(t):
                    return t[:].rearrange("p (k a c) -> p k a c", k=8, a=2)

                jobs = []
                for t8 in range(2):
                    for sl in range(11):
                        jobs.append(("bf16", wb2i.ap()[sl].rearrange("p k a c -> p (k a c)"), [Bw["w2i_%d" % (0 if sl < 4 else 1 if sl < 8 else 2)]]))
                slabs = Slabs(slots, jobs)
                for t8 in range(2):
                    for lc in range(8):
                        s = t8 * 8 + lc
                        c.dma(sp, hx[:, lc, :], H1.ap()[s * 128:(s + 1) * 128, :], writes=[hxb[lc]], sembuf=hxb[lc])
                    norm_tile(8, hx, hxb, g2T, xT, xTb, small, xn_rot)

                    def last(lc, t8=t8):
                        s = t8 * 8 + lc
                        c.dma("pool", out.ap()[s * 128:(s + 1) * 128, :], hx[:, lc, :], reads=[hxb[lc]], writes=[outb], sembuf=hxb[lc])

                    ffn(8, hx, hxb, xT, xTb, gT, gTb, w2h, w2hb, slabs, t8 * 11, wb2o, [Bw["w2o"]] * len(PARTS), sil_rot, last=last)
                c.barrier()
            print("inst counts", c.ninst, "sems", c.nsem)
    return nc


def host_consts(p):
    idx = np.arange(NT)
    orig = idx if p == 1 else idx - 128
    valid = (orig >= 112).astype(np.float32)
    pos = np.where(valid > 0, orig - 112, 0).astype(np.float64)
    half = 64
    inv = 10000.0 ** (-np.arange(half, dtype=np.float64) / half)
    ang = pos[:, None] * inv[None, :]
    cos = np.cos(ang)
    sin = np.sin(ang)

    def pc(a):
        return np.ascontiguousarray(a.reshape(NCH, 128, -1).transpose(1, 0, 2)).astype(np.float32)

    vc = np.ascontiguousarray(valid.reshape(NCH, 128).T)
    log_gamma = np.log1p(-np.exp2(-5.0 - np.arange(4, dtype=np.float64)))
    n = np.arange(128, dtype=np.float64)
    diff = n[None, :] - n[:, None]
    decT = np.where(diff[:, None, :] >= 0, np.exp(log_gamma[None, :, None] * np.maximum(diff[:, None, :], 0.0)), 0.0).astype(np.float32)
    zeta = np.exp(log_gamma[None, :] * (127 - n)[:, None]).astype(np.float32)
    xi = np.exp(log_gamma[None, :] * (n + 1.0)[:, None]).astype(np.float32)
    gam = np.broadcast_to(np.exp(log_gamma * 128)[None, :], (128, 4)).astype(np.float32)
    tri = (n[:, None] <= n[None, :]).astype(np.float32)
    C2 = np.concatenate([cos, cos], 1)
    S2 = np.concatenate([-sin, sin], 1)
    vk = (valid * (128 ** -0.5))[:, None]
    return dict(ck_t=pc(C2 * vk), sk_t=pc(S2 * vk), cq_t=pc(C2), sq_t=pc(S2), vcol=vc, nvcol=-vc, vkcol=(vc * (128 ** -0.5)).astype(np.float32),
                decayT=np.ascontiguousarray(decT), zeta=np.ascontiguousarray(zeta), xi=np.ascontiguousarray(xi),
                gam=np.ascontiguousarray(gam), tri=tri)


_NC_CACHE = {}


def make_in_maps(inputs):
    x = np.asarray(inputs["x"], dtype=np.float32)
    meta = np.asarray(inputs["meta_tokens"], dtype=np.float32)
    shared = {}
    for k in ["w_ffn1_in", "w_ffn1_out", "w_ffn2_in", "w_ffn2_out", "w_in", "w_o_fox", "w_o_ret", "w_out"]:
        shared[k] = np.ascontiguousarray(np.asarray(inputs[k], dtype=np.float32)[0])
    for k in ["norm_ffn1", "norm_mix", "norm_ffn2", "b_forget", "b_gate", "fox_q_norm", "fox_k_norm", "ret_gn"]:
        shared[k] = np.ascontiguousarray(np.asarray(inputs[k], dtype=np.float32)[0])
    consts = [host_consts(0), host_consts(1)]
    in_maps = []
    for core in range(8):
        b, p = core // 2, core % 2
        lead = np.concatenate([np.zeros((112, D), np.float32), meta], axis=0)
        if p == 1:
            seq = np.concatenate([lead, x[b]], axis=0)
        else:
            seq = np.concatenate([np.zeros((128, D), np.float32), lead, x[b, :NT - 256]], axis=0)
        m = dict(shared)
        m.update(consts[p])
        m["xs"] = np.ascontiguousarray(seq)
        in_maps.append(m)
    return in_maps


def kernel(**inputs):
    if "nc" not in _NC_CACHE:
        _NC_CACHE["nc"] = build_nc()
    nc = _NC_CACHE["nc"]
    in_maps = make_in_maps(inputs)
    res = run_bass_kernel_spmd(nc, in_maps, core_ids=list(range(8)))
    B = 4
    out = np.empty((B, 4096, D), np.float32)
    for core in range(8):
        b, p = core // 2, core % 2
        o = res.results[core]["out"].reshape(NOWN, 128, D)
        for s in range(NOWN):
            i = 2 * s + 2
            orig = i if p == 1 else i - 1
            out[b, (orig - 1) * 128:orig * 128] = o[s]
    return out
```

```python
import numpy as np
import concourse.bass as bass
import concourse.mybir as mybir
from concourse.bass_utils import run_bass_kernel_spmd
from contextlib import ExitStack

F32 = mybir.dt.float32
BF16 = mybir.dt.bfloat16
AF = mybir.ActivationFunctionType
ALU = mybir.AluOpType
AX = mybir.AxisListType

NCH = 33
NT = NCH * 128
D = 1024
DFF = 2816
NOWN = 16
EPS = 1e-6
GN_EPS = 1e-5
TILES = [list(range(0, 7)), list(range(7, 14)), list(range(14, 21)), list(range(21, 27)), list(range(27, 33))]
TW = 896
PARTS = [(0, 3), (3, 6), (6, 9), (9, 11)]
WIN_SLABS = [("fk0", 1024), ("fk1", 1536), ("fv0", 2048), ("fv1", 2560), ("fq0", 0), ("fq1", 512),
             ("rk", 3600), ("rks", -3600), ("rv0", 4112), ("rv1", 4624), ("rq", 3088), ("rqs", -3088), ("rg0", 5136), ("rg1", 5648),
             ("ga0", 6160), ("ga1", 6672), ("gb0", 7184), ("gb1", 7696)]
WIN_IDX = {n: i for i, (n, _) in enumerate(WIN_SLABS)}
FF_OFF = 3072


def is_own(i):
    return i % 2 == 0 and i > 0


class Tok:
    __slots__ = ("eng", "sem", "val", "key")

    def __init__(self, eng, sem, val, key):
        self.eng, self.sem, self.val, self.key = eng, sem, val, key


class Buf:
    def __init__(self, name):
        self.name = name
        self.w = None
        self.r = []
        self.dsem = None
        self.dcnt = 0
        self.dtok = None


class Ctx:
    def __init__(self, nc, es):
        self.nc, self.es = nc, es
        self.engs = {"pe": nc.tensor, "act": nc.scalar, "dve": nc.vector, "pool": nc.gpsimd, "sp": nc.sync}
        self.sem = {k: es.enter_context(nc.semaphore("s_" + k)) for k in ["pe", "act", "dve", "pool"]}
        self.cnt = {k: 0 for k in self.sem}
        self.seen = {k: {} for k in self.engs}
        self.pending = {k: [] for k in self.sem}
        self.nsem = 0
        self.ninst = {k: 0 for k in self.engs}
        self.dbufs = []

    def _deps(self, reads, writes):
        toks = []
        for b in reads:
            if b.w is not None:
                toks.append(b.w)
        for b in writes:
            if b.w is not None:
                toks.append(b.w)
            toks.extend(b.r)
        return toks

    def _wait(self, e, toks):
        need = {}
        for t in toks:
            if t.eng == e and e == "pe":
                continue
            assert t.val is not None, "dependency on un-milestoned instruction (%s)" % t.eng
            if t.key not in need or need[t.key][1] < t.val:
                need[t.key] = (t.sem, t.val)
        for key, (sem, val) in need.items():
            if self.seen[e].get(key, 0) >= val:
                continue
            self.engs[e].wait_ge(sem, val)
            self.seen[e][key] = val

    def op(self, e, fn, reads=(), writes=(), inc=True):
        self._wait(e, self._deps(reads, writes))
        ins = fn()
        self.ninst[e] += 1
        tok = Tok(e, self.sem[e], None, e)
        for b in reads:
            b.r.append(tok)
        for b in writes:
            b.w = tok
            b.r = []
        self.pending[e].append(tok)
        if inc:
            ins.then_inc(self.sem[e], 1)
            self.cnt[e] += 1
            for t in self.pending[e]:
                t.val = self.cnt[e]
            self.pending[e] = []
        return ins

    def dma(self, q, out, in_, reads=(), writes=(), sembuf=None, group=False, **kw):
        if sembuf.dsem is None:
            sembuf.dsem = self.es.enter_context(self.nc.semaphore("d%d" % self.nsem))
            sembuf.dkey = "d%d" % self.nsem
            self.nsem += 1
            self.dbufs.append(sembuf)
        toks = self._deps(reads, writes)
        if sembuf.dtok is not None and not group:
            toks.append(sembuf.dtok)
        self._wait(q, toks)
        ins = self.engs[q].dma_start(out=out, in_=in_, **kw)
        ins.then_inc(sembuf.dsem, 16)
        self.ninst[q] += 1
        sembuf.dcnt += 16
        tok = Tok("dma", sembuf.dsem, sembuf.dcnt, sembuf.dkey)
        sembuf.dtok = tok
        for b in reads:
            b.r.append(tok)
        for b in writes:
            b.w = tok
            b.r = []
        return tok

    def barrier(self):
        toks = []
        for e in self.sem:
            assert not self.pending[e], e
            if self.cnt[e] > 0:
                toks.append(Tok(e + "_b", self.sem[e], self.cnt[e], e))
        for b in self.dbufs:
            if b.dtok is not None:
                toks.append(b.dtok)
        for e in self.engs:
            self._wait(e, [t for t in toks if t.key != e])


class Rot:
    def __init__(self, items):
        self.items = items
        self.i = 0

    def next(self):
        it = self.items[self.i % len(self.items)]
        self.i += 1
        return it


def build_nc(dbg=False):
    nc = bass.Bass("TRN2", target_bir_lowering=False)

    def I(n, s):
        return nc.dram_tensor(n, list(s), F32, kind="ExternalInput")

    kind_s = "ExternalOutput" if dbg else "Internal"

    def S(n, s, dt):
        return nc.dram_tensor(n, list(s), dt, kind=kind_s)

    xs = I("xs", [NT, D])
    w1i, w1o = I("w_ffn1_in", [D, 2 * DFF]), I("w_ffn1_out", [DFF, D])
    w2i, w2o = I("w_ffn2_in", [D, 2 * DFF]), I("w_ffn2_out", [DFF, D])
    w_in = I("w_in", [D, 8208])
    w_of, w_or, w_ou = I("w_o_fox", [D, D]), I("w_o_ret", [D, D]), I("w_out", [D, D])
    g1, gm, g2 = I("norm_ffn1", [D]), I("norm_mix", [D]), I("norm_ffn2", [D])
    b_forget, b_gate = I("b_forget", [16]), I("b_gate", [2048])
    qn, kn, gn = I("fox_q_norm", [64]), I("fox_k_norm", [64]), I("ret_gn", [D])
    ck_t, sk_t = I("ck_t", [128, NCH, 128]), I("sk_t", [128, NCH, 128])
    cq_t, sq_t = I("cq_t", [128, NCH, 128]), I("sq_t", [128, NCH, 128])
    vcol_d, nvcol_d, vkcol_d = I("vcol", [128, NCH]), I("nvcol", [128, NCH]), I("vkcol", [128, NCH])
    decayT_d = I("decayT", [128, 4, 128])
    zeta_d, xi_d, gam_d = I("zeta", [128, 4]), I("xi", [128, 4]), I("gam", [128, 4])
    tri_d = I("tri", [128, 128])
    out = nc.dram_tensor("out", [NOWN * 128, D], F32, kind="ExternalOutput")

    wb1i, wb2i = S("wb1i", [11, 128, 8, 2, 256], BF16), S("wb2i", [11, 128, 8, 2, 256], BF16)
    wb1o, wb2o = S("wb1o", [DFF, D], BF16), S("wb2o", [DFF, D], BF16)
    wbin = S("wbin", [18, 128, 8, 512], BF16)
    wbff = S("wbff", [128, 8, 16], BF16)
    wbof, wbor, wbou = S("wbof", [D, D], BF16), S("wbor", [D, D], BF16), S("wbou", [D, D], BF16)
    H1 = S("H1", [NOWN * 128, D], F32)
    KTa = S("KTa", [16, 68, NT], BF16)
    VAa = S("VAa", [NT, 16 * 65], BF16)
    QTa = S("QTa", [16, 68, NOWN * 128], BF16)
    KZ = S("KZ", [NT, 512], BF16)
    VB = S("VB", [NT, 1024], BF16)
    KRT = S("KRT", [4, 128, NT], BF16)
    QRT = S("QRT", [4, 128, NOWN * 128], BF16)
    QXT = S("QXT", [4, 128, NOWN * 128], BF16)
    SRG = S("SRG", [NOWN * 128, D], F32)
    GT = S("GT", [16, 128, NOWN * 128], F32)
    YB = S("YB", [NOWN * 128, D], BF16)

    with ExitStack() as es:
        def sb(n, s, d, stack=None):
            return (stack or es).enter_context(nc.sbuf_tensor(n, list(s), d))

        PS = es.enter_context(nc.psum_tensor("PS", [128, 8, 512], F32))
        PSB = PS.bitcast(BF16)
        block = es.enter_context(nc.Block())
        c = Ctx(nc, es)
        psb_ = [Buf("ps%d" % i) for i in range(8)]

        def A(e, fn, r=(), w=(), inc=True):
            return c.op(e, fn, r, w, inc)

        V, G, ACT, PE = nc.vector, nc.gpsimd, nc.scalar, nc.tensor

        def bcast(t, off, dims):
            return bass.AP(tensor=t, offset=off, ap=[list(d) for d in dims])

        ident = sb("ident", [128, 128], BF16)
        identf = sb("identf", [128, 128], F32)
        tri_f = sb("tri_f", [128, 128], F32)
        ones_f = sb("ones_f", [128, 128], F32)
        maskT = sb("maskT", [128, 128], BF16)
        g1T, gmT, g2T = sb("g1T", [128, 8], F32), sb("gmT", [128, 8], F32), sb("g2T", [128, 8], F32)
        bgT = sb("bgT", [128, 16], F32)
        vcol, nvcol, vkcol = sb("vcol_s", [128, NCH], F32), sb("nvcol_s", [128, NCH], F32), sb("vkcol_s", [128, NCH], F32)
        Bc = Buf("consts")
        YAb = [Buf("YA%d" % i) for i in range(NOWN)]

        @block.sync
        def _(sync):
            sp = "sp"
            Bw = {}
            late = []

            def emit_late(n=1):
                for _ in range(n):
                    if late:
                        late.pop(0)()

            def castbuf(n):
                Bw[n] = Buf(n)
                return Bw[n]

            def cast_ffn_in(src, dst, name):
                sv = src.ap().rearrange("(kc p) n -> p kc n", p=128)
                for part, (s0, s1) in enumerate([(0, 4), (4, 8), (8, 11)]):
                    b = castbuf("%s_%d" % (name, part))
                    for sl in range(s0, s1):
                        for ab in range(2):
                            late.append(lambda b=b, sl=sl, ab=ab: c.dma("pool", dst.ap()[sl, :, :, ab, :], sv[:, :, ab * DFF + sl * 256: ab * DFF + sl * 256 + 256],
                                                                    writes=[b], sembuf=b, group=True))

            def cast_rows(src, dst, name, nsplit):
                b = castbuf(name)
                rows = src.shape[0]
                step = rows // nsplit
                for k in range(nsplit):
                    r0 = k * step
                    r1 = rows if k == nsplit - 1 else r0 + step
                    late.append(lambda b=b, r0=r0, r1=r1: c.dma("pool", dst.ap()[r0:r1, :], src.ap()[r0:r1, :], writes=[b], sembuf=b, group=True))

            def cast_win():
                sv = w_in.ap().rearrange("(kc p) n -> p kc n", p=128)
                b = castbuf("wbff")
                c.dma("pool", wbff.ap(), sv[:, :, FF_OFF:FF_OFF + 16], writes=[b], sembuf=b, group=True)

            cast_win()
            cast_rows(w_of, wbof, "wof", 2)
            cast_rows(w_or, wbor, "wor", 2)
            cast_rows(w_ou, wbou, "wou", 2)
            cast_ffn_in(w2i, wb2i, "w2i")
            cast_rows(w2o, wb2o, "w2o", 4)

            c.dma(sp, tri_f[:], tri_d.ap(), writes=[Bc], sembuf=Bc, group=True)
            for t, d_ in [(g1T, g1), (gmT, gm), (g2T, g2)]:
                c.dma(sp, t[:], d_.ap().rearrange("(k p) -> p k", p=128), writes=[Bc], sembuf=Bc, group=True,
                      allow_slow_non_contiguous=True)
            c.dma(sp, bgT[:], b_gate.ap().rearrange("(k p) -> p k", p=128), writes=[Bc], sembuf=Bc, group=True,
                  allow_slow_non_contiguous=True)
            for t, d_ in [(vcol, vcol_d), (nvcol, nvcol_d), (vkcol, vkcol_d)]:
                c.dma(sp, t[:], d_.ap(), writes=[Bc], sembuf=Bc, group=True)
            A("pool", lambda: G.memset(identf[:], 0.0), w=[Bc])
            A("pool", lambda: G.affine_select(out=identf[:], in_=identf[:], pattern=[[-1, 128]], compare_op=ALU.not_equal,
                                              fill=1.0, base=0, channel_multiplier=1), r=[Bc], w=[Bc])
            A("pool", lambda: G.memset(ones_f[:], 1.0), w=[Bc])
            A("dve", lambda: V.tensor_copy(out=ident[:], in_=identf[:]), r=[Bc], w=[Bc])
            A("dve", lambda: V.tensor_copy(out=maskT[:], in_=tri_f[:]), r=[Bc], w=[Bc])

            psrot = Rot(list(range(8)))

            def ps():
                b = psrot.next()
                return b, psb_[b]

            def ps2():
                if psrot.i % 2 == 1:
                    psrot.i += 1
                b = psrot.next()
                psrot.next()
                return b, [psb_[b], psb_[b + 1]]

            def norm_tile(n, hx, hxb, gainT, xT, xTb, small, xn_rot, pre=None):
                for g0 in range(0, n, 4):
                    grp = list(range(g0, min(n, g0 + 4)))
                    items = []
                    for lc in grp:
                        if pre is not None:
                            pre(lc)
                        src, srcb = hx[:, lc, :], hxb[lc]
                        st, stb = small.next()
                        xn, xnb = xn_rot.next()
                        A("act", lambda: ACT.activation(out=xn[:], in_=src, func=AF.Square, accum_out=st[:, 0:1]), r=[srcb], w=[stb, xnb])
                        A("act", lambda: ACT.activation(out=st[:, 1:2], in_=st[:, 0:1], func=AF.Sqrt, scale=1.0 / D, bias=EPS), r=[stb], w=[stb])
                        items.append((lc, src, srcb, st, stb, xn, xnb))
                    for n_, (lc, src, srcb, st, stb, xn, xnb) in enumerate(items):
                        A("dve", lambda: V.reciprocal(out=st[:, 2:3], in_=st[:, 1:2]), r=[stb], w=[stb])
                        if n_ % 2 == 0:
                            A("dve", lambda: V.tensor_scalar(out=xn[:], in0=src, scalar1=st[:, 2:3], scalar2=None, op0=ALU.mult), r=[srcb, stb], w=[xnb])
                        else:
                            A("act", lambda: ACT.activation(out=xn[:], in_=src, func=AF.Copy, scale=st[:, 2:3]), r=[srcb, stb], w=[xnb])
                    for (lc, src, srcb, st, stb, xn, xnb) in items:
                        b, bb = ps()
                        for k in range(8):
                            A("pe", lambda: PE.transpose(out=PSB[:, b, k * 128:(k + 1) * 128], in_=xn[:, k * 128:(k + 1) * 128], identity=ident[:]),
                              r=[xnb, Bc], w=[bb], inc=(k == 7))
                        A("dve", lambda: V.tensor_tensor(out=xT[:, :, lc * 128:(lc + 1) * 128], in0=PSB[:, b, :].rearrange("p (k t) -> p k t", k=8),
                                                         in1=bcast(gainT, 0, [[8, 128], [1, 8], [0, 128]]), op=ALU.mult), r=[bb, Bc], w=[xTb[lc]])

            class Slabs:
                def __init__(self, slots, jobs, stage=None):
                    self.slots, self.jobs, self.stage = slots, jobs, stage
                    self.loadptr, self.convptr, self.nload_f32, self.nconv_f32 = 0, 0, 0, 0
                    self.stage_of = {}

                def _load(self, j):
                    t, b = self.slots[j % len(self.slots)]
                    job = self.jobs[j]
                    if job[0] == "bf16":
                        c.dma(sp, t[:], job[1], reads=job[2], writes=[b], sembuf=b)
                    elif job[0] == "f32":
                        st, stb = self.stage.next()
                        self.stage_of[j] = (st, stb)
                        for k_, (vf, src) in enumerate(job[1]):
                            c.dma(sp, vf(st), src, writes=[stb], sembuf=stb, group=(k_ > 0))
                        self.nload_f32 += 1

                def _conv(self, j):
                    t, b = self.slots[j % len(self.slots)]
                    job = self.jobs[j]
                    if job[0] == "f32":
                        st, stb = self.stage_of.pop(j)
                        A("act", lambda: ACT.copy(out=t[:, 0:2048], in_=st[:, 0:2048]), r=[stb], w=[b])
                        A("dve", lambda: V.tensor_copy(out=t[:, 2048:4096], in_=st[:, 2048:4096]), r=[stb], w=[b])
                        c.dma("pool", job[2], t[:], reads=[b], writes=[job[3]], sembuf=b)
                        self.nconv_f32 += 1
                    elif job[0] == "swap":
                        ts_, bs_ = self.slots[(j - 1) % len(self.slots)]
                        ov = t[:].rearrange("p (k h t d) -> p k h t d", k=8, h=4, t=2)
                        iv = ts_[:].rearrange("p (k h t d) -> p k h t d", k=8, h=4, t=2)
                        A("act", lambda: ACT.copy(out=ov[:, :, :, 0, :], in_=iv[:, :, :, 1, :]), r=[bs_], w=[b])
                        A("dve", lambda: V.tensor_copy(out=ov[:, :, :, 1, :], in_=iv[:, :, :, 0, :]), r=[bs_], w=[b])

                def get(self, j, oldest=None):
                    oldest = j if oldest is None else oldest
                    n = len(self.slots)
                    while True:
                        progressed = False
                        while self.convptr < self.loadptr and self.convptr <= j + 1 and self.convptr < oldest + n:
                            self._conv(self.convptr)
                            self.convptr += 1
                            progressed = True
                        while self.loadptr < min(len(self.jobs), oldest + n):
                            if self.jobs[self.loadptr][0] == "f32" and self.nload_f32 - self.nconv_f32 >= len(self.stage.items):
                                break
                            self._load(self.loadptr)
                            self.loadptr += 1
                            progressed = True
                        if not progressed:
                            break
                    assert self.convptr > j, (self.convptr, self.loadptr, j)
                    return self.slots[j % n]

            def ffn(tile_n, hx, hxb, xT, xTb, gT, gTb, w2h, w2hb, slabs, j0, wbo, wbob, sil_rot, last=None, first_src=None, stage=None):
                ntok = tile_n * 128
                cgs = [(o, min(512, ntok - o)) for o in range(0, ntok, 512)]
                for half, (sl0, sl1) in enumerate(PARTS):
                    nj = (sl1 - sl0) * 2
                    jbase = sl0 * 2
                    if first_src is None:
                        c.dma(sp, w2h[:, 0:nj, :], wbo.ap()[jbase * 128:(jbase + nj) * 128, :].rearrange("(j p) n -> p j n", p=128),
                              reads=[wbob[half]], writes=[w2hb], sembuf=w2hb)
                    else:
                        for q0 in range(nj):
                            st, stb = stage.next()
                            r0 = (jbase + q0) * 128
                            c.dma(sp, st[:], first_src.ap()[r0:r0 + 128, :], writes=[stb], sembuf=stb)
                            A("act" if q0 % 2 == 0 else "dve", lambda: (ACT.copy if q0 % 2 == 0 else V.tensor_copy)(out=w2h[:, q0, :], in_=st[:]),
                              r=[stb], w=[w2hb])
                        c.dma("pool", wbo.ap()[jbase * 128:(jbase + nj) * 128, :].rearrange("(j p) n -> p j n", p=128), w2h[:, 0:nj, :],
                              reads=[w2hb], writes=[wbob[half]], sembuf=w2hb)
                    for sl in range(sl0, sl1):
                        wt, wtb = slabs.get(j0 + sl)
                        wv = wt[:].rearrange("p (k a c) -> p k a c", k=8, a=2)
                        for jj in range(2):
                            jl = (sl - sl0) * 2 + jj
                            for (o, n) in cgs:
                                lcs = list(range(o // 128, (o + n) // 128))
                                pa, pab = ps()
                                pb, pbb = ps()
                                for k in range(8):
                                    A("pe", lambda: PE.matmul(PS[:, pa, 0:n], lhsT=wv[:, k, 0, jj * 128:(jj + 1) * 128], rhs=xT[:, k, o:o + n],
                                                              start=(k == 0), stop=(k == 7)), r=[wtb] + [xTb[l] for l in lcs], w=[pab], inc=(k == 7))
                                for k in range(8):
                                    A("pe", lambda: PE.matmul(PS[:, pb, 0:n], lhsT=wv[:, k, 1, jj * 128:(jj + 1) * 128], rhs=xT[:, k, o:o + n],
                                                              start=(k == 0), stop=(k == 7)), r=[wtb] + [xTb[l] for l in lcs], w=[pbb], inc=(k == 7))
                                sa, sab = sil_rot.next()
                                A("act", lambda: ACT.activation(out=sa[:, 0:n], in_=PS[:, pa, 0:n], func=AF.Silu), r=[pab], w=[sab])
                                A("dve", lambda: V.tensor_tensor(out=gT[:, jl, o:o + n], in0=sa[:, 0:n], in1=PS[:, pb, 0:n], op=ALU.mult),
                                  r=[sab, pbb], w=[gTb[o // 512]])
                    for lc in range(tile_n):
                        for fh in range(2):
                            po, pob = ps()
                            for jl in range(nj):
                                A("pe", lambda: PE.matmul(PS[:, po, :], lhsT=gT[:, jl, lc * 128:(lc + 1) * 128], rhs=w2h[:, jl, fh * 512:(fh + 1) * 512],
                                                          start=(jl == 0), stop=(jl == nj - 1)), r=[gTb[lc // 4], w2hb], w=[pob], inc=(jl == nj - 1))
                            A("dve", lambda: V.scalar_tensor_tensor(out=hx[:, lc, fh * 512:(fh + 1) * 512], in0=PS[:, po, :], scalar=0.5,
                                                                    in1=hx[:, lc, fh * 512:(fh + 1) * 512], op0=ALU.mult, op1=ALU.add),
                              r=[pob, hxb[lc]], w=[hxb[lc]])
                        if half == len(PARTS) - 1 and last is not None:
                            last(lc)

            with ExitStack() as ea:
                hx = sb("hx", [128, 7, D], F32, ea)
                xT = sb("xT", [128, 8, TW], BF16, ea)
                gT = sb("gT", [128, 6, TW], BF16, ea)
                w2h = sb("w2h", [128, 6, D], BF16, ea)
                slots = [(sb("wsl%d" % i, [128, 4096], BF16, ea), Buf("wsl%d" % i)) for i in range(4)]
                small = Rot([(sb("sm%d" % i, [128, 64], F32, ea), Buf("sm%d" % i)) for i in range(8)])
                xn_rot = Rot([(sb("xn%d" % i, [128, D], BF16, ea), Buf("xn%d" % i)) for i in range(4)])
                sil_rot = Rot([(sb("sil%d" % i, [128, 512], F32, ea), Buf("sil%d" % i)) for i in range(3)])
                tf_rot = Rot([(sb("tf%d" % i, [128, 512], F32, ea), Buf("tf%d" % i)) for i in range(4)])
                tb_rot = Rot([(sb("tb%d" % i, [128, 1024], BF16, ea), Buf("tb%d" % i)) for i in range(6)])
                kA_rot = Rot([(sb("kA%d" % i, [128, 16, 68], BF16, ea), Buf("kA%d" % i)) for i in range(2)])
                qA_rot = Rot([(sb("qA%d" % i, [128, 16, 68], BF16, ea), Buf("qA%d" % i)) for i in range(2)])
                vA_rot = Rot([(sb("vA%d" % i, [128, 16, 65], BF16, ea), Buf("vA%d" % i)) for i in range(2)])
                st_rot = Rot([(sb("stg%d" % i, [128, 16, 128], BF16, ea), Buf("stg%d" % i)) for i in range(2)])
                cs_rot = Rot([(sb("cs%d" % i, [128, 2, 128], F32, ea), Buf("cs%d" % i)) for i in range(3)])
                wff = sb("wff", [128, 8, 16], BF16, ea)
                bfb = sb("bfb", [128, 16], F32, ea)
                qgc = sb("qgc", [128, 1], F32, ea)
                kgc = sb("kgc", [128, 1], F32, ea)
                tf2_rot = Rot([(sb("tf2_%d" % i, [128, 1024], F32, ea), Buf("tf2_%d" % i)) for i in range(2)])
                cc_all = sb("cc_all", [128, NCH + 1, 16], F32, ea)
                cref_all = sb("cref_all", [128, NCH + 1, 16], F32, ea)
                zeta, xi = sb("zeta_s", [128, 4], F32, ea), sb("xi_s", [128, 4], F32, ea)
                hxb = [Buf("hx%d" % i) for i in range(9)]
                xTb = [Buf("xT%d" % i) for i in range(9)]
                gTb = [Buf("gT%d" % i) for i in range(3)]
                w2hb = Buf("w2h")
                Ba = Buf("constsA")
                ccb = Buf("cc")

                c.dma(sp, wff[:], wbff.ap(), reads=[Bw["wbff"]], writes=[Ba], sembuf=Ba, group=True)
                c.dma(sp, bfb[:], bcast(b_forget, 0, [[0, 128], [1, 16]]), writes=[Ba], sembuf=Ba, group=True)
                A("pool", lambda: G.memset(qgc[:], 1.0), w=[Ba])
                A("pool", lambda: G.memset(kgc[:], 1.0), w=[Ba])
                c.dma(sp, qgc[0:64, :], qn.ap().rearrange("(d o) -> d o", o=1), writes=[Ba], sembuf=Ba)
                c.dma(sp, kgc[0:64, :], kn.ap().rearrange("(d o) -> d o", o=1), writes=[Ba], sembuf=Ba)
                c.dma(sp, zeta[:], zeta_d.ap(), writes=[Ba], sembuf=Ba, group=True)
                c.dma(sp, xi[:], xi_d.ap(), writes=[Ba], sembuf=Ba, group=True)
                A("act", lambda: ACT.mul(qgc[0:64, :], qgc[0:64, :], 0.125), r=[Ba], w=[Ba])
                A("pool", lambda: G.memset(cc_all[:, 0, :], 0.0), w=[ccb])
                A("pool", lambda: G.memset(cref_all[:, 0, :], 0.0), r=[ccb], w=[ccb])
                for (t, _) in kA_rot.items:
                    A("pool", lambda: G.memset(t[:, :, 67:68], 1.0), w=[_])
                for (t, _) in qA_rot.items:
                    A("pool", lambda: G.memset(t[:, :, 64:67], 1.0), w=[_])

                jobs = []
                tile_j0 = []

                def v_ffn(t):
                    return t[:].rearrange("p (k a c) -> p k a c", k=8, a=2)

                def v_win(t):
                    return t[:].rearrange("p (k c) -> p k c", k=8)

                Bw1i = [Buf("w1i_%d" % sl) for sl in range(11)]
                Bwin = {n: Buf("win_" + n) for n, _ in WIN_SLABS}
                Bw1o = [Buf("w1o_%d" % k) for k in range(len(PARTS))]
                stage = Rot([(sb("stage%d" % i, [128, 4096], F32, ea), Buf("stage%d" % i)) for i in range(2)])
                w1i_v = w1i.ap().rearrange("(kc p) n -> p kc n", p=128)
                win_v = w_in.ap().rearrange("(kc p) n -> p kc n", p=128)
                for ti, tile in enumerate(TILES):
                    tile_j0.append(len(jobs))
                    for sl in range(11):
                        dst = wb1i.ap()[sl].rearrange("p k a c -> p (k a c)")
                        if ti == 0:
                            parts = [((lambda st, ab=ab: st[:].rearrange("p (k a c) -> p k a c", k=8, a=2)[:, :, ab, :]),
                                      w1i_v[:, :, ab * DFF + sl * 256: ab * DFF + sl * 256 + 256]) for ab in range(2)]
                            jobs.append(("f32", parts, dst, Bw1i[sl]))
                        else:
                            jobs.append(("bf16", dst, [Bw1i[sl]]))
                    for n, off in WIN_SLABS:
                        dst = wbin.ap()[WIN_IDX[n]].rearrange("p k c -> p (k c)")
                        if off < 0:
                            jobs.append(("swap",))
                        elif ti == 0:
                            parts = [((lambda st: st[:].rearrange("p (k c) -> p k c", k=8)), win_v[:, :, off:off + 512])]
                            jobs.append(("f32", parts, dst, Bwin[n]))
                        else:
                            jobs.append(("bf16", dst, [Bwin[n]]))
                slabs = Slabs(slots, jobs, stage)

                def load_x(ti_):
                    for lc_, i_ in enumerate(TILES[ti_]):
                        c.dma(sp, hx[:, lc_, :], xs.ap()[i_ * 128:(i_ + 1) * 128, :], writes=[hxb[lc_]], sembuf=hxb[lc_])

                for ti, tile in enumerate(TILES):
                    tn = len(tile)
                    ntok = tn * 128
                    j0 = tile_j0[ti]
                    own_lc = [lc for lc in range(tn) if is_own(tile[lc])]
                    if ti == 0:
                        load_x(0)
                    norm_tile(tn, hx, hxb, g1T, xT, xTb, small, xn_rot)
                    ffn(tn, hx, hxb, xT, xTb, gT, gTb, w2h, w2hb, slabs, j0, wb1o, Bw1o, sil_rot,
                        first_src=(w1o if ti == 0 else None), stage=tf2_rot)
                    def store_h1(lc):
                        i = tile[lc]
                        if is_own(i):
                            s = i // 2 - 1
                            c.dma("pool", H1.ap()[s * 128:(s + 1) * 128, :], hx[:, lc, :], reads=[hxb[lc]], sembuf=hxb[lc])

                    norm_tile(tn, hx, hxb, gmT, xT, xTb, small, xn_rot, pre=store_h1)
                    if ti + 1 < len(TILES):
                        load_x(ti + 1)
                    jw = j0 + 11
                    J = lambda n_: jw + WIN_IDX[n_]

                    def proj(lc, wt, wtb, ncols=512):
                        b, bb = ps()
                        wv = wt[:].rearrange("p (k c) -> p k c", k=8) if ncols == 512 else wt
                        for k in range(8):
                            A("pe", lambda: PE.matmul(PS[:, b, 0:ncols], lhsT=xT[:, k, lc * 128:(lc + 1) * 128], rhs=wv[:, k, 0:ncols],
                                                      start=(k == 0), stop=(k == 7)), r=[wtb, xTb[lc]], w=[bb], inc=(k == 7))
                        return b, bb

                    def proj2(lc, wa, wab, wb_, wbb):
                        b, bbs = ps2()
                        for g_, (wt, wtb) in enumerate([(wa, wab), (wb_, wbb)]):
                            wv = wt[:].rearrange("p (k c) -> p k c", k=8)
                            for k in range(8):
                                A("pe", lambda: PE.matmul(PS[:, b + g_, :], lhsT=xT[:, k, lc * 128:(lc + 1) * 128], rhs=wv[:, k, :],
                                                          start=(k == 0), stop=(k == 7)), r=[wtb, xTb[lc]], w=[bbs[g_]], inc=(k == 7))
                        return b, bbs

                    for lc in range(tn):
                        i = tile[lc]
                        b, bb = proj(lc, wff, Ba, 16)
                        st, stb = small.next()
                        A("dve", lambda: V.tensor_tensor(out=st[:, 0:16], in0=PS[:, b, 0:16], in1=bfb[:], op=ALU.add), r=[bb, Ba], w=[stb])
                        A("act", lambda: ACT.activation(out=st[:, 16:32], in_=st[:, 0:16], func=AF.Exp, scale=-1.0), r=[stb], w=[stb])
                        A("act", lambda: ACT.activation(out=st[:, 32:48], in_=st[:, 16:32], func=AF.Ln, bias=1.0), r=[stb], w=[stb])
                        A("dve", lambda: V.tensor_scalar(out=st[:, 48:64], in0=st[:, 32:48], scalar1=nvcol[:, i:i + 1], scalar2=None, op0=ALU.mult),
                          r=[stb, Bc], w=[stb])
                        b2, bb2 = ps()
                        A("pe", lambda: PE.matmul(PS[:, b2, 0:16], lhsT=tri_f[:], rhs=st[:, 48:64], start=True, stop=True), r=[Bc, stb], w=[bb2], inc=False)
                        A("pe", lambda: PE.matmul(PS[:, b2, 16:32], lhsT=ones_f[:], rhs=st[:, 48:64], start=True, stop=True), r=[Bc, stb], w=[bb2])
                        A("dve", lambda: V.tensor_tensor(out=cc_all[:, i + 1, :], in0=PS[:, b2, 0:16], in1=cref_all[:, i, :], op=ALU.add), r=[bb2, ccb], w=[ccb])
                        A("dve", lambda: V.tensor_tensor(out=cref_all[:, i + 1, :], in0=PS[:, b2, 16:32], in1=cref_all[:, i, :], op=ALU.add), r=[bb2, ccb], w=[ccb])

                    def headnorm(b, bbs, dstA, dstAb):
                        tf, tfb = tf2_rot.next()
                        st, stb = small.next()
                        A("act", lambda: ACT.activation(out=tf[:].rearrange("p (g c) -> p g c", g=2), in_=PS[:, b:b + 2, :], func=AF.Square), r=bbs, w=[tfb])
                        A("dve", lambda: V.tensor_reduce(out=st[:, 0:16], in_=tf[:].rearrange("p (h d) -> p h d", h=16), axis=AX.X, op=ALU.add), r=[tfb], w=[stb])
                        A("act", lambda: ACT.activation(out=st[:, 16:32], in_=st[:, 0:16], func=AF.Sqrt, scale=1.0 / 64, bias=EPS), r=[stb], w=[stb])
                        A("dve", lambda: V.reciprocal(out=st[:, 32:48], in_=st[:, 16:32]), r=[stb], w=[stb])
                        A("dve", lambda: V.tensor_tensor(out=dstA[:, :, 0:64], in0=PS[:, b:b + 2, :].rearrange("p g (h d) -> p (g h) d", h=8),
                                                         in1=bcast(st, 32, [[64, 128], [1, 16], [0, 64]]), op=ALU.mult), r=bbs + [stb], w=[dstAb])

                    def transpose16(srcA, srcAb, width, dst_dram_fn, gcol=None):
                        stg, stgb = st_rot.next()
                        for half in range(2):
                            b, bb = ps()
                            for hh in range(8):
                                h = half * 8 + hh
                                A("pe", lambda: PE.transpose(out=PSB[0:width, b, hh * 128:(hh + 1) * 128], in_=srcA[:, h, 0:width], identity=ident[:]),
                                  r=[srcAb, Bc], w=[bb], inc=(hh == 7))
                            if gcol is None:
                                A("act", lambda: ACT.copy(out=stg[0:width, half * 8:half * 8 + 8, :], in_=PSB[0:width, b, :].rearrange("p (h t) -> p h t", h=8)),
                                  r=[bb], w=[stgb])
                            else:
                                A("act", lambda: ACT.activation(out=stg[0:width, half * 8:half * 8 + 8, :], in_=PSB[0:width, b, :].rearrange("p (h t) -> p h t", h=8),
                                                                func=AF.Copy, scale=gcol[0:width, 0:1]), r=[bb, Ba], w=[stgb])
                        c.dma("pool", dst_dram_fn(), stg[0:width, :, :], reads=[stgb], sembuf=stgb)

                    w0, w0b = slabs.get(J("fk0"))
                    w1, w1b = slabs.get(J("fk1"), J("fk0"))
                    pend = None

                    def k_stage1(lc):
                        i = tile[lc]
                        b, bbs = proj2(lc, w0, w0b, w1, w1b)
                        kA, kAb = kA_rot.next()
                        headnorm(b, bbs, kA, kAb)
                        cs, csb = cs_rot.next()
                        st, stb = small.next()
                        hb = st[:].bitcast(BF16)
                        cc = cc_all[:, i + 1, :]
                        A("dve", lambda: V.tensor_copy(out=hb[:, 0:16], in_=cc), r=[ccb], w=[stb])
                        A("dve", lambda: V.tensor_tensor(out=cs[:, 0, 0:16], in0=cc, in1=hb[:, 0:16], op=ALU.subtract), r=[ccb, stb], w=[csb])
                        A("dve", lambda: V.tensor_copy(out=hb[:, 16:32], in_=cs[:, 0, 0:16]), r=[csb], w=[stb])
                        A("dve", lambda: V.tensor_tensor(out=cs[:, 1, 0:16], in0=cs[:, 0, 0:16], in1=hb[:, 16:32], op=ALU.subtract), r=[csb, stb], w=[csb])
                        A("dve", lambda: V.tensor_scalar(out=kA[:, :, 64:65], in0=hb[:, 0:16].rearrange("p (h o) -> p h o", o=1), scalar1=-1.0, scalar2=None, op0=ALU.mult),
                          r=[stb], w=[kAb])
                        A("dve", lambda: V.tensor_scalar(out=kA[:, :, 65:66], in0=hb[:, 16:32].rearrange("p (h o) -> p h o", o=1), scalar1=-1.0, scalar2=None, op0=ALU.mult),
                          r=[stb], w=[kAb])
                        A("dve", lambda: V.tensor_scalar(out=kA[:, :, 66:67], in0=cs[:, 1, 0:16].rearrange("p (h o) -> p h o", o=1), scalar1=-1.0, scalar2=None, op0=ALU.mult),
                          r=[csb], w=[kAb])
                        return (kA, kAb, i)

                    def k_stage2(p_):
                        kA, kAb, i = p_
                        transpose16(kA, kAb, 68, lambda: KTa.ap()[:, :, i * 128:(i + 1) * 128].rearrange("h r t -> r h t"), gcol=kgc)

                    pend = k_stage1(0)
                    for lc in range(tn):
                        nxt = k_stage1(lc + 1) if lc + 1 < tn else None
                        k_stage2(pend)
                        pend = nxt
                        if ti >= 1:
                            emit_late(1)

                    w0, w0b = slabs.get(J("fv0"))
                    w1, w1b = slabs.get(J("fv1"), J("fv0"))
                    for lc in range(tn):
                        i = tile[lc]
                        vA, vAb = vA_rot.next()
                        for g_, (wt, wtb) in enumerate([(w0, w0b), (w1, w1b)]):
                            b, bb = proj(lc, wt, wtb)
                            A("act", lambda: ACT.activation(out=vA[:, 8 * g_:8 * g_ + 8, 0:64], in_=PS[:, b, :].rearrange("p (h d) -> p h d", h=8),
                                                            func=AF.Copy, scale=vcol[:, i:i + 1]), r=[bb, Bc], w=[vAb])
                        A("dve", lambda: V.tensor_copy(out=vA[:, :, 64:65], in_=bcast(vcol, i, [[NCH, 128], [0, 16], [1, 1]])), r=[Bc], w=[vAb])
                        c.dma("pool", VAa.ap()[i * 128:(i + 1) * 128, :], vA[:].rearrange("p h d -> p (h d)"), reads=[vAb], sembuf=vAb)
                        if ti >= 1:
                            emit_late(1)

                    w0, w0b = slabs.get(J("fq0"))
                    w1, w1b = slabs.get(J("fq1"), J("fq0"))

                    def q_stage1(lc):
                        i = tile[lc]
                        b, bbs = proj2(lc, w0, w0b, w1, w1b)
                        qA, qAb = qA_rot.next()
                        headnorm(b, bbs, qA, qAb)
                        A("dve", lambda: V.tensor_copy(out=qA[:, :, 67:68], in_=cref_all[:, i + 1, :].rearrange("p (h o) -> p h o", o=1)), r=[ccb], w=[qAb])
                        return (qA, qAb, i)

                    def q_stage2(p_):
                        qA, qAb, i = p_
                        s = i // 2 - 1
                        transpose16(qA, qAb, 68, lambda: QTa.ap()[:, :, s * 128:(s + 1) * 128].rearrange("h r t -> r h t"), gcol=qgc)

                    if own_lc:
                        pend = q_stage1(own_lc[0])
                        for n_, lc in enumerate(own_lc):
                            nxt = q_stage1(own_lc[n_ + 1]) if n_ + 1 < len(own_lc) else None
                            q_stage2(pend)
                            pend = nxt

                    def rotary(b, bbs, i, ct, st_):
                        cs, csb = cs_rot.next()
                        c.dma(sp, cs[:, 0, :], ct.ap()[:, i, :], writes=[csb], sembuf=csb, group=True)
                        c.dma(sp, cs[:, 1, :], st_.ap()[:, i, :], writes=[csb], sembuf=csb, group=True)
                        t1, t1b = tf_rot.next()
                        t2, t2b = tf_rot.next()
                        A("dve", lambda: V.tensor_tensor(out=t1[:].rearrange("p (h d) -> p h d", h=4), in0=PS[:, b, :].rearrange("p (h d) -> p h d", h=4),
                                                         in1=bcast(cs, 0, [[256, 128], [0, 4], [1, 128]]), op=ALU.mult), r=[bbs[0], csb], w=[t1b])
                        A("dve", lambda: V.tensor_tensor(out=t2[:].rearrange("p (h d) -> p h d", h=4), in0=PS[:, b + 1, :].rearrange("p (h d) -> p h d", h=4),
                                                         in1=bcast(cs, 128, [[256, 128], [0, 4], [1, 128]]), op=ALU.mult), r=[bbs[1], csb], w=[t2b])
                        o, ob = tb_rot.next()
                        A("dve", lambda: V.tensor_tensor(out=o[:, 0:512], in0=t1[:], in1=t2[:], op=ALU.add), r=[t1b, t2b], w=[ob])
                        return o, ob

                    def transpose4(src, srcb, dst_fn):
                        stg, stgb = st_rot.next()
                        b, bb = ps()
                        for h in range(4):
                            A("pe", lambda: PE.transpose(out=PSB[:, b, h * 128:(h + 1) * 128], in_=src[:, h * 128:(h + 1) * 128], identity=ident[:]),
                              r=[srcb, Bc], w=[bb], inc=(h == 3))
                        A("act", lambda: ACT.copy(out=stg[:, 0:4, :], in_=PSB[:, b, 0:512].rearrange("p (h t) -> p h t", h=4)), r=[bb], w=[stgb])
                        c.dma("pool", dst_fn(), stg[:, 0:4, :], reads=[stgb], sembuf=stgb)

                    w0, w0b = slabs.get(J("rk"))
                    w1, w1b = slabs.get(J("rks"), J("rk"))

                    def rk_stage1(lc):
                        i = tile[lc]
                        b, bbs = proj2(lc, w0, w0b, w1, w1b)
                        kr, krb = rotary(b, bbs, i, ck_t, sk_t)
                        kz, kzb = tb_rot.next()
                        A("dve", lambda: V.tensor_tensor(out=kz[:, 0:512].rearrange("p (h d) -> p h d", h=4), in0=kr[:, 0:512].rearrange("p (h d) -> p h d", h=4),
                                                         in1=bcast(zeta, 0, [[4, 128], [1, 4], [0, 128]]), op=ALU.mult), r=[krb, Ba], w=[kzb])
                        c.dma("pool", KZ.ap()[i * 128:(i + 1) * 128, :], kz[:, 0:512], reads=[kzb], sembuf=kzb)
                        return (kr, krb, i)

                    pend = rk_stage1(0)
                    for lc in range(tn):
                        nxt = rk_stage1(lc + 1) if lc + 1 < tn else None
                        kr, krb, i = pend
                        transpose4(kr, krb, lambda: KRT.ap()[:, :, i * 128:(i + 1) * 128].rearrange("h d t -> d h t"))
                        pend = nxt

                    w0, w0b = slabs.get(J("rv0"))
                    w1, w1b = slabs.get(J("rv1"), J("rv0"))
                    for lc in range(tn):
                        i = tile[lc]
                        vb, vbb = tb_rot.next()
                        for g_, (wt, wtb) in enumerate([(w0, w0b), (w1, w1b)]):
                            b, bb = proj(lc, wt, wtb)
                            A("act", lambda: ACT.activation(out=vb[:, g_ * 512:(g_ + 1) * 512], in_=PS[:, b, :], func=AF.Copy, scale=vcol[:, i:i + 1]),
                              r=[bb, Bc], w=[vbb])
                        c.dma("pool", VB.ap()[i * 128:(i + 1) * 128, :], vb[:], reads=[vbb], sembuf=vbb)

                    w0, w0b = slabs.get(J("rq"))
                    w1, w1b = slabs.get(J("rqs"), J("rq"))

                    def rq_stage1(lc):
                        i = tile[lc]
                        b, bbs = proj2(lc, w0, w0b, w1, w1b)
                        qr, qrb = rotary(b, bbs, i, cq_t, sq_t)
                        qx, qxb = tb_rot.next()
                        A("dve", lambda: V.tensor_tensor(out=qx[:, 0:512].rearrange("p (h d) -> p h d", h=4), in0=qr[:, 0:512].rearrange("p (h d) -> p h d", h=4),
                                                         in1=bcast(xi, 0, [[4, 128], [1, 4], [0, 128]]), op=ALU.mult), r=[qrb, Ba], w=[qxb])
                        return (qr, qrb, qx, qxb, i)

                    if own_lc:
                        pend = rq_stage1(own_lc[0])
                        for n_, lc in enumerate(own_lc):
                            nxt = rq_stage1(own_lc[n_ + 1]) if n_ + 1 < len(own_lc) else None
                            qr, qrb, qx, qxb, i = pend
                            s = i // 2 - 1
                            transpose4(qr, qrb, lambda: QRT.ap()[:, :, s * 128:(s + 1) * 128].rearrange("h d t -> d h t"))
                            transpose4(qx, qxb, lambda: QXT.ap()[:, :, s * 128:(s + 1) * 128].rearrange("h d t -> d h t"))
                            pend = nxt

                    w0, w0b = slabs.get(J("rg0"))
                    w1, w1b = slabs.get(J("rg1"), J("rg0"))
                    for lc in own_lc:
                        i = tile[lc]
                        s = i // 2 - 1
                        for g_, (wt, wtb) in enumerate([(w0, w0b), (w1, w1b)]):
                            b, bb = proj(lc, wt, wtb)
                            tf, tfb = tf_rot.next()
                            A("act", lambda: ACT.activation(out=tf[:], in_=PS[:, b, :], func=AF.Silu), r=[bb], w=[tfb])
                            c.dma("pool", SRG.ap()[s * 128:(s + 1) * 128, g_ * 512:(g_ + 1) * 512], tf[:], reads=[tfb], sembuf=tfb)

                    if own_lc:
                        ogs = [own_lc[k:k + 4] for k in range(0, len(own_lc), 4)]
                        for gi in range(4):
                            wt, wtb = slabs.get(J("ga0") + gi)
                            wv = wt[:].rearrange("p (k c) -> p k c", k=8)
                            for fb in range(4):
                                f16 = gi * 4 + fb
                                for og in ogs:
                                    n = len(og) * 128
                                    s0 = tile[og[0]] // 2 - 1
                                    b, bb = ps()
                                    for k in range(8):
                                        rhs = bcast(xT, k * TW + og[0] * 128, [[8 * TW, 128], [256, len(og)], [1, 128]])
                                        A("pe", lambda: PE.matmul(PS[:, b, 0:n], lhsT=wv[:, k, fb * 128:(fb + 1) * 128], rhs=rhs, start=(k == 0), stop=(k == 7)),
                                          r=[wtb] + [xTb[l] for l in og], w=[bb], inc=(k == 7))
                                    tf, tfb = tf_rot.next()
                                    A("act", lambda: ACT.activation(out=tf[:, 0:n], in_=PS[:, b, 0:n], func=AF.Sigmoid, bias=bgT[:, f16:f16 + 1]), r=[bb, Bc], w=[tfb])
                                    c.dma("pool", GT.ap()[f16, :, s0 * 128:s0 * 128 + n], tf[:, 0:n], reads=[tfb], sembuf=tfb)
                emit_late(len(late))
                c.barrier()

            ya_cm = nc.sbuf_tensor("YA", [128, NOWN, D], BF16)
            YA = ya_cm.__enter__()
            eb1 = ExitStack()
            eb1.__enter__()
            if True:
                eb = eb1
                R = sb("R", [128, 4, 256], F32, eb)
                Rb = sb("Rb", [128, 4, 256], BF16, eb)
                decT = sb("decT", [128, 4, 128], F32, eb)
                gam = sb("gam_s", [128, 4], F32, eb)
                gnb = sb("gnb", [128, D], F32, eb)
                Rbuf, Rbb, Bb = Buf("R"), Buf("Rb"), Buf("constsB")
                kz_rot = Rot([(sb("kz%d" % i, [128, 512], BF16, eb), Buf("kz%d" % i)) for i in range(3)])
                vb_rot = Rot([(sb("vb%d" % i, [128, 1024], BF16, eb), Buf("vb%d" % i)) for i in range(3)])
                kt_rot = Rot([(sb("kt%d" % i, [128, 3, 4, 128], BF16, eb), Buf("kt%d" % i)) for i in range(2)])
                sg_rot = Rot([(sb("sg%d" % i, [128, D], F32, eb), Buf("sg%d" % i)) for i in range(2)])
                s4_rot = Rot([(sb("sT%d" % i, [128, 512], BF16, eb), Buf("sT%d" % i)) for i in range(2)])
                o_rot = Rot([(sb("o%d" % i, [128, D], F32, eb), Buf("o%d" % i)) for i in range(2)])
                yb_rot = Rot([(sb("yb%d" % i, [128, D], BF16, eb), Buf("yb%d" % i)) for i in range(2)])
                small1 = Rot([(sb("smb%d" % i, [128, 64], F32, eb), Buf("smb%d" % i)) for i in range(4)])
                c.dma(sp, decT[:], decayT_d.ap(), writes=[Bb], sembuf=Bb, group=True)
                c.dma(sp, gam[:], gam_d.ap(), writes=[Bb], sembuf=Bb, group=True)
                c.dma(sp, gnb[:], bcast(gn, 0, [[0, 128], [1, D]]), writes=[Bb], sembuf=Bb, group=True)
                A("pool", lambda: G.memset(R[:], 0.0), w=[Rbuf])
                A("pool", lambda: G.memset(Rb[:], 0.0), w=[Rbb])
                BO, BS = 7, 7

                def loadB(i):
                    kz, kzb = kz_rot.next()
                    vb, vbb = vb_rot.next()
                    c.dma(sp, kz[:], KZ.ap()[i * 128:(i + 1) * 128, :], writes=[kzb], sembuf=kzb)
                    c.dma(sp, vb[:], VB.ap()[i * 128:(i + 1) * 128, :], writes=[vbb], sembuf=vbb)
                    o = None
                    if is_own(i):
                        s = i // 2 - 1
                        kt, ktb = kt_rot.next()
                        sg, sgb = sg_rot.next()
                        c.dma(sp, kt[:, 0, :, :], KRT.ap()[:, :, i * 128:(i + 1) * 128].rearrange("h d t -> d h t"), writes=[ktb], sembuf=ktb, group=True)
                        c.dma(sp, kt[:, 1, :, :], QRT.ap()[:, :, s * 128:(s + 1) * 128].rearrange("h d t -> d h t"), writes=[ktb], sembuf=ktb, group=True)
                        c.dma(sp, kt[:, 2, :, :], QXT.ap()[:, :, s * 128:(s + 1) * 128].rearrange("h d t -> d h t"), writes=[ktb], sembuf=ktb, group=True)
                        c.dma(sp, sg[:], SRG.ap()[s * 128:(s + 1) * 128, :], writes=[sgb], sembuf=sgb)
                        o = (kt, ktb, sg, sgb, s)
                    return (kz, kzb, vb, vbb, o)

                def b1_gen():
                    nxt = loadB(0)
                    yield
                    for i in range(NCH):
                        kz, kzb, vb, vbb, o = nxt
                        if i + 1 < NCH:
                            nxt = loadB(i + 1)
                        if o is not None:
                            kt, ktb, sg, sgb, s = o
                            ot, otb = o_rot.next()
                            for h in range(4):
                                A("pe", lambda: PE.matmul(PS[:, BS, h * 128:(h + 1) * 128], lhsT=kt[:, 0, h, :], rhs=kt[:, 1, h, :], start=True, stop=True),
                                  r=[ktb], w=[psb_[BS]], inc=(h == 3))
                            yield
                            s4, s4b = s4_rot.next()
                            A("dve", lambda: V.tensor_tensor(out=s4[:], in0=PS[:, BS, :], in1=decT[:].rearrange("p h n -> p (h n)"), op=ALU.mult), r=[psb_[BS], Bb], w=[s4b])
                            yield
                            for hp in range(2):
                                for hh in range(2):
                                    h = hp * 2 + hh
                                    A("pe", lambda: PE.matmul(PS[:, BO, hh * 256:(hh + 1) * 256], lhsT=s4[:, h * 128:(h + 1) * 128], rhs=vb[:, h * 256:(h + 1) * 256],
                                                              start=True, stop=False), r=[s4b, vbb], w=[psb_[BO]], inc=False)
                                    A("pe", lambda: PE.matmul(PS[:, BO, hh * 256:(hh + 1) * 256], lhsT=kt[:, 2, h, :], rhs=Rb[:, h, :], start=False, stop=True),
                                      r=[ktb, Rbb], w=[psb_[BO]])
                                yield
                                A("dve", lambda: V.tensor_copy(out=ot[:, hp * 512:(hp + 1) * 512], in_=PS[:, BO, :]), r=[psb_[BO]], w=[otb])
                                yield
                            st, stb = small1.next()
                            sq, sqb = o_rot.next()
                            A("dve", lambda: V.tensor_reduce(out=st[:, 0:4], in_=ot[:].rearrange("p (h e) -> p h e", h=4), axis=AX.X, op=ALU.add), r=[otb], w=[stb])
                            A("dve", lambda: V.tensor_tensor(out=sq[:], in0=ot[:], in1=ot[:], op=ALU.mult), r=[otb], w=[sqb])
                            yield
                            A("dve", lambda: V.tensor_reduce(out=st[:, 4:8], in_=sq[:].rearrange("p (h e) -> p h e", h=4), axis=AX.X, op=ALU.add), r=[sqb], w=[stb])
                            A("dve", lambda: V.tensor_scalar(out=st[:, 8:12], in0=st[:, 0:4], scalar1=1.0 / 256, scalar2=None, op0=ALU.mult), r=[stb], w=[stb])
                            A("dve", lambda: V.tensor_tensor(out=st[:, 12:16], in0=st[:, 8:12], in1=st[:, 8:12], op=ALU.mult), r=[stb], w=[stb])
                            A("dve", lambda: V.scalar_tensor_tensor(out=st[:, 16:20], in0=st[:, 4:8], scalar=1.0 / 256, in1=st[:, 12:16], op0=ALU.mult, op1=ALU.subtract),
                              r=[stb], w=[stb])
                            yield
                            A("act", lambda: ACT.activation(out=st[:, 20:24], in_=st[:, 16:20], func=AF.Sqrt, bias=GN_EPS), r=[stb], w=[stb])
                            yield
                            A("dve", lambda: V.reciprocal(out=st[:, 24:28], in_=st[:, 20:24]), r=[stb], w=[stb])
                            for h in range(4):
                                A("dve", lambda: V.tensor_scalar(out=sq[:, h * 256:(h + 1) * 256], in0=ot[:, h * 256:(h + 1) * 256], scalar1=st[:, 8 + h:9 + h],
                                                                 scalar2=st[:, 24 + h:25 + h], op0=ALU.subtract, op1=ALU.mult), r=[otb, stb], w=[sqb])
                            yield
                            A("dve", lambda: V.tensor_tensor(out=sq[:], in0=sq[:], in1=gnb[:], op=ALU.mult), r=[sqb, Bb], w=[sqb])
                            yb, ybb = yb_rot.next()
                            A("dve", lambda: V.tensor_tensor(out=yb[:], in0=sq[:], in1=sg[:], op=ALU.mult), r=[sqb, sgb], w=[ybb])
                            c.dma("pool", YB.ap()[s * 128:(s + 1) * 128, :], yb[:], reads=[ybb], sembuf=ybb)
                            yield
                        if i + 1 < NCH:
                            for hp in range(2):
                                for hh in range(2):
                                    h = hp * 2 + hh
                                    A("pe", lambda: PE.matmul(PS[:, BS, hh * 256:(hh + 1) * 256], lhsT=kz[:, h * 128:(h + 1) * 128], rhs=vb[:, h * 256:(h + 1) * 256],
                                                              start=True, stop=True), r=[kzb, vbb], w=[psb_[BS]], inc=(hh == 1))
                                yield
                                for hh in range(2):
                                    h = hp * 2 + hh
                                    A("dve", lambda: V.scalar_tensor_tensor(out=R[:, h, :], in0=R[:, h, :], scalar=gam[:, h:h + 1], in1=PS[:, BS, hh * 256:(hh + 1) * 256],
                                                                            op0=ALU.mult, op1=ALU.add), r=[psb_[BS], Rbuf, Bb], w=[Rbuf])
                                yield
                            A("dve", lambda: V.tensor_copy(out=Rb[:], in_=R[:]), r=[Rbuf], w=[Rbb])
                            yield

                b1 = b1_gen()

            with ExitStack() as eb:
                NS = 2
                KTs = [(sb("KT%d" % i, [128, NT], BF16, eb), Buf("KT%d" % i)) for i in range(NS)]
                VAs = [(sb("VA%d" % i, [128, NCH, 65], BF16, eb), Buf("VA%d" % i)) for i in range(NS)]
                QTs = [(sb("QT%d" % i, [128, NOWN * 128], BF16, eb), Buf("QT%d" % i)) for i in range(NS)]
                pt_rot = Rot([(sb("pt%d" % i, [128, 1024], BF16, eb), Buf("pt%d" % i)) for i in range(4)])
                oT_rot = Rot([(sb("oT%d" % i, [128, 512], F32, eb), Buf("oT%d" % i)) for i in range(2)])
                small = Rot([(sb("smc%d" % i, [128, 8], F32, eb), Buf("smc%d" % i)) for i in range(4)])
                prot = Rot([0, 2, 4])
                arot = Rot([6])

                def loadH(h):
                    kt, ktb = KTs[h % NS]
                    va, vab = VAs[h % NS]
                    qt, qtb = QTs[h % NS]
                    c.dma(sp, kt[0:68, :], KTa.ap()[h], writes=[ktb], sembuf=ktb)
                    c.dma(sp, va[:], VAa.ap()[:, h * 65:(h + 1) * 65].rearrange("(c p) d -> p c d", p=128), writes=[vab], sembuf=vab)
                    c.dma(sp, qt[0:68, :], QTa.ap()[h], writes=[qtb], sembuf=qtb)

                loadH(0)
                for h in range(16):
                    if h + 1 < 16:
                        loadH(h + 1)
                    kt, ktb = KTs[h % NS]
                    va, vab = VAs[h % NS]
                    qt, qtb = QTs[h % NS]
                    work = []
                    for g in range(4):
                        nfull = 8 * g + 3
                        j = 0
                        while j + 1 < nfull - 1 or (j + 1 < nfull and (j % 2 == 0) and j + 1 <= 8 * g + 1):
                            work.append((g, [j, j + 1]))
                            j += 2
                        while j <= 8 * g + 8:
                            work.append((g, [j]))
                            j += 1

                    def stageS(wk):
                        g, js = wk
                        pb_ = prot.next()
                        pt, ptb = pt_rot.next()
                        col0s = []
                        for n_, j in enumerate(js):
                            smin = max(4 * g, (j - 1) // 2)
                            col0 = (smin - 4 * g) * 128
                            col0s.append(col0)
                            A("pe", lambda: PE.matmul(PS[:, pb_ + n_, col0:512], lhsT=kt[0:68, j * 128:(j + 1) * 128],
                                                      rhs=qt[0:68, 4 * g * 128 + col0:(4 * g + 4) * 128], start=True, stop=True), r=[ktb, qtb], w=[psb_[pb_ + n_]])
                        if len(js) == 2:
                            A("act", lambda: ACT.activation(out=pt[:].rearrange("p (b c) -> p b c", b=2), in_=PS[:, pb_:pb_ + 2, :], func=AF.Exp),
                              r=[psb_[pb_], psb_[pb_ + 1]], w=[ptb])
                        else:
                            col0 = col0s[0]
                            A("act", lambda: ACT.activation(out=pt[:, col0:512], in_=PS[:, pb_, col0:512], func=AF.Exp), r=[psb_[pb_]], w=[ptb])
                        for n_, j in enumerate(js):
                            if j >= 2 and j % 2 == 0 and 4 * g <= (j - 2) // 2 <= 4 * g + 3:
                                cd = n_ * 512 + ((j - 2) // 2 - 4 * g) * 128
                                A("dve", lambda: V.tensor_tensor(out=pt[:, cd:cd + 128], in0=pt[:, cd:cd + 128], in1=maskT[:], op=ALU.mult), r=[ptb, Bc], w=[ptb])
                        return (pt, ptb, col0s)

                    acc = {}

                    def stagePV(wk, p_):
                        g, js = wk
                        pt, ptb, col0s = p_
                        jmax = 8 * g + 8
                        for n_, j in enumerate(js):
                            col0 = col0s[n_]
                            if j == 0:
                                acc[g] = arot.next()
                            ab_ = acc[g]
                            abb = psb_[ab_]
                            A("pe", lambda: PE.matmul(PS[0:65, ab_, col0:512], lhsT=va[:, j, :], rhs=pt[:, n_ * 512 + col0:n_ * 512 + 512], start=(j == 0), stop=(j == jmax)),
                              r=[ptb, vab], w=[abb])
                        if js[-1] == jmax:
                            oT, oTb = oT_rot.next()
                            A("dve", lambda: V.tensor_copy(out=oT[0:65, :], in_=PS[0:65, ab_, :]), r=[abb], w=[oTb])
                            tb_ = prot.next()
                            tbb = psb_[tb_]
                            for sl in range(4):
                                A("pe", lambda: PE.transpose(out=PS[:, tb_, sl * 65:(sl + 1) * 65], in_=oT[0:65, sl * 128:(sl + 1) * 128], identity=identf[0:65, 0:65]),
                                  r=[oTb, Bc], w=[tbb], inc=(sl == 3))
                            st, stb = small.next()
                            A("dve", lambda: V.reciprocal(out=st[:, 0:4], in_=bass.AP(tensor=PS, offset=tb_ * 512 + 64, ap=[[4096, 128], [65, 4]])), r=[tbb], w=[stb])
                            A("dve", lambda: V.tensor_tensor(out=YA[:, 4 * g:4 * g + 4, h * 64:(h + 1) * 64],
                                                             in0=bass.AP(tensor=PS, offset=tb_ * 512, ap=[[4096, 128], [65, 4], [1, 64]]),
                                                             in1=bcast(st, 0, [[8, 128], [1, 4], [0, 64]]), op=ALU.mult),
                              r=[tbb, stb], w=[YAb[s_] for s_ in range(4 * g, 4 * g + 4)])

                    DEPTH = 2
                    pendq = [stageS(work[k_]) for k_ in range(min(DEPTH, len(work)))]
                    for n_, wk in enumerate(work):
                        if n_ + DEPTH < len(work):
                            pendq.append(stageS(work[n_ + DEPTH]))
                        stagePV(wk, pendq.pop(0))
                        if n_ % 3 == 2:
                            next(b1, None)
                for _ in b1:
                    pass
                c.barrier()
            eb1.close()

            with ExitStack() as ec:
                Wof = sb("Wof", [128, 8, D], BF16, ec)
                Wor = sb("Wor", [128, 8, D], BF16, ec)
                Wou = sb("Wou", [128, 8, D], BF16, ec)
                Bcw = Buf("constsC")
                c.dma(sp, Wof[:], wbof.ap().rearrange("(k p) n -> p k n", p=128), reads=[Bw["wof"]], writes=[Bcw], sembuf=Bcw, group=True)
                c.dma(sp, Wor[:], wbor.ap().rearrange("(k p) n -> p k n", p=128), reads=[Bw["wor"]], writes=[Bcw], sembuf=Bcw, group=True)
                c.dma(sp, Wou[:], wbou.ap().rearrange("(k p) n -> p k n", p=128), reads=[Bw["wou"]], writes=[Bcw], sembuf=Bcw, group=True)
                yT_rot = Rot([(sb("yT%d" % i, [128, 8, 512], BF16, ec), Buf("yT%d" % i)) for i in range(4)])
                ybl_rot = Rot([(sb("ybl%d" % i, [128, D], BF16, ec), Buf("ybl%d" % i)) for i in range(3)])
                mix_rot = Rot([(sb("mix%d" % i, [128, 8, 512], BF16, ec), Buf("mix%d" % i)) for i in range(2)])
                g_rot = Rot([(sb("gl%d" % i, [128, 2, 512], F32, ec), Buf("gl%d" % i)) for i in range(3)])
                t_rot = Rot([(sb("tc%d" % i, [128, 512], F32, ec), Buf("tc%d" % i)) for i in range(4)])
                h_rot = Rot([(sb("hc%d" % i, [128, D], F32, ec), Buf("hc%d" % i)) for i in range(3)])
                H1b = [Buf("H1_%d" % i) for i in range(NOWN)]
                for t4 in range(4):
                    s0 = t4 * 4
                    yaT, yaTb = yT_rot.next()
                    ybT, ybTb = yT_rot.next()
                    for sl in range(4):
                        s = s0 + sl
                        ybl, yblb = ybl_rot.next()
                        c.dma(sp, ybl[:], YB.ap()[s * 128:(s + 1) * 128, :], writes=[yblb], sembuf=yblb)
                        for (src_fn, srcb, dst, dstb) in [(lambda k: YA[:, s, k * 128:(k + 1) * 128], YAb[s], yaT, yaTb),
                                                          (lambda k: ybl[:, k * 128:(k + 1) * 128], yblb, ybT, ybTb)]:
                            b, bb = ps()
                            for k in range(8):
                                A("pe", lambda: PE.transpose(out=PSB[:, b, k * 128:(k + 1) * 128], in_=src_fn(k), identity=ident[:]), r=[srcb, Bc], w=[bb], inc=(k == 7))
                            A("act", lambda: ACT.copy(out=dst[:, :, sl * 128:(sl + 1) * 128], in_=PSB[:, b, :].rearrange("p (k t) -> p k t", k=8)), r=[bb], w=[dstb])
                    mix, mixb = mix_rot.next()
                    for fb in range(8):
                        gl, glb = g_rot.next()
                        c.dma(sp, gl[:, 0, :], GT.ap()[fb, :, s0 * 128:(s0 + 4) * 128], writes=[glb], sembuf=glb, group=True)
                        c.dma(sp, gl[:, 1, :], GT.ap()[8 + fb, :, s0 * 128:(s0 + 4) * 128], writes=[glb], sembuf=glb, group=True)
                        ba, bab = ps()
                        for k in range(8):
                            A("pe", lambda: PE.matmul(PS[:, ba, :], lhsT=Wof[:, k, fb * 128:(fb + 1) * 128], rhs=yaT[:, k, :], start=(k == 0), stop=(k == 7)),
                              r=[Bcw, yaTb], w=[bab], inc=(k == 7))
                        bbk, bbb = ps()
                        for k in range(8):
                            A("pe", lambda: PE.matmul(PS[:, bbk, :], lhsT=Wor[:, k, fb * 128:(fb + 1) * 128], rhs=ybT[:, k, :], start=(k == 0), stop=(k == 7)),
                              r=[Bcw, ybTb], w=[bbb], inc=(k == 7))
                        t1, t1b = t_rot.next()
                        t2, t2b = t_rot.next()
                        A("dve", lambda: V.tensor_tensor(out=t1[:], in0=PS[:, ba, :], in1=gl[:, 0, :], op=ALU.mult), r=[bab, glb], w=[t1b])
                        A("dve", lambda: V.tensor_tensor(out=t2[:], in0=PS[:, bbk, :], in1=gl[:, 1, :], op=ALU.mult), r=[bbb, glb], w=[t2b])
                        A("dve", lambda: V.tensor_tensor(out=mix[:, fb, :], in0=t1[:], in1=t2[:], op=ALU.add), r=[t1b, t2b], w=[mixb])
                    for sl in range(4):
                        s = s0 + sl
                        hc, hcb = h_rot.next()
                        c.dma(sp, hc[:], H1.ap()[s * 128:(s + 1) * 128, :], reads=[H1b[s]], writes=[hcb], sembuf=hcb)
                        for fh in range(2):
                            b, bb = ps()
                            for k in range(8):
                                A("pe", lambda: PE.matmul(PS[:, b, :], lhsT=mix[:, k, sl * 128:(sl + 1) * 128], rhs=Wou[:, k, fh * 512:(fh + 1) * 512],
                                                          start=(k == 0), stop=(k == 7)), r=[mixb, Bcw], w=[bb], inc=(k == 7))
                            A("dve", lambda: V.tensor_tensor(out=hc[:, fh * 512:(fh + 1) * 512], in0=PS[:, b, :], in1=hc[:, fh * 512:(fh + 1) * 512], op=ALU.add),
                              r=[bb, hcb], w=[hcb])
                        c.dma("pool", H1.ap()[s * 128:(s + 1) * 128, :], hc[:], reads=[hcb], writes=[H1b[s]], sembuf=hcb)
                c.barrier()
            ya_cm.__exit__(None, None, None)

            with ExitStack() as ec:
                hx = sb("hx2", [128, 8, D], F32, ec)
                xT = sb("xT2", [128, 8, 1024], BF16, ec)
                gT = sb("gT2", [128, 6, 1024], BF16, ec)
                w2h = sb("w2h2", [128, 6, D], BF16, ec)
                slots = [(sb("wsc%d" % i, [128, 4096], BF16, ec), Buf("wsc%d" % i)) for i in range(4)]
                small = Rot([(sb("smd%d" % i, [128, 64], F32, ec), Buf("smd%d" % i)) for i in range(6)])
                xn_rot = Rot([(sb("xnd%d" % i, [128, D], BF16, ec), Buf("xnd%d" % i)) for i in range(4)])
                sil_rot = Rot([(sb("sild%d" % i, [128, 512], F32, ec), Buf("sild%d" % i)) for i in range(3)])
                hxb = [Buf("hxd%d" % i) for i in range(8)]
                xTb = [Buf("xTd%d" % i) for i in range(8)]
                gTb = [Buf("gTd%d" % i) for i in range(2)]
                w2hb = Buf("w2hd")
                outb = Buf("out")

                def v_ffn2(t):
                    return t[:].rearrange("p (k a c) -> p k a c", k=8, a=2)

                jobs = []
                for t8 in range(2):
                    for sl in range(11):
                        jobs.append(("bf16", wb2i.ap()[sl].rearrange("p k a c -> p (k a c)"), [Bw["w2i_%d" % (0 if sl < 4 else 1 if sl < 8 else 2)]]))
                slabs = Slabs(slots, jobs)
                for t8 in range(2):
                    for lc in range(8):
                        s = t8 * 8 + lc
                        c.dma(sp, hx[:, lc, :], H1.ap()[s * 128:(s + 1) * 128, :], writes=[hxb[lc]], sembuf=hxb[lc])
                    norm_tile(8, hx, hxb, g2T, xT, xTb, small, xn_rot)

                    def last(lc, t8=t8):
                        s = t8 * 8 + lc
                        c.dma("pool", out.ap()[s * 128:(s + 1) * 128, :], hx[:, lc, :], reads=[hxb[lc]], writes=[outb], sembuf=hxb[lc])

                    ffn(8, hx, hxb, xT, xTb, gT, gTb, w2h, w2hb, slabs, t8 * 11, wb2o, [Bw["w2o"]] * len(PARTS), sil_rot, last=last)
                c.barrier()
            print("inst counts", c.ninst, "sems", c.nsem)
    return nc


def host_consts(p):
    idx = np.arange(NT)
    orig = idx if p == 1 else idx - 128
    valid = (orig >= 112).astype(np.float32)
    pos = np.where(valid > 0, orig - 112, 0).astype(np.float64)
    half = 64
    inv = 10000.0 ** (-np.arange(half, dtype=np.float64) / half)
    ang = pos[:, None] * inv[None, :]
    cos = np.cos(ang)
    sin = np.sin(ang)

    def pc(a):
        return np.ascontiguousarray(a.reshape(NCH, 128, -1).transpose(1, 0, 2)).astype(np.float32)

    vc = np.ascontiguousarray(valid.reshape(NCH, 128).T)
    log_gamma = np.log1p(-np.exp2(-5.0 - np.arange(4, dtype=np.float64)))
    n = np.arange(128, dtype=np.float64)
    diff = n[None, :] - n[:, None]
    decT = np.where(diff[:, None, :] >= 0, np.exp(log_gamma[None, :, None] * np.maximum(diff[:, None, :], 0.0)), 0.0).astype(np.float32)
    zeta = np.exp(log_gamma[None, :] * (127 - n)[:, None]).astype(np.float32)
    xi = np.exp(log_gamma[None, :] * (n + 1.0)[:, None]).astype(np.float32)
    gam = np.broadcast_to(np.exp(log_gamma * 128)[None, :], (128, 4)).astype(np.float32)
    tri = (n[:, None] <= n[None, :]).astype(np.float32)
    C2 = np.concatenate([cos, cos], 1)
    S2 = np.concatenate([-sin, sin], 1)
    vk = (valid * (128 ** -0.5))[:, None]
    return dict(ck_t=pc(C2 * vk), sk_t=pc(S2 * vk), cq_t=pc(C2), sq_t=pc(S2), vcol=vc, nvcol=-vc, vkcol=(vc * (128 ** -0.5)).astype(np.float32),
                decayT=np.ascontiguousarray(decT), zeta=np.ascontiguousarray(zeta), xi=np.ascontiguousarray(xi),
                gam=np.ascontiguousarray(gam), tri=tri)


_NC_CACHE = {}


def make_in_maps(inputs):
    x = np.asarray(inputs["x"], dtype=np.float32)
    meta = np.asarray(inputs["meta_tokens"], dtype=np.float32)
    shared = {}
    for k in ["w_ffn1_in", "w_ffn1_out", "w_ffn2_in", "w_ffn2_out", "w_in", "w_o_fox", "w_o_ret", "w_out"]:
        shared[k] = np.ascontiguousarray(np.asarray(inputs[k], dtype=np.float32)[0])
    for k in ["norm_ffn1", "norm_mix", "norm_ffn2", "b_forget", "b_gate", "fox_q_norm", "fox_k_norm", "ret_gn"]:
        shared[k] = np.ascontiguousarray(np.asarray(inputs[k], dtype=np.float32)[0])
    consts = [host_consts(0), host_consts(1)]
    in_maps = []
    for core in range(8):
        b, p = core // 2, core % 2
        lead = np.concatenate([np.zeros((112, D), np.float32), meta], axis=0)
        if p == 1:
            seq = np.concatenate([lead, x[b]], axis=0)
        else:
            seq = np.concatenate([np.zeros((128, D), np.float32), lead, x[b, :NT - 256]], axis=0)
        m = dict(shared)
        m.update(consts[p])
        m["xs"] = np.ascontiguousarray(seq)
        in_maps.append(m)
    return in_maps


def kernel(**inputs):
    if "nc" not in _NC_CACHE:
        _NC_CACHE["nc"] = build_nc()
    nc = _NC_CACHE["nc"]
    in_maps = make_in_maps(inputs)
    res = run_bass_kernel_spmd(nc, in_maps, core_ids=list(range(8)))
    B = 4
    out = np.empty((B, 4096, D), np.float32)
    for core in range(8):
        b, p = core // 2, core % 2
        o = res.results[core]["out"].reshape(NOWN, 128, D)
        for s in range(NOWN):
            i = 2 * s + 2
            orig = i if p == 1 else i - 1
            out[b, (orig - 1) * 128:orig * 128] = o[s]
    return out
```

```python
import numpy as np
import concourse.bass as bass
import concourse.mybir as mybir
from concourse.bass_utils import run_bass_kernel_spmd
from contextlib import ExitStack

F32 = mybir.dt.float32
BF16 = mybir.dt.bfloat16
AF = mybir.ActivationFunctionType
ALU = mybir.AluOpType
AX = mybir.AxisListType

NCH = 33
NT = NCH * 128
D = 1024
DFF = 2816
NOWN = 16
EPS = 1e-6
GN_EPS = 1e-5
TILES = [list(range(0, 7)), list(range(7, 14)), list(range(14, 21)), list(range(21, 27)), list(range(27, 33))]
TW = 896
PARTS = [(0, 3), (3, 6), (6, 9), (9, 11)]
WIN_SLABS = [("fk0", 1024), ("fk1", 1536), ("fv0", 2048), ("fv1", 2560), ("fq0", 0), ("fq1", 512),
             ("rk", 3600), ("rks", -3600), ("rv0", 4112), ("rv1", 4624), ("rq", 3088), ("rqs", -3088), ("rg0", 5136), ("rg1", 5648),
             ("ga0", 6160), ("ga1", 6672), ("gb0", 7184), ("gb1", 7696)]
WIN_IDX = {n: i for i, (n, _) in enumerate(WIN_SLABS)}
FF_OFF = 3072


def is_own(i):
    return i % 2 == 0 and i > 0


class Tok:
    __slots__ = ("eng", "sem", "val", "key")

    def __init__(self, eng, sem, val, key):
        self.eng, self.sem, self.val, self.key = eng, sem, val, key


class Buf:
    def __init__(self, name):
        self.name = name
        self.w = None
        self.r = []
        self.dsem = None
        self.dcnt = 0
        self.dtok = None


class Ctx:
    def __init__(self, nc, es):
        self.nc, self.es = nc, es
        self.engs = {"pe": nc.tensor, "act": nc.scalar, "dve": nc.vector, "pool": nc.gpsimd, "sp": nc.sync}
        self.sem = {k: es.enter_context(nc.semaphore("s_" + k)) for k in ["pe", "act", "dve", "pool"]}
        self.cnt = {k: 0 for k in self.sem}
        self.seen = {k: {} for k in self.engs}
        self.pending = {k: [] for k in self.sem}
        self.nsem = 0
        self.ninst = {k: 0 for k in self.engs}
        self.dbufs = []

    def _deps(self, reads, writes):
        toks = []
        for b in reads:
            if b.w is not None:
                toks.append(b.w)
        for b in writes:
            if b.w is not None:
                toks.append(b.w)
            toks.extend(b.r)
        return toks

    def _wait(self, e, toks):
        need = {}
        for t in toks:
            if t.eng == e and e == "pe":
                continue
            assert t.val is not None, "dependency on un-milestoned instruction (%s)" % t.eng
            if t.key not in need or need[t.key][1] < t.val:
                need[t.key] = (t.sem, t.val)
        for key, (sem, val) in need.items():
            if self.seen[e].get(key, 0) >= val:
                continue
            self.engs[e].wait_ge(sem, val)
            self.seen[e][key] = val

    def op(self, e, fn, reads=(), writes=(), inc=True):
        self._wait(e, self._deps(reads, writes))
        ins = fn()
        self.ninst[e] += 1
        tok = Tok(e, self.sem[e], None, e)
        for b in reads:
            b.r.append(tok)
        for b in writes:
            b.w = tok
            b.r = []
        self.pending[e].append(tok)
        if inc:
            ins.then_inc(self.sem[e], 1)
            self.cnt[e] += 1
            for t in self.pending[e]:
                t.val = self.cnt[e]
            self.pending[e] = []
        return ins

    def dma(self, q, out, in_, reads=(), writes=(), sembuf=None, group=False, **kw):
        if sembuf.dsem is None:
            sembuf.dsem = self.es.enter_context(self.nc.semaphore("d%d" % self.nsem))
            sembuf.dkey = "d%d" % self.nsem
            self.nsem += 1
            self.dbufs.append(sembuf)
        toks = self._deps(reads, writes)
        if sembuf.dtok is not None and not group:
            toks.append(sembuf.dtok)
        self._wait(q, toks)
        ins = self.engs[q].dma_start(out=out, in_=in_, **kw)
        ins.then_inc(sembuf.dsem, 16)
        self.ninst[q] += 1
        sembuf.dcnt += 16
        tok = Tok("dma", sembuf.dsem, sembuf.dcnt, sembuf.dkey)
        sembuf.dtok = tok
        for b in reads:
            b.r.append(tok)
        for b in writes:
            b.w = tok
            b.r = []
        return tok

    def barrier(self):
        toks = []
        for e in self.sem:
            assert not self.pending[e], e
            if self.cnt[e] > 0:
                toks.append(Tok(e + "_b", self.sem[e], self.cnt[e], e))
        for b in self.dbufs:
            if b.dtok is not None:
                toks.append(b.dtok)
        for e in self.engs:
            self._wait(e, [t for t in toks if t.key != e])


class Rot:
    def __init__(self, items):
        self.items = items
        self.i = 0

    def next(self):
        it = self.items[self.i % len(self.items)]
        self.i += 1
        return it


def build_nc(dbg=False):
    nc = bass.Bass("TRN2", target_bir_lowering=False)

    def I(n, s):
        return nc.dram_tensor(n, list(s), F32, kind="ExternalInput")

    kind_s = "ExternalOutput" if dbg else "Internal"

    def S(n, s, dt):
        return nc.dram_tensor(n, list(s), dt, kind=kind_s)

    xs = I("xs", [NT, D])
    w1i, w1o = I("w_ffn1_in", [D, 2 * DFF]), I("w_ffn1_out", [DFF, D])
    w2i, w2o = I("w_ffn2_in", [D, 2 * DFF]), I("w_ffn2_out", [DFF, D])
    w_in = I("w_in", [D, 8208])
    w_of, w_or, w_ou = I("w_o_fox", [D, D]), I("w_o_ret", [D, D]), I("w_out", [D, D])
    g1, gm, g2 = I("norm_ffn1", [D]), I("norm_mix", [D]), I("norm_ffn2", [D])
    b_forget, b_gate = I("b_forget", [16]), I("b_gate", [2048])
    qn, kn, gn = I("fox_q_norm", [64]), I("fox_k_norm", [64]), I("ret_gn", [D])
    ck_t, sk_t = I("ck_t", [128, NCH, 128]), I("sk_t", [128, NCH, 128])
    cq_t, sq_t = I("cq_t", [128, NCH, 128]), I("sq_t", [128, NCH, 128])
    vcol_d, nvcol_d, vkcol_d = I("vcol", [128, NCH]), I("nvcol", [128, NCH]), I("vkcol", [128, NCH])
    decayT_d = I("decayT", [128, 4, 128])
    zeta_d, xi_d, gam_d = I("zeta", [128, 4]), I("xi", [128, 4]), I("gam", [128, 4])
    tri_d = I("tri", [128, 128])
    out = nc.dram_tensor("out", [NOWN * 128, D], F32, kind="ExternalOutput")

    wb1i, wb2i = S("wb1i", [11, 128, 8, 2, 256], BF16), S("wb2i", [11, 128, 8, 2, 256], BF16)
    wb1o, wb2o = S("wb1o", [DFF, D], BF16), S("wb2o", [DFF, D], BF16)
    wbin = S("wbin", [18, 128, 8, 512], BF16)
    wbff = S("wbff", [128, 8, 16], BF16)
    wbof, wbor, wbou = S("wbof", [D, D], BF16), S("wbor", [D, D], BF16), S("wbou", [D, D], BF16)
    H1 = S("H1", [NOWN * 128, D], F32)
    KTa = S("KTa", [16, 68, NT], BF16)
    VAa = S("VAa", [NT, 16 * 65], BF16)
    QTa = S("QTa", [16, 68, NOWN * 128], BF16)
    KZ = S("KZ", [NT, 512], BF16)
    VB = S("VB", [NT, 1024], BF16)
    KRT = S("KRT", [4, 128, NT], BF16)
    QRT = S("QRT", [4, 128, NOWN * 128], BF16)
    QXT = S("QXT", [4, 128, NOWN * 128], BF16)
    SRG = S("SRG", [NOWN * 128, D], F32)
    GT = S("GT", [16, 128, NOWN * 128], F32)
    YB = S("YB", [NOWN * 128, D], BF16)

    with ExitStack() as es:
        def sb(n, s, d, stack=None):
            return (stack or es).enter_context(nc.sbuf_tensor(n, list(s), d))

        PS = es.enter_context(nc.psum_tensor("PS", [128, 8, 512], F32))
        PSB = PS.bitcast(BF16)
        block = es.enter_context(nc.Block())
        c = Ctx(nc, es)
        psb_ = [Buf("ps%d" % i) for i in range(8)]

        def A(e, fn, r=(), w=(), inc=True):
            return c.op(e, fn, r, w, inc)

        V, G, ACT, PE = nc.vector, nc.gpsimd, nc.scalar, nc.tensor

        def bcast(t, off, dims):
            return bass.AP(tensor=t, offset=off, ap=[list(d) for d in dims])

        ident = sb("ident", [128, 128], BF16)
        identf = sb("identf", [128, 128], F32)
        tri_f = sb("tri_f", [128, 128], F32)
        ones_f = sb("ones_f", [128, 128], F32)
        maskT = sb("maskT", [128, 128], BF16)
        g1T, gmT, g2T = sb("g1T", [128, 8], F32), sb("gmT", [128, 8], F32), sb("g2T", [128, 8], F32)
        bgT = sb("bgT", [128, 16], F32)
        vcol, nvcol, vkcol = sb("vcol_s", [128, NCH], F32), sb("nvcol_s", [128, NCH], F32), sb("vkcol_s", [128, NCH], F32)
        Bc = Buf("consts")
        YAb = [Buf("YA%d" % i) for i in range(NOWN)]

        @block.sync
        def _(sync):
            sp = "sp"
            Bw = {}
            late = []

            def emit_late(n=1):
                for _ in range(n):
                    if late:
                        late.pop(0)()

            def castbuf(n):
                Bw[n] = Buf(n)
                return Bw[n]

            def cast_ffn_in(src, dst, name):
                sv = src.ap().rearrange("(kc p) n -> p kc n", p=128)
                for part, (s0, s1) in enumerate([(0, 4), (4, 8), (8, 11)]):
                    b = castbuf("%s_%d" % (name, part))
                    for sl in range(s0, s1):
                        for ab in range(2):
                            late.append(lambda b=b, sl=sl, ab=ab: c.dma("pool", dst.ap()[sl, :, :, ab, :], sv[:, :, ab * DFF + sl * 256: ab * DFF + sl * 256 + 256],
                                                                    writes=[b], sembuf=b, group=True))

            def cast_rows(src, dst, name, nsplit):
                b = castbuf(name)
                rows = src.shape[0]
                step = rows // nsplit
                for k in range(nsplit):
                    r0 = k * step
                    r1 = rows if k == nsplit - 1 else r0 + step
                    late.append(lambda b=b, r0=r0, r1=r1: c.dma("pool", dst.ap()[r0:r1, :], src.ap()[r0:r1, :], writes=[b], sembuf=b, group=True))

            def cast_win():
                sv = w_in.ap().rearrange("(kc p) n -> p kc n", p=128)
                b = castbuf("wbff")
                c.dma("pool", wbff.ap(), sv[:, :, FF_OFF:FF_OFF + 16], writes=[b], sembuf=b, group=True)

            cast_win()
            cast_rows(w_of, wbof, "wof", 2)
            cast_rows(w_or, wbor, "wor", 2)
            cast_rows(w_ou, wbou, "wou", 2)
            cast_ffn_in(w2i, wb2i, "w2i")
            cast_rows(w2o, wb2o, "w2o", 4)

            c.dma(sp, tri_f[:], tri_d.ap(), writes=[Bc], sembuf=Bc, group=True)
            for t, d_ in [(g1T, g1), (gmT, gm), (g2T, g2)]:
                c.dma(sp, t[:], d_.ap().rearrange("(k p) -> p k", p=128), writes=[Bc], sembuf=Bc, group=True,
                      allow_slow_non_contiguous=True)
            c.dma(sp, bgT[:], b_gate.ap().rearrange("(k p) -> p k", p=128), writes=[Bc], sembuf=Bc, group=True,
                  allow_slow_non_contiguous=True)
            for t, d_ in [(vcol, vcol_d), (nvcol, nvcol_d), (vkcol, vkcol_d)]:
                c.dma(sp, t[:], d_.ap(), writes=[Bc], sembuf=Bc, group=True)
            A("pool", lambda: G.memset(identf[:], 0.0), w=[Bc])
            A("pool", lambda: G.affine_select(out=identf[:], in_=identf[:], pattern=[[-1, 128]], compare_op=ALU.not_equal,
                                              fill=1.0, base=0, channel_multiplier=1), r=[Bc], w=[Bc])
            A("pool", lambda: G.memset(ones_f[:], 1.0), w=[Bc])
            A("dve", lambda: V.tensor_copy(out=ident[:], in_=identf[:]), r=[Bc], w=[Bc])
            A("dve", lambda: V.tensor_copy(out=maskT[:], in_=tri_f[:]), r=[Bc], w=[Bc])

            psrot = Rot(list(range(8)))

            def ps():
                b = psrot.next()
                return b, psb_[b]

            def ps2():
                if psrot.i % 2 == 1:
                    psrot.i += 1
                b = psrot.next()
                psrot.next()
                return b, [psb_[b], psb_[b + 1]]

            def norm_tile(n, hx, hxb, gainT, xT, xTb, small, xn_rot, pre=None):
                for g0 in range(0, n, 4):
                    grp = list(range(g0, min(n, g0 + 4)))
                    items = []
                    for lc in grp:
                        if pre is not None:
                            pre(lc)
                        src, srcb = hx[:, lc, :], hxb[lc]
                        st, stb = small.next()
                        xn, xnb = xn_rot.next()
                        A("act", lambda: ACT.activation(out=xn[:], in_=src, func=AF.Square, accum_out=st[:, 0:1]), r=[srcb], w=[stb, xnb])
                        A("act", lambda: ACT.activation(out=st[:, 1:2], in_=st[:, 0:1], func=AF.Sqrt, scale=1.0 / D, bias=EPS), r=[stb], w=[stb])
                        items.append((lc, src, srcb, st, stb, xn, xnb))
                    for n_, (lc, src, srcb, st, stb, xn, xnb) in enumerate(items):
                        A("dve", lambda: V.reciprocal(out=st[:, 2:3], in_=st[:, 1:2]), r=[stb], w=[stb])
                        if n_ % 2 == 0:
                            A("dve", lambda: V.tensor_scalar(out=xn[:], in0=src, scalar1=st[:, 2:3], scalar2=None, op0=ALU.mult), r=[srcb, stb], w=[xnb])
                        else:
                            A("act", lambda: ACT.activation(out=xn[:], in_=src, func=AF.Copy, scale=st[:, 2:3]), r=[srcb, stb], w=[xnb])
                    for (lc, src, srcb, st, stb, xn, xnb) in items:
                        b, bb = ps()
                        for k in range(8):
                            A("pe", lambda: PE.transpose(out=PSB[:, b, k * 128:(k + 1) * 128], in_=xn[:, k * 128:(k + 1) * 128], identity=ident[:]),
                              r=[xnb, Bc], w=[bb], inc=(k == 7))
                        A("dve", lambda: V.tensor_tensor(out=xT[:, :, lc * 128:(lc + 1) * 128], in0=PSB[:, b, :].rearrange("p (k t) -> p k t", k=8),
                                                         in1=bcast(gainT, 0, [[8, 128], [1, 8], [0, 128]]), op=ALU.mult), r=[bb, Bc], w=[xTb[lc]])

            class Slabs:
                def __init__(self, slots, jobs, stage=None):
                    self.slots, self.jobs, self.stage = slots, jobs, stage
                    self.loadptr, self.convptr, self.nload_f32, self.nconv_f32 = 0, 0, 0, 0
                    self.stage_of = {}

                def _load(self, j):
                    t, b = self.slots[j % len(self.slots)]
                    job = self.jobs[j]
                    if job[0] == "bf16":
                        c.dma(sp, t[:], job[1], reads=job[2], writes=[b], sembuf=b)
                    elif job[0] == "f32":
                        st, stb = self.stage.next()
                        self.stage_of[j] = (st, stb)
                        for k_, (vf, src) in enumerate(job[1]):
                            c.dma(sp, vf(st), src, writes=[stb], sembuf=stb, group=(k_ > 0))
                        self.nload_f32 += 1

                def _conv(self, j):
                    t, b = self.slots[j % len(self.slots)]
                    job = self.jobs[j]
                    if job[0] == "f32":
                        st, stb = self.stage_of.pop(j)
                        A("act", lambda: ACT.copy(out=t[:, 0:2048], in_=st[:, 0:2048]), r=[stb], w=[b])
                        A("dve", lambda: V.tensor_copy(out=t[:, 2048:4096], in_=st[:, 2048:4096]), r=[stb], w=[b])
                        c.dma("pool", job[2], t[:], reads=[b], writes=[job[3]], sembuf=b)
                        self.nconv_f32 += 1
                    elif job[0] == "swap":
                        ts_, bs_ = self.slots[(j - 1) % len(self.slots)]
                        ov = t[:].rearrange("p (k h t d) -> p k h t d", k=8, h=4, t=2)
                        iv = ts_[:].rearrange("p (k h t d) -> p k h t d", k=8, h=4, t=2)
                        A("act", lambda: ACT.copy(out=ov[:, :, :, 0, :], in_=iv[:, :, :, 1, :]), r=[bs_], w=[b])
                        A("dve", lambda: V.tensor_copy(out=ov[:, :, :, 1, :], in_=iv[:, :, :, 0, :]), r=[bs_], w=[b])

                def get(self, j, oldest=None):
                    oldest = j if oldest is None else oldest
                    n = len(self.slots)
                    while True:
                        progressed = False
                        while self.convptr < self.loadptr and self.convptr <= j + 1 and self.convptr < oldest + n:
                            self._conv(self.convptr)
                            self.convptr += 1
                            progressed = True
                        while self.loadptr < min(len(self.jobs), oldest + n):
                            if self.jobs[self.loadptr][0] == "f32" and self.nload_f32 - self.nconv_f32 >= len(self.stage.items):
                                break
                            self._load(self.loadptr)
                            self.loadptr += 1
                            progressed = True
                        if not progressed:
                            break
                    assert self.convptr > j, (self.convptr, self.loadptr, j)
                    return self.slots[j % n]

            def ffn(tile_n, hx, hxb, xT, xTb, gT, gTb, w2h, w2hb, slabs, j0, wbo, wbob, sil_rot, last=None, first_src=None, stage=None):
                ntok = tile_n * 128
                cgs = [(o, min(512, ntok - o)) for o in range(0, ntok, 512)]
                for half, (sl0, sl1) in enumerate(PARTS):
                    nj = (sl1 - sl0) * 2
                    jbase = sl0 * 2
                    if first_src is None:
                        c.dma(sp, w2h[:, 0:nj, :], wbo.ap()[jbase * 128:(jbase + nj) * 128, :].rearrange("(j p) n -> p j n", p=128),
                              reads=[wbob[half]], writes=[w2hb], sembuf=w2hb)
                    else:
                        for q0 in range(nj):
                            st, stb = stage.next()
                            r0 = (jbase + q0) * 128
                            c.dma(sp, st[:], first_src.ap()[r0:r0 + 128, :], writes=[stb], sembuf=stb)
                            A("act" if q0 % 2 == 0 else "dve", lambda: (ACT.copy if q0 % 2 == 0 else V.tensor_copy)(out=w2h[:, q0, :], in_=st[:]),
                              r=[stb], w=[w2hb])
                        c.dma("pool", wbo.ap()[jbase * 128:(jbase + nj) * 128, :].rearrange("(j p) n -> p j n", p=128), w2h[:, 0:nj, :],
                              reads=[w2hb], writes=[wbob[half]], sembuf=w2hb)
                    for sl in range(sl0, sl1):
                        wt, wtb = slabs.get(j0 + sl)
                        wv = wt[:].rearrange("p (k a c) -> p k a c", k=8, a=2)
                        for jj in range(2):
                            jl = (sl - sl0) * 2 + jj
                            for (o, n) in cgs:
                                lcs = list(range(o // 128, (o + n) // 128))
                                pa, pab = ps()
                                pb, pbb = ps()
                                for k in range(8):
                                    A("pe", lambda: PE.matmul(PS[:, pa, 0:n], lhsT=wv[:, k, 0, jj * 128:(jj + 1) * 128], rhs=xT[:, k, o:o + n],
                                                              start=(k == 0), stop=(k == 7)), r=[wtb] + [xTb[l] for l in lcs], w=[pab], inc=(k == 7))
                                for k in range(8):
                                    A("pe", lambda: PE.matmul(PS[:, pb, 0:n], lhsT=wv[:, k, 1, jj * 128:(jj + 1) * 128], rhs=xT[:, k, o:o + n],
                                                              start=(k == 0), stop=(k == 7)), r=[wtb] + [xTb[l] for l in lcs], w=[pbb], inc=(k == 7))
                                sa, sab = sil_rot.next()
                                A("act", lambda: ACT.activation(out=sa[:, 0:n], in_=PS[:, pa, 0:n], func=AF.Silu), r=[pab], w=[sab])
                                A("dve", lambda: V.tensor_tensor(out=gT[:, jl, o:o + n], in0=sa[:, 0:n], in1=PS[:, pb, 0:n], op=ALU.mult),
                                  r=[sab, pbb], w=[gTb[o // 512]])
                    for lc in range(tile_n):
                        for fh in range(2):
                            po, pob = ps()
                            for jl in range(nj):
                                A("pe", lambda: PE.matmul(PS[:, po, :], lhsT=gT[:, jl, lc * 128:(lc + 1) * 128], rhs=w2h[:, jl, fh * 512:(fh + 1) * 512],
                                                          start=(jl == 0), stop=(jl == nj - 1)), r=[gTb[lc // 4], w2hb], w=[pob], inc=(jl == nj - 1))
                            A("dve", lambda: V.scalar_tensor_tensor(out=hx[:, lc, fh * 512:(fh + 1) * 512], in0=PS[:, po, :], scalar=0.5,
                                                                    in1=hx[:, lc, fh * 512:(fh + 1) * 512], op0=ALU.mult, op1=ALU.add),
                              r=[pob, hxb[lc]], w=[hxb[lc]])
                        if half == len(PARTS) - 1 and last is not None:
                            last(lc)

            with ExitStack() as ea:
                hx = sb("hx", [128, 7, D], F32, ea)
                xT = sb("xT", [128, 8, TW], BF16, ea)
                gT = sb("gT", [128, 6, TW], BF16, ea)
                w2h = sb("w2h", [128, 6, D], BF16, ea)
                slots = [(sb("wsl%d" % i, [128, 4096], BF16, ea), Buf("wsl%d" % i)) for i in range(4)]
                small = Rot([(sb("sm%d" % i, [128, 64], F32, ea), Buf("sm%d" % i)) for i in range(8)])
                xn_rot = Rot([(sb("xn%d" % i, [128, D], BF16, ea), Buf("xn%d" % i)) for i in range(4)])
                sil_rot = Rot([(sb("sil%d" % i, [128, 512], F32, ea), Buf("sil%d" % i)) for i in range(3)])
                tf_rot = Rot([(sb("tf%d" % i, [128, 512], F32, ea), Buf("tf%d" % i)) for i in range(4)])
                tb_rot = Rot([(sb("tb%d" % i, [128, 1024], BF16, ea), Buf("tb%d" % i)) for i in range(6)])
                kA_rot = Rot([(sb("kA%d" % i, [128, 16, 68], BF16, ea), Buf("kA%d" % i)) for i in range(2)])
                qA_rot = Rot([(sb("qA%d" % i, [128, 16, 68], BF16, ea), Buf("qA%d" % i)) for i in range(2)])
                vA_rot = Rot([(sb("vA%d" % i, [128, 16, 65], BF16, ea), Buf("vA%d" % i)) for i in range(2)])
                st_rot = Rot([(sb("stg%d" % i, [128, 16, 128], BF16, ea), Buf("stg%d" % i)) for i in range(2)])
                cs_rot = Rot([(sb("cs%d" % i, [128, 2, 128], F32, ea), Buf("cs%d" % i)) for i in range(3)])
                wff = sb("wff", [128, 8, 16], BF16, ea)
                bfb = sb("bfb", [128, 16], F32, ea)
                qgc = sb("qgc", [128, 1], F32, ea)
                kgc = sb("kgc", [128, 1], F32, ea)
                tf2_rot = Rot([(sb("tf2_%d" % i, [128, 1024], F32, ea), Buf("tf2_%d" % i)) for i in range(2)])
                cc_all = sb("cc_all", [128, NCH + 1, 16], F32, ea)
                cref_all = sb("cref_all", [128, NCH + 1, 16], F32, ea)
                zeta, xi = sb("zeta_s", [128, 4], F32, ea), sb("xi_s", [128, 4], F32, ea)
                hxb = [Buf("hx%d" % i) for i in range(9)]
                xTb = [Buf("xT%d" % i) for i in range(9)]
                gTb = [Buf("gT%d" % i) for i in range(3)]
                w2hb = Buf("w2h")
                Ba = Buf("constsA")
                ccb = Buf("cc")

                c.dma(sp, wff[:], wbff.ap(), reads=[Bw["wbff"]], writes=[Ba], sembuf=Ba, group=True)
                c.dma(sp, bfb[:], bcast(b_forget, 0, [[0, 128], [1, 16]]), writes=[Ba], sembuf=Ba, group=True)
                A("pool", lambda: G.memset(qgc[:], 1.0), w=[Ba])
                A("pool", lambda: G.memset(kgc[:], 1.0), w=[Ba])
                c.dma(sp, qgc[0:64, :], qn.ap().rearrange("(d o) -> d o", o=1), writes=[Ba], sembuf=Ba)
                c.dma(sp, kgc[0:64, :], kn.ap().rearrange("(d o) -> d o", o=1), writes=[Ba], sembuf=Ba)
                c.dma(sp, zeta[:], zeta_d.ap(), writes=[Ba], sembuf=Ba, group=True)
                c.dma(sp, xi[:], xi_d.ap(), writes=[Ba], sembuf=Ba, group=True)
                A("act", lambda: ACT.mul(qgc[0:64, :], qgc[0:64, :], 0.125), r=[Ba], w=[Ba])
                A("pool", lambda: G.memset(cc_all[:, 0, :], 0.0), w=[ccb])
                A("pool", lambda: G.memset(cref_all[:, 0, :], 0.0), r=[ccb], w=[ccb])
                for (t, _) in kA_rot.items:
                    A("pool", lambda: G.memset(t[:, :, 67:68], 1.0), w=[_])
                for (t, _) in qA_rot.items:
                    A("pool", lambda: G.memset(t[:, :, 64:67], 1.0), w=[_])

                jobs = []
                tile_j0 = []

                def v_ffn(t):
                    return t[:].rearrange("p (k a c) -> p k a c", k=8, a=2)

                def v_win(t):
                    return t[:].rearrange("p (k c) -> p k c", k=8)

                Bw1i = [Buf("w1i_%d" % sl) for sl in range(11)]
                Bwin = {n: Buf("win_" + n) for n, _ in WIN_SLABS}
                Bw1o = [Buf("w1o_%d" % k) for k in range(len(PARTS))]
                stage = Rot([(sb("stage%d" % i, [128, 4096], F32, ea), Buf("stage%d" % i)) for i in range(2)])
                w1i_v = w1i.ap().rearrange("(kc p) n -> p kc n", p=128)
                win_v = w_in.ap().rearrange("(kc p) n -> p kc n", p=128)
                for ti, tile in enumerate(TILES):
                    tile_j0.append(len(jobs))
                    for sl in range(11):
                        dst = wb1i.ap()[sl].rearrange("p k a c -> p (k a c)")
                        if ti == 0:
                            parts = [((lambda st, ab=ab: st[:].rearrange("p (k a c) -> p k a c", k=8, a=2)[:, :, ab, :]),
                                      w1i_v[:, :, ab * DFF + sl * 256: ab * DFF + sl * 256 + 256]) for ab in range(2)]
                            jobs.append(("f32", parts, dst, Bw1i[sl]))
                        else:
                            jobs.append(("bf16", dst, [Bw1i[sl]]))
                    for n, off in WIN_SLABS:
                        dst = wbin.ap()[WIN_IDX[n]].rearrange("p k c -> p (k c)")
                        if off < 0:
                            jobs.append(("swap",))
                        elif ti == 0:
                            parts = [((lambda st: st[:].rearrange("p (k c) -> p k c", k=8)), win_v[:, :, off:off + 512])]
                            jobs.append(("f32", parts, dst, Bwin[n]))
                        else:
                            jobs.append(("bf16", dst, [Bwin[n]]))
                slabs = Slabs(slots, jobs, stage)

                def load_x(ti_):
                    for lc_, i_ in enumerate(TILES[ti_]):
                        c.dma(sp, hx[:, lc_, :], xs.ap()[i_ * 128:(i_ + 1) * 128, :], writes=[hxb[lc_]], sembuf=hxb[lc_])

                for ti, tile in enumerate(TILES):
                    tn = len(tile)
                    ntok = tn * 128
                    j0 = tile_j0[ti]
                    own_lc = [lc for lc in range(tn) if is_own(tile[lc])]
                    if ti == 0:
                        load_x(0)
                    norm_tile(tn, hx, hxb, g1T, xT, xTb, small, xn_rot)
                    ffn(tn, hx, hxb, xT, xTb, gT, gTb, w2h, w2hb, slabs, j0, wb1o, Bw1o, sil_rot,
                        first_src=(w1o if ti == 0 else None), stage=tf2_rot)
                    def store_h1(lc):
                        i = tile[lc]
                        if is_own(i):
                            s = i // 2 - 1
                            c.dma("pool", H1.ap()[s * 128:(s + 1) * 128, :], hx[:, lc, :], reads=[hxb[lc]], sembuf=hxb[lc])

                    norm_tile(tn, hx, hxb, gmT, xT, xTb, small, xn_rot, pre=store_h1)
                    if ti + 1 < len(TILES):
                        load_x(ti + 1)
                    jw = j0 + 11
                    J = lambda n_: jw + WIN_IDX[n_]

                    def proj(lc, wt, wtb, ncols=512):
                        b, bb = ps()
                        wv = wt[:].rearrange("p (k c) -> p k c", k=8) if ncols == 512 else wt
                        for k in range(8):
                            A("pe", lambda: PE.matmul(PS[:, b, 0:ncols], lhsT=xT[:, k, lc * 128:(lc + 1) * 128], rhs=wv[:, k, 0:ncols],
                                                      start=(k == 0), stop=(k == 7)), r=[wtb, xTb[lc]], w=[bb], inc=(k == 7))
                        return b, bb

                    def proj2(lc, wa, wab, wb_, wbb):
                        b, bbs = ps2()
                        for g_, (wt, wtb) in enumerate([(wa, wab), (wb_, wbb)]):
                            wv = wt[:].rearrange("p (k c) -> p k c", k=8)
                            for k in range(8):
                                A("pe", lambda: PE.matmul(PS[:, b + g_, :], lhsT=xT[:, k, lc * 128:(lc + 1) * 128], rhs=wv[:, k, :],
                                                          start=(k == 0), stop=(k == 7)), r=[wtb, xTb[lc]], w=[bbs[g_]], inc=(k == 7))
                        return b, bbs

                    for lc in range(tn):
                        i = tile[lc]
                        b, bb = proj(lc, wff, Ba, 16)
                        st, stb = small.next()
                        A("dve", lambda: V.tensor_tensor(out=st[:, 0:16], in0=PS[:, b, 0:16], in1=bfb[:], op=ALU.add), r=[bb, Ba], w=[stb])
                        A("act", lambda: ACT.activation(out=st[:, 16:32], in_=st[:, 0:16], func=AF.Exp, scale=-1.0), r=[stb], w=[stb])
                        A("act", lambda: ACT.activation(out=st[:, 32:48], in_=st[:, 16:32], func=AF.Ln, bias=1.0), r=[stb], w=[stb])
                        A("dve", lambda: V.tensor_scalar(out=st[:, 48:64], in0=st[:, 32:48], scalar1=nvcol[:, i:i + 1], scalar2=None, op0=ALU.mult),
                          r=[stb, Bc], w=[stb])
                        b2, bb2 = ps()
                        A("pe", lambda: PE.matmul(PS[:, b2, 0:16], lhsT=tri_f[:], rhs=st[:, 48:64], start=True, stop=True), r=[Bc, stb], w=[bb2], inc=False)
                        A("pe", lambda: PE.matmul(PS[:, b2, 16:32], lhsT=ones_f[:], rhs=st[:, 48:64], start=True, stop=True), r=[Bc, stb], w=[bb2])
                        A("dve", lambda: V.tensor_tensor(out=cc_all[:, i + 1, :], in0=PS[:, b2, 0:16], in1=cref_all[:, i, :], op=ALU.add), r=[bb2, ccb], w=[ccb])
                        A("dve", lambda: V.tensor_tensor(out=cref_all[:, i + 1, :], in0=PS[:, b2, 16:32], in1=cref_all[:, i, :], op=ALU.add), r=[bb2, ccb], w=[ccb])

                    def headnorm(b, bbs, dstA, dstAb):
                        tf, tfb = tf2_rot.next()
                        st, stb = small.next()
                        A("act", lambda: ACT.activation(out=tf[:].rearrange("p (g c) -> p g c", g=2), in_=PS[:, b:b + 2, :], func=AF.Square), r=bbs, w=[tfb])
                        A("dve", lambda: V.tensor_reduce(out=st[:, 0:16], in_=tf[:].rearrange("p (h d) -> p h d", h=16), axis=AX.X, op=ALU.add), r=[tfb], w=[stb])
                        A("act", lambda: ACT.activation(out=st[:, 16:32], in_=st[:, 0:16], func=AF.Sqrt, scale=1.0 / 64, bias=EPS), r=[stb], w=[stb])
                        A("dve", lambda: V.reciprocal(out=st[:, 32:48], in_=st[:, 16:32]), r=[stb], w=[stb])
                        A("dve", lambda: V.tensor_tensor(out=dstA[:, :, 0:64], in0=PS[:, b:b + 2, :].rearrange("p g (h d) -> p (g h) d", h=8),
                                                         in1=bcast(st, 32, [[64, 128], [1, 16], [0, 64]]), op=ALU.mult), r=bbs + [stb], w=[dstAb])

                    def transpose16(srcA, srcAb, width, dst_dram_fn, gcol=None):
                        stg, stgb = st_rot.next()
                        for half in range(2):
                            b, bb = ps()
                            for hh in range(8):
                                h = half * 8 + hh
                                A("pe", lambda: PE.transpose(out=PSB[0:width, b, hh * 128:(hh + 1) * 128], in_=srcA[:, h, 0:width], identity=ident[:]),
                                  r=[srcAb, Bc], w=[bb], inc=(hh == 7))
                            if gcol is None:
                                A("act", lambda: ACT.copy(out=stg[0:width, half * 8:half * 8 + 8, :], in_=PSB[0:width, b, :].rearrange("p (h t) -> p h t", h=8)),
                                  r=[bb], w=[stgb])
                            else:
                                A("act", lambda: ACT.activation(out=stg[0:width, half * 8:half * 8 + 8, :], in_=PSB[0:width, b, :].rearrange("p (h t) -> p h t", h=8),
                                                                func=AF.Copy, scale=gcol[0:width, 0:1]), r=[bb, Ba], w=[stgb])
                        c.dma("pool", dst_dram_fn(), stg[0:width, :, :], reads=[stgb], sembuf=stgb)

                    w0, w0b = slabs.get(J("fk0"))
                    w1, w1b = slabs.get(J("fk1"), J("fk0"))
                    pend = None

                    def k_stage1(lc):
                        i = tile[lc]
                        b, bbs = proj2(lc, w0, w0b, w1, w1b)
                        kA, kAb = kA_rot.next()
                        headnorm(b, bbs, kA, kAb)
                        cs, csb = cs_rot.next()
                        st, stb = small.next()
                        hb = st[:].bitcast(BF16)
                        cc = cc_all[:, i + 1, :]
                        A("dve", lambda: V.tensor_copy(out=hb[:, 0:16], in_=cc), r=[ccb], w=[stb])
                        A("dve", lambda: V.tensor_tensor(out=cs[:, 0, 0:16], in0=cc, in1=hb[:, 0:16], op=ALU.subtract), r=[ccb, stb], w=[csb])
                        A("dve", lambda: V.tensor_copy(out=hb[:, 16:32], in_=cs[:, 0, 0:16]), r=[csb], w=[stb])
                        A("dve", lambda: V.tensor_tensor(out=cs[:, 1, 0:16], in0=cs[:, 0, 0:16], in1=hb[:, 16:32], op=ALU.subtract), r=[csb, stb], w=[csb])
                        A("dve", lambda: V.tensor_scalar(out=kA[:, :, 64:65], in0=hb[:, 0:16].rearrange("p (h o) -> p h o", o=1), scalar1=-1.0, scalar2=None, op0=ALU.mult),
                          r=[stb], w=[kAb])
                        A("dve", lambda: V.tensor_scalar(out=kA[:, :, 65:66], in0=hb[:, 16:32].rearrange("p (h o) -> p h o", o=1), scalar1=-1.0, scalar2=None, op0=ALU.mult),
                          r=[stb], w=[kAb])
                        A("dve", lambda: V.tensor_scalar(out=kA[:, :, 66:67], in0=cs[:, 1, 0:16].rearrange("p (h o) -> p h o", o=1), scalar1=-1.0, scalar2=None, op0=ALU.mult),
                          r=[csb], w=[kAb])
                        return (kA, kAb, i)

                    def k_stage2(p_):
                        kA, kAb, i = p_
                        transpose16(kA, kAb, 68, lambda: KTa.ap()[:, :, i * 128:(i + 1) * 128].rearrange("h r t -> r h t"), gcol=kgc)

                    pend = k_stage1(0)
                    for lc in range(tn):
                        nxt = k_stage1(lc + 1) if lc + 1 < tn else None
                        k_stage2(pend)
                        pend = nxt
                        if ti >= 1:
                            emit_late(1)

                    w0, w0b = slabs.get(J("fv0"))
                    w1, w1b = slabs.get(J("fv1"), J("fv0"))
                    for lc in range(tn):
                        i = tile[lc]
                        vA, vAb = vA_rot.next()
                        for g_, (wt, wtb) in enumerate([(w0, w0b), (w1, w1b)]):
                            b, bb = proj(lc, wt, wtb)
                            A("act", lambda: ACT.activation(out=vA[:, 8 * g_:8 * g_ + 8, 0:64], in_=PS[:, b, :].rearrange("p (h d) -> p h d", h=8),
                                                            func=AF.Copy, scale=vcol[:, i:i + 1]), r=[bb, Bc], w=[vAb])
                        A("dve", lambda: V.tensor_copy(out=vA[:, :, 64:65], in_=bcast(vcol, i, [[NCH, 128], [0, 16], [1, 1]])), r=[Bc], w=[vAb])
                        c.dma("pool", VAa.ap()[i * 128:(i + 1) * 128, :], vA[:].rearrange("p h d -> p (h d)"), reads=[vAb], sembuf=vAb)
                        if ti >= 1:
                            emit_late(1)

                    w0, w0b = slabs.get(J("fq0"))
                    w1, w1b = slabs.get(J("fq1"), J("fq0"))

                    def q_stage1(lc):
                        i = tile[lc]
                        b, bbs = proj2(lc, w0, w0b, w1, w1b)
                        qA, qAb = qA_rot.next()
                        headnorm(b, bbs, qA, qAb)
                        A("dve", lambda: V.tensor_copy(out=qA[:, :, 67:68], in_=cref_all[:, i + 1, :].rearrange("p (h o) -> p h o", o=1)), r=[ccb], w=[qAb])
                        return (qA, qAb, i)

                    def q_stage2(p_):
                        qA, qAb, i = p_
                        s = i // 2 - 1
                        transpose16(qA, qAb, 68, lambda: QTa.ap()[:, :, s * 128:(s + 1) * 128].rearrange("h r t -> r h t"), gcol=qgc)

                    if own_lc:
                        pend = q_stage1(own_lc[0])
                        for n_, lc in enumerate(own_lc):
                            nxt = q_stage1(own_lc[n_ + 1]) if n_ + 1 < len(own_lc) else None
                            q_stage2(pend)
                            pend = nxt

                    def rotary(b, bbs, i, ct, st_):
                        cs, csb = cs_rot.next()
                        c.dma(sp, cs[:, 0, :], ct.ap()[:, i, :], writes=[csb], sembuf=csb, group=True)
                        c.dma(sp, cs[:, 1, :], st_.ap()[:, i, :], writes=[csb], sembuf=csb, group=True)
                        t1, t1b = tf_rot.next()
                        t2, t2b = tf_rot.next()
                        A("dve", lambda: V.tensor_tensor(out=t1[:].rearrange("p (h d) -> p h d", h=4), in0=PS[:, b, :].rearrange("p (h d) -> p h d", h=4),
                                                         in1=bcast(cs, 0, [[256, 128], [0, 4], [1, 128]]), op=ALU.mult), r=[bbs[0], csb], w=[t1b])
                        A("dve", lambda: V.tensor_tensor(out=t2[:].rearrange("p (h d) -> p h d", h=4), in0=PS[:, b + 1, :].rearrange("p (h d) -> p h d", h=4),
                                                         in1=bcast(cs, 128, [[256, 128], [0, 4], [1, 128]]), op=ALU.mult), r=[bbs[1], csb], w=[t2b])
                        o, ob = tb_rot.next()
                        A("dve", lambda: V.tensor_tensor(out=o[:, 0:512], in0=t1[:], in1=t2[:], op=ALU.add), r=[t1b, t2b], w=[ob])
                        return o, ob

                    def transpose4(src, srcb, dst_fn):
                        stg, stgb = st_rot.next()
                        b, bb = ps()
                        for h in range(4):
                            A("pe", lambda: PE.transpose(out=PSB[:, b, h * 128:(h + 1) * 128], in_=src[:, h * 128:(h + 1) * 128], identity=ident[:]),
                              r=[srcb, Bc], w=[bb], inc=(h == 3))
                        A("act", lambda: ACT.copy(out=stg[:, 0:4, :], in_=PSB[:, b, 0:512].rearrange("p (h t) -> p h t", h=4)), r=[bb], w=[stgb])
                        c.dma("pool", dst_fn(), stg[:, 0:4, :], reads=[stgb], sembuf=stgb)

                    w0, w0b = slabs.get(J("rk"))
                    w1, w1b = slabs.get(J("rks"), J("rk"))

                    def rk_stage1(lc):
                        i = tile[lc]
                        b, bbs = proj2(lc, w0, w0b, w1, w1b)
                        kr, krb = rotary(b, bbs, i, ck_t, sk_t)
                        kz, kzb = tb_rot.next()
                        A("dve", lambda: V.tensor_tensor(out=kz[:, 0:512].rearrange("p (h d) -> p h d", h=4), in0=kr[:, 0:512].rearrange("p (h d) -> p h d", h=4),
                                                         in1=bcast(zeta, 0, [[4, 128], [1, 4], [0, 128]]), op=ALU.mult), r=[krb, Ba], w=[kzb])
                        c.dma("pool", KZ.ap()[i * 128:(i + 1) * 128, :], kz[:, 0:512], reads=[kzb], sembuf=kzb)
                        return (kr, krb, i)

                    pend = rk_stage1(0)
                    for lc in range(tn):
                        nxt = rk_stage1(lc + 1) if lc + 1 < tn else None
                        kr, krb, i = pend
                        transpose4(kr, krb, lambda: KRT.ap()[:, :, i * 128:(i + 1) * 128].rearrange("h d t -> d h t"))
                        pend = nxt

                    w0, w0b = slabs.get(J("rv0"))
                    w1, w1b = slabs.get(J("rv1"), J("rv0"))
                    for lc in range(tn):
                        i = tile[lc]
                        vb, vbb = tb_rot.next()
                        for g_, (wt, wtb) in enumerate([(w0, w0b), (w1, w1b)]):
                            b, bb = proj(lc, wt, wtb)
                            A("act", lambda: ACT.activation(out=vb[:, g_ * 512:(g_ + 1) * 512], in_=PS[:, b, :], func=AF.Copy, scale=vcol[:, i:i + 1]),
                              r=[bb, Bc], w=[vbb])
                        c.dma("pool", VB.ap()[i * 128:(i + 1) * 128, :], vb[:], reads=[vbb], sembuf=vbb)

                    w0, w0b = slabs.get(J("rq"))
                    w1, w1b = slabs.get(J("rqs"), J("rq"))

                    def rq_stage1(lc):
                        i = tile[lc]
                        b, bbs = proj2(lc, w0, w0b, w1, w1b)
                        qr, qrb = rotary(b, bbs, i, cq_t, sq_t)
                        qx, qxb = tb_rot.next()
                        A("dve", lambda: V.tensor_tensor(out=qx[:, 0:512].rearrange("p (h d) -> p h d", h=4), in0=qr[:, 0:512].rearrange("p (h d) -> p h d", h=4),
                                                         in1=bcast(xi, 0, [[4, 128], [1, 4], [0, 128]]), op=ALU.mult), r=[qrb, Ba], w=[qxb])
                        return (qr, qrb, qx, qxb, i)

                    if own_lc:
                        pend = rq_stage1(own_lc[0])
                        for n_, lc in enumerate(own_lc):
                            nxt = rq_stage1(own_lc[n_ + 1]) if n_ + 1 < len(own_lc) else None
                            qr, qrb, qx, qxb, i = pend
                            s = i // 2 - 1
                            transpose4(qr, qrb, lambda: QRT.ap()[:, :, s * 128:(s + 1) * 128].rearrange("h d t -> d h t"))
                            transpose4(qx, qxb, lambda: QXT.ap()[:, :, s * 128:(s + 1) * 128].rearrange("h d t -> d h t"))
                            pend = nxt

                    w0, w0b = slabs.get(J("rg0"))
                    w1, w1b = slabs.get(J("rg1"), J("rg0"))
                    for lc in own_lc:
                        i = tile[lc]
                        s = i // 2 - 1
                        for g_, (wt, wtb) in enumerate([(w0, w0b), (w1, w1b)]):
                            b, bb = proj(lc, wt, wtb)
                            tf, tfb = tf_rot.next()
                            A("act", lambda: ACT.activation(out=tf[:], in_=PS[:, b, :], func=AF.Silu), r=[bb], w=[tfb])
                            c.dma("pool", SRG.ap()[s * 128:(s + 1) * 128, g_ * 512:(g_ + 1) * 512], tf[:], reads=[tfb], sembuf=tfb)

                    if own_lc:
                        ogs = [own_lc[k:k + 4] for k in range(0, len(own_lc), 4)]
                        for gi in range(4):
                            wt, wtb = slabs.get(J("ga0") + gi)
                            wv = wt[:].rearrange("p (k c) -> p k c", k=8)
                            for fb in range(4):
                                f16 = gi * 4 + fb
                                for og in ogs:
                                    n = len(og) * 128
                                    s0 = tile[og[0]] // 2 - 1
                                    b, bb = ps()
                                    for k in range(8):
                                        rhs = bcast(xT, k * TW + og[0] * 128, [[8 * TW, 128], [256, len(og)], [1, 128]])
                                        A("pe", lambda: PE.matmul(PS[:, b, 0:n], lhsT=wv[:, k, fb * 128:(fb + 1) * 128], rhs=rhs, start=(k == 0), stop=(k == 7)),
                                          r=[wtb] + [xTb[l] for l in og], w=[bb], inc=(k == 7))
                                    tf, tfb = tf_rot.next()
                                    A("act", lambda: ACT.activation(out=tf[:, 0:n], in_=PS[:, b, 0:n], func=AF.Sigmoid, bias=bgT[:, f16:f16 + 1]), r=[bb, Bc], w=[tfb])
                                    c.dma("pool", GT.ap()[f16, :, s0 * 128:s0 * 128 + n], tf[:, 0:n], reads=[tfb], sembuf=tfb)
                emit_late(len(late))
                c.barrier()

            ya_cm = nc.sbuf_tensor("YA", [128, NOWN, D], BF16)
            YA = ya_cm.__enter__()
            eb1 = ExitStack()
            eb1.__enter__()
            if True:
                eb = eb1
                R = sb("R", [128, 4, 256], F32, eb)
                Rb = sb("Rb", [128, 4, 256], BF16, eb)
                decT = sb("decT", [128, 4, 128], F32, eb)
                gam = sb("gam_s", [128, 4], F32, eb)
                gnb = sb("gnb", [128, D], F32, eb)
                Rbuf, Rbb, Bb = Buf("R"), Buf("Rb"), Buf("constsB")
                kz_rot = Rot([(sb("kz%d" % i, [128, 512], BF16, eb), Buf("kz%d" % i)) for i in range(3)])
                vb_rot = Rot([(sb("vb%d" % i, [128, 1024], BF16, eb), Buf("vb%d" % i)) for i in range(3)])
                kt_rot = Rot([(sb("kt%d" % i, [128, 3, 4, 128], BF16, eb), Buf("kt%d" % i)) for i in range(2)])
                sg_rot = Rot([(sb("sg%d" % i, [128, D], F32, eb), Buf("sg%d" % i)) for i in range(2)])
                s4_rot = Rot([(sb("sT%d" % i, [128, 512], BF16, eb), Buf("sT%d" % i)) for i in range(2)])
                o_rot = Rot([(sb("o%d" % i, [128, D], F32, eb), Buf("o%d" % i)) for i in range(2)])
                yb_rot = Rot([(sb("yb%d" % i, [128, D], BF16, eb), Buf("yb%d" % i)) for i in range(2)])
                small1 = Rot([(sb("smb%d" % i, [128, 64], F32, eb), Buf("smb%d" % i)) for i in range(4)])
                c.dma(sp, decT[:], decayT_d.ap(), writes=[Bb], sembuf=Bb, group=True)
                c.dma(sp, gam[:], gam_d.ap(), writes=[Bb], sembuf=Bb, group=True)
                c.dma(sp, gnb[:], bcast(gn, 0, [[0, 128], [1, D]]), writes=[Bb], sembuf=Bb, group=True)
                A("pool", lambda: G.memset(R[:], 0.0), w=[Rbuf])
                A("pool", lambda: G.memset(Rb[:], 0.0), w=[Rbb])
                BO, BS = 6, 7

                def loadB(i):
                    kz, kzb = kz_rot.next()
                    vb, vbb = vb_rot.next()
                    c.dma(sp, kz[:], KZ.ap()[i * 128:(i + 1) * 128, :], writes=[kzb], sembuf=kzb)
                    c.dma(sp, vb[:], VB.ap()[i * 128:(i + 1) * 128, :], writes=[vbb], sembuf=vbb)
                    o = None
                    if is_own(i):
                        s = i // 2 - 1
                        kt, ktb = kt_rot.next()
                        sg, sgb = sg_rot.next()
                        c.dma(sp, kt[:, 0, :, :], KRT.ap()[:, :, i * 128:(i + 1) * 128].rearrange("h d t -> d h t"), writes=[ktb], sembuf=ktb, group=True)
                        c.dma(sp, kt[:, 1, :, :], QRT.ap()[:, :, s * 128:(s + 1) * 128].rearrange("h d t -> d h t"), writes=[ktb], sembuf=ktb, group=True)
                        c.dma(sp, kt[:, 2, :, :], QXT.ap()[:, :, s * 128:(s + 1) * 128].rearrange("h d t -> d h t"), writes=[ktb], sembuf=ktb, group=True)
                        c.dma(sp, sg[:], SRG.ap()[s * 128:(s + 1) * 128, :], writes=[sgb], sembuf=sgb)
                        o = (kt, ktb, sg, sgb, s)
                    return (kz, kzb, vb, vbb, o)

                def b1_gen():
                    nxt = loadB(0)
                    yield
                    for i in range(NCH):
                        kz, kzb, vb, vbb, o = nxt
                        if i + 1 < NCH:
                            nxt = loadB(i + 1)
                        if o is not None:
                            kt, ktb, sg, sgb, s = o
                            ot, otb = o_rot.next()
                            for h in range(4):
                                A("pe", lambda: PE.matmul(PS[:, BS, h * 128:(h + 1) * 128], lhsT=kt[:, 0, h, :], rhs=kt[:, 1, h, :], start=True, stop=True),
                                  r=[ktb], w=[psb_[BS]], inc=(h == 3))
                            yield
                            s4, s4b = s4_rot.next()
                            A("dve", lambda: V.tensor_tensor(out=s4[:], in0=PS[:, BS, :], in1=decT[:].rearrange("p h n -> p (h n)"), op=ALU.mult), r=[psb_[BS], Bb], w=[s4b])
                            yield
                            for hp in range(2):
                                for hh in range(2):
                                    h = hp * 2 + hh
                                    A("pe", lambda: PE.matmul(PS[:, BO, hh * 256:(hh + 1) * 256], lhsT=s4[:, h * 128:(h + 1) * 128], rhs=vb[:, h * 256:(h + 1) * 256],
                                                              start=True, stop=False), r=[s4b, vbb], w=[psb_[BO]], inc=False)
                                    A("pe", lambda: PE.matmul(PS[:, BO, hh * 256:(hh + 1) * 256], lhsT=kt[:, 2, h, :], rhs=Rb[:, h, :], start=False, stop=True),
                                      r=[ktb, Rbb], w=[psb_[BO]])
                                yield
                                A("dve", lambda: V.tensor_copy(out=ot[:, hp * 512:(hp + 1) * 512], in_=PS[:, BO, :]), r=[psb_[BO]], w=[otb])
                                yield
                            st, stb = small1.next()
                            sq, sqb = o_rot.next()
                            A("dve", lambda: V.tensor_reduce(out=st[:, 0:4], in_=ot[:].rearrange("p (h e) -> p h e", h=4), axis=AX.X, op=ALU.add), r=[otb], w=[stb])
                            A("dve", lambda: V.tensor_tensor(out=sq[:], in0=ot[:], in1=ot[:], op=ALU.mult), r=[otb], w=[sqb])
                            yield
                            A("dve", lambda: V.tensor_reduce(out=st[:, 4:8], in_=sq[:].rearrange("p (h e) -> p h e", h=4), axis=AX.X, op=ALU.add), r=[sqb], w=[stb])
                            A("dve", lambda: V.tensor_scalar(out=st[:, 8:12], in0=st[:, 0:4], scalar1=1.0 / 256, scalar2=None, op0=ALU.mult), r=[stb], w=[stb])
                            A("dve", lambda: V.tensor_tensor(out=st[:, 12:16], in0=st[:, 8:12], in1=st[:, 8:12], op=ALU.mult), r=[stb], w=[stb])
                            A("dve", lambda: V.scalar_tensor_tensor(out=st[:, 16:20], in0=st[:, 4:8], scalar=1.0 / 256, in1=st[:, 12:16], op0=ALU.mult, op1=ALU.subtract),
                              r=[stb], w=[stb])
                            yield
                            A("act", lambda: ACT.activation(out=st[:, 20:24], in_=st[:, 16:20], func=AF.Sqrt, bias=GN_EPS), r=[stb], w=[stb])
                            yield
                            A("dve", lambda: V.reciprocal(out=st[:, 24:28], in_=st[:, 20:24]), r=[stb], w=[stb])
                            for h in range(4):
                                A("dve", lambda: V.tensor_scalar(out=sq[:, h * 256:(h + 1) * 256], in0=ot[:, h * 256:(h + 1) * 256], scalar1=st[:, 8 + h:9 + h],
                                                                 scalar2=st[:, 24 + h:25 + h], op0=ALU.subtract, op1=ALU.mult), r=[otb, stb], w=[sqb])
                            yield
                            A("dve", lambda: V.tensor_tensor(out=sq[:], in0=sq[:], in1=gnb[:], op=ALU.mult), r=[sqb, Bb], w=[sqb])
                            yb, ybb = yb_rot.next()
                            A("dve", lambda: V.tensor_tensor(out=yb[:], in0=sq[:], in1=sg[:], op=ALU.mult), r=[sqb, sgb], w=[ybb])
                            c.dma("pool", YB.ap()[s * 128:(s + 1) * 128, :], yb[:], reads=[ybb], sembuf=ybb)
                            yield
                        if i + 1 < NCH:
                            for hp in range(2):
                                for hh in range(2):
                                    h = hp * 2 + hh
                                    A("pe", lambda: PE.matmul(PS[:, BS, hh * 256:(hh + 1) * 256], lhsT=kz[:, h * 128:(h + 1) * 128], rhs=vb[:, h * 256:(h + 1) * 256],
                                                              start=True, stop=True), r=[kzb, vbb], w=[psb_[BS]], inc=(hh == 1))
                                yield
                                for hh in range(2):
                                    h = hp * 2 + hh
                                    A("dve", lambda: V.scalar_tensor_tensor(out=R[:, h, :], in0=R[:, h, :], scalar=gam[:, h:h + 1], in1=PS[:, BS, hh * 256:(hh + 1) * 256],
                                                                            op0=ALU.mult, op1=ALU.add), r=[psb_[BS], Rbuf, Bb], w=[Rbuf])
                                yield
                            A("dve", lambda: V.tensor_copy(out=Rb[:], in_=R[:]), r=[Rbuf], w=[Rbb])
                            yield

                b1 = b1_gen()

            with ExitStack() as eb:
                NS = 2
                KTs = [(sb("KT%d" % i, [128, NT], BF16, eb), Buf("KT%d" % i)) for i in range(NS)]
                VAs = [(sb("VA%d" % i, [128, NCH, 65], BF16, eb), Buf("VA%d" % i)) for i in range(NS)]
                QTs = [(sb("QT%d" % i, [128, NOWN * 128], BF16, eb), Buf("QT%d" % i)) for i in range(NS)]
                pt_rot = Rot([(sb("pt%d" % i, [128, 512], BF16, eb), Buf("pt%d" % i)) for i in range(6)])
                oT_rot = Rot([(sb("oT%d" % i, [128, 512], F32, eb), Buf("oT%d" % i)) for i in range(2)])
                small = Rot([(sb("smc%d" % i, [128, 8], F32, eb), Buf("smc%d" % i)) for i in range(4)])
                srot = Rot([0, 1, 2, 3])
                arot = Rot([4, 5])
                trot = srot

                def loadH(h):
                    kt, ktb = KTs[h % NS]
                    va, vab = VAs[h % NS]
                    qt, qtb = QTs[h % NS]
                    c.dma(sp, kt[0:68, :], KTa.ap()[h], writes=[ktb], sembuf=ktb)
                    c.dma(sp, va[:], VAa.ap()[:, h * 65:(h + 1) * 65].rearrange("(c p) d -> p c d", p=128), writes=[vab], sembuf=vab)
                    c.dma(sp, qt[0:68, :], QTa.ap()[h], writes=[qtb], sembuf=qtb)

                loadH(0)
                for h in range(16):
                    if h + 1 < 16:
                        loadH(h + 1)
                    kt, ktb = KTs[h % NS]
                    va, vab = VAs[h % NS]
                    qt, qtb = QTs[h % NS]
                    work = [(g, j) for g in range(4) for j in range(8 * g + 9)]

                    def stageS(wk):
                        g, j = wk
                        smin = max(4 * g, (j - 1) // 2)
                        col0 = (smin - 4 * g) * 128
                        b = srot.next()
                        bb = psb_[b]
                        A("pe", lambda: PE.matmul(PS[:, b, col0:512], lhsT=kt[0:68, j * 128:(j + 1) * 128], rhs=qt[0:68, 4 * g * 128 + col0:(4 * g + 4) * 128],
                                                  start=True, stop=True), r=[ktb, qtb], w=[bb])
                        pt, ptb = pt_rot.next()
                        A("act", lambda: ACT.activation(out=pt[:, col0:512], in_=PS[:, b, col0:512], func=AF.Exp), r=[bb], w=[ptb])
                        if j >= 2 and j % 2 == 0 and 4 * g <= (j - 2) // 2 <= 4 * g + 3:
                            cd = ((j - 2) // 2 - 4 * g) * 128
                            A("dve", lambda: V.tensor_tensor(out=pt[:, cd:cd + 128], in0=pt[:, cd:cd + 128], in1=maskT[:], op=ALU.mult), r=[ptb, Bc], w=[ptb])
                        return (pt, ptb, col0)

                    acc = {}

                    def stagePV(wk, p_):
                        g, j = wk
                        pt, ptb, col0 = p_
                        jmax = 8 * g + 8
                        if j == 0:
                            acc[g] = arot.next()
                        ab_ = acc[g]
                        abb = psb_[ab_]
                        A("pe", lambda: PE.matmul(PS[0:65, ab_, col0:512], lhsT=va[:, j, :], rhs=pt[:, col0:512], start=(j == 0), stop=(j == jmax)),
                          r=[ptb, vab], w=[abb])
                        if j == jmax:
                            oT, oTb = oT_rot.next()
                            A("dve", lambda: V.tensor_copy(out=oT[0:65, :], in_=PS[0:65, ab_, :]), r=[abb], w=[oTb])
                            tb_ = trot.next()
                            tbb = psb_[tb_]
                            for sl in range(4):
                                A("pe", lambda: PE.transpose(out=PS[:, tb_, sl * 65:(sl + 1) * 65], in_=oT[0:65, sl * 128:(sl + 1) * 128], identity=identf[0:65, 0:65]),
                                  r=[oTb, Bc], w=[tbb], inc=(sl == 3))
                            st, stb = small.next()
                            A("dve", lambda: V.reciprocal(out=st[:, 0:4], in_=bass.AP(tensor=PS, offset=tb_ * 512 + 64, ap=[[4096, 128], [65, 4]])), r=[tbb], w=[stb])
                            A("dve", lambda: V.tensor_tensor(out=YA[:, 4 * g:4 * g + 4, h * 64:(h + 1) * 64],
                                                             in0=bass.AP(tensor=PS, offset=tb_ * 512, ap=[[4096, 128], [65, 4], [1, 64]]),
                                                             in1=bcast(st, 0, [[8, 128], [1, 4], [0, 64]]), op=ALU.mult),
                              r=[tbb, stb], w=[YAb[s_] for s_ in range(4 * g, 4 * g + 4)])

                    DEPTH = 3
                    pendq = [stageS(work[k_]) for k_ in range(min(DEPTH, len(work)))]
                    for n_, wk in enumerate(work):
                        if n_ + DEPTH < len(work):
                            pendq.append(stageS(work[n_ + DEPTH]))
                        stagePV(wk, pendq.pop(0))
                        if n_ % 4 == 3:
                            next(b1, None)
                for _ in b1:
                    pass
                c.barrier()
            eb1.close()

            with ExitStack() as ec:
                Wof = sb("Wof", [128, 8, D], BF16, ec)
                Wor = sb("Wor", [128, 8, D], BF16, ec)
                Wou = sb("Wou", [128, 8, D], BF16, ec)
                Bcw = Buf("constsC")
                c.dma(sp, Wof[:], wbof.ap().rearrange("(k p) n -> p k n", p=128), reads=[Bw["wof"]], writes=[Bcw], sembuf=Bcw, group=True)
                c.dma(sp, Wor[:], wbor.ap().rearrange("(k p) n -> p k n", p=128), reads=[Bw["wor"]], writes=[Bcw], sembuf=Bcw, group=True)
                c.dma(sp, Wou[:], wbou.ap().rearrange("(k p) n -> p k n", p=128), reads=[Bw["wou"]], writes=[Bcw], sembuf=Bcw, group=True)
                yT_rot = Rot([(sb("yT%d" % i, [128, 8, 512], BF16, ec), Buf("yT%d" % i)) for i in range(4)])
                ybl_rot = Rot([(sb("ybl%d" % i, [128, D], BF16, ec), Buf("ybl%d" % i)) for i in range(3)])
                mix_rot = Rot([(sb("mix%d" % i, [128, 8, 512], BF16, ec), Buf("mix%d" % i)) for i in range(2)])
                g_rot = Rot([(sb("gl%d" % i, [128, 2, 512], F32, ec), Buf("gl%d" % i)) for i in range(3)])
                t_rot = Rot([(sb("tc%d" % i, [128, 512], F32, ec), Buf("tc%d" % i)) for i in range(4)])
                h_rot = Rot([(sb("hc%d" % i, [128, D], F32, ec), Buf("hc%d" % i)) for i in range(3)])
                H1b = [Buf("H1_%d" % i) for i in range(NOWN)]
                for t4 in range(4):
                    s0 = t4 * 4
                    yaT, yaTb = yT_rot.next()
                    ybT, ybTb = yT_rot.next()
                    for sl in range(4):
                        s = s0 + sl
                        ybl, yblb = ybl_rot.next()
                        c.dma(sp, ybl[:], YB.ap()[s * 128:(s + 1) * 128, :], writes=[yblb], sembuf=yblb)
                        for (src_fn, srcb, dst, dstb) in [(lambda k: YA[:, s, k * 128:(k + 1) * 128], YAb[s], yaT, yaTb),
                                                          (lambda k: ybl[:, k * 128:(k + 1) * 128], yblb, ybT, ybTb)]:
                            b, bb = ps()
                            for k in range(8):
                                A("pe", lambda: PE.transpose(out=PSB[:, b, k * 128:(k + 1) * 128], in_=src_fn(k), identity=ident[:]), r=[srcb, Bc], w=[bb], inc=(k == 7))
                            A("act", lambda: ACT.copy(out=dst[:, :, sl * 128:(sl + 1) * 128], in_=PSB[:, b, :].rearrange("p (k t) -> p k t", k=8)), r=[bb], w=[dstb])
                    mix, mixb = mix_rot.next()
                    for fb in range(8):
                        gl, glb = g_rot.next()
                        c.dma(sp, gl[:, 0, :], GT.ap()[fb, :, s0 * 128:(s0 + 4) * 128], writes=[glb], sembuf=glb, group=True)
                        c.dma(sp, gl[:, 1, :], GT.ap()[8 + fb, :, s0 * 128:(s0 + 4) * 128], writes=[glb], sembuf=glb, group=True)
                        ba, bab = ps()
                        for k in range(8):
                            A("pe", lambda: PE.matmul(PS[:, ba, :], lhsT=Wof[:, k, fb * 128:(fb + 1) * 128], rhs=yaT[:, k, :], start=(k == 0), stop=(k == 7)),
                              r=[Bcw, yaTb], w=[bab], inc=(k == 7))
                        bbk, bbb = ps()
                        for k in range(8):
                            A("pe", lambda: PE.matmul(PS[:, bbk, :], lhsT=Wor[:, k, fb * 128:(fb + 1) * 128], rhs=ybT[:, k, :], start=(k == 0), stop=(k == 7)),
                              r=[Bcw, ybTb], w=[bbb], inc=(k == 7))
                        t1, t1b = t_rot.next()
                        t2, t2b = t_rot.next()
                        A("dve", lambda: V.tensor_tensor(out=t1[:], in0=PS[:, ba, :], in1=gl[:, 0, :], op=ALU.mult), r=[bab, glb], w=[t1b])
                        A("dve", lambda: V.tensor_tensor(out=t2[:], in0=PS[:, bbk, :], in1=gl[:, 1, :], op=ALU.mult), r=[bbb, glb], w=[t2b])
                        A("dve", lambda: V.tensor_tensor(out=mix[:, fb, :], in0=t1[:], in1=t2[:], op=ALU.add), r=[t1b, t2b], w=[mixb])
                    for sl in range(4):
                        s = s0 + sl
                        hc, hcb = h_rot.next()
                        c.dma(sp, hc[:], H1.ap()[s * 128:(s + 1) * 128, :], reads=[H1b[s]], writes=[hcb], sembuf=hcb)
                        for fh in range(2):
                            b, bb = ps()
                            for k in range(8):
                                A("pe", lambda: PE.matmul(PS[:, b, :], lhsT=mix[:, k, sl * 128:(sl + 1) * 128], rhs=Wou[:, k, fh * 512:(fh + 1) * 512],
                                                          start=(k == 0), stop=(k == 7)), r=[mixb, Bcw], w=[bb], inc=(k == 7))
                            A("dve", lambda: V.tensor_tensor(out=hc[:, fh * 512:(fh + 1) * 512], in0=PS[:, b, :], in1=hc[:, fh * 512:(fh + 1) * 512], op=ALU.add),
                              r=[bb, hcb], w=[hcb])
                        c.dma("pool", H1.ap()[s * 128:(s + 1) * 128, :], hc[:], reads=[hcb], writes=[H1b[s]], sembuf=hcb)
                c.barrier()
            ya_cm.__exit__(None, None, None)

            with ExitStack() as ec:
                hx = sb("hx2", [128, 8, D], F32, ec)
                xT = sb("xT2", [128, 8, 1024], BF16, ec)
                gT = sb("gT2", [128, 6, 1024], BF16, ec)
                w2h = sb("w2h2", [128, 6, D], BF16, ec)
                slots = [(sb("wsc%d" % i, [128, 4096], BF16, ec), Buf("wsc%d" % i)) for i in range(4)]
                small = Rot([(sb("smd%d" % i, [128, 64], F32, ec), Buf("smd%d" % i)) for i in range(6)])
                xn_rot = Rot([(sb("xnd%d" % i, [128, D], BF16, ec), Buf("xnd%d" % i)) for i in range(4)])
                sil_rot = Rot([(sb("sild%d" % i, [128, 512], F32, ec), Buf("sild%d" % i)) for i in range(3)])
                hxb = [Buf("hxd%d" % i) for i in range(8)]
                xTb = [Buf("xTd%d" % i) for i in range(8)]
                gTb = [Buf("gTd%d" % i) for i in range(2)]
                w2hb = Buf("w2hd")
                outb = Buf("out")

                def v_ffn2(t):
                    return t[:].rearrange("p (k a c) -> p k a c", k=8, a=2)

                jobs = []
                for t8 in range(2):
                    for sl in range(11):
                        jobs.append(("bf16", wb2i.ap()[sl].rearrange("p k a c -> p (k a c)"), [Bw["w2i_%d" % (0 if sl < 4 else 1 if sl < 8 else 2)]]))
                slabs = Slabs(slots, jobs)
                for t8 in range(2):
                    for lc in range(8):
                        s = t8 * 8 + lc
                        c.dma(sp, hx[:, lc, :], H1.ap()[s * 128:(s + 1) * 128, :], writes=[hxb[lc]], sembuf=hxb[lc])
                    norm_tile(8, hx, hxb, g2T, xT, xTb, small, xn_rot)

                    def last(lc, t8=t8):
                        s = t8 * 8 + lc
                        c.dma("pool", out.ap()[s * 128:(s + 1) * 128, :], hx[:, lc, :], reads=[hxb[lc]], writes=[outb], sembuf=hxb[lc])

                    ffn(8, hx, hxb, xT, xTb, gT, gTb, w2h, w2hb, slabs, t8 * 11, wb2o, [Bw["w2o"]] * len(PARTS), sil_rot, last=last)
                c.barrier()
            print("inst counts", c.ninst, "sems", c.nsem)
    return nc


def host_consts(p):
    idx = np.arange(NT)
    orig = idx if p == 1 else idx - 128
    valid = (orig >= 112).astype(np.float32)
    pos = np.where(valid > 0, orig - 112, 0).astype(np.float64)
    half = 64
    inv = 10000.0 ** (-np.arange(half, dtype=np.float64) / half)
    ang = pos[:, None] * inv[None, :]
    cos = np.cos(ang)
    sin = np.sin(ang)

    def pc(a):
        return np.ascontiguousarray(a.reshape(NCH, 128, -1).transpose(1, 0, 2)).astype(np.float32)

    vc = np.ascontiguousarray(valid.reshape(NCH, 128).T)
    log_gamma = np.log1p(-np.exp2(-5.0 - np.arange(4, dtype=np.float64)))
    n = np.arange(128, dtype=np.float64)
    diff = n[None, :] - n[:, None]
    decT = np.where(diff[:, None, :] >= 0, np.exp(log_gamma[None, :, None] * np.maximum(diff[:, None, :], 0.0)), 0.0).astype(np.float32)
    zeta = np.exp(log_gamma[None, :] * (127 - n)[:, None]).astype(np.float32)
    xi = np.exp(log_gamma[None, :] * (n + 1.0)[:, None]).astype(np.float32)
    gam = np.broadcast_to(np.exp(log_gamma * 128)[None, :], (128, 4)).astype(np.float32)
    tri = (n[:, None] <= n[None, :]).astype(np.float32)
    C2 = np.concatenate([cos, cos], 1)
    S2 = np.concatenate([-sin, sin], 1)
    vk = (valid * (128 ** -0.5))[:, None]
    return dict(ck_t=pc(C2 * vk), sk_t=pc(S2 * vk), cq_t=pc(C2), sq_t=pc(S2), vcol=vc, nvcol=-vc, vkcol=(vc * (128 ** -0.5)).astype(np.float32),
                decayT=np.ascontiguousarray(decT), zeta=np.ascontiguousarray(zeta), xi=np.ascontiguousarray(xi),
                gam=np.ascontiguousarray(gam), tri=tri)


_NC_CACHE = {}


def make_in_maps(inputs):
    x = np.asarray(inputs["x"], dtype=np.float32)
    meta = np.asarray(inputs["meta_tokens"], dtype=np.float32)
    shared = {}
    for k in ["w_ffn1_in", "w_ffn1_out", "w_ffn2_in", "w_ffn2_out", "w_in", "w_o_fox", "w_o_ret", "w_out"]:
        shared[k] = np.ascontiguousarray(np.asarray(inputs[k], dtype=np.float32)[0])
    for k in ["norm_ffn1", "norm_mix", "norm_ffn2", "b_forget", "b_gate", "fox_q_norm", "fox_k_norm", "ret_gn"]:
        shared[k] = np.ascontiguousarray(np.asarray(inputs[k], dtype=np.float32)[0])
    consts = [host_consts(0), host_consts(1)]
    in_maps = []
    for core in range(8):
        b, p = core // 2, core % 2
        lead = np.concatenate([np.zeros((112, D), np.float32), meta], axis=0)
        if p == 1:
            seq = np.concatenate([lead, x[b]], axis=0)
        else:
            seq = np.concatenate([np.zeros((128, D), np.float32), lead, x[b, :NT - 256]], axis=0)
        m = dict(shared)
        m.update(consts[p])
        m["xs"] = np.ascontiguousarray(seq)
        in_maps.append(m)
    return in_maps


def kernel(**inputs):
    if "nc" not in _NC_CACHE:
        _NC_CACHE["nc"] = build_nc()
    nc = _NC_CACHE["nc"]
    in_maps = make_in_maps(inputs)
    res = run_bass_kernel_spmd(nc, in_maps, core_ids=list(range(8)))
    B = 4
    out = np.empty((B, 4096, D), np.float32)
    for core in range(8):
        b, p = core // 2, core % 2
        o = res.results[core]["out"].reshape(NOWN, 128, D)
        for s in range(NOWN):
            i = 2 * s + 2
            orig = i if p == 1 else i - 1
            out[b, (orig - 1) * 128:orig * 128] = o[s]
    return out
```
